# Optimizing a Trainium2 kernel written in Bass

```python
import math
import jax, jax.numpy as jnp
from jax import lax
import numpy as np

D_MODEL = 4096
BATCH = 2
SEQ = 8192
DEPTH = 2

H_A = 8
DH_A = 128
R_Q = 768
R_KV = 256
H_I = 16
D_I = 64
TOPK_MAX = 256
Q_BLOCK = 128
N_BUCKETS = 32
MAX_DIST = 128
H_B = 24
N_B = 64
W_B = H_B * N_B
LORA_W = 128
LORA_A = 128
LORA_G = 480
H_C = 12
DK_C = 128
DV_C = 128
W_C = H_C * DK_C
CHUNK = 64
D_MIX = H_A * DH_A + W_B + W_C
D_FF = 4 * D_MODEL
EPS = 1e-6
GN_EPS = 64e-5
A_COLS = R_Q + R_KV + D_I + H_I
B_COLS = 3 * W_B + LORA_W + LORA_A + LORA_G
C_COLS = 4 * W_C
N_IN = A_COLS + B_COLS + C_COLS

kernel_name = "hybrid_dsa_rwkv7_hgrn2_trunk"


def rms_norm(x, g, eps=EPS):
    xf = x.astype(jnp.float32)
    y = xf * lax.rsqrt(jnp.mean(xf * xf, axis=-1, keepdims=True) + eps)
    return (y * g.astype(jnp.float32)).astype(x.dtype)


def token_shift(p, mu):
    prev = jnp.pad(p, ((0, 0), (1, 0), (0, 0)))[:, :-1]
    return p + mu * (prev - p)


def rel_bucket(n):
    max_exact = N_BUCKETS // 2
    nf = jnp.maximum(n, 1).astype(jnp.float32)
    large = max_exact + (jnp.log(nf / max_exact) / math.log(MAX_DIST / max_exact)
                         * (N_BUCKETS - max_exact)).astype(jnp.int32)
    large = jnp.minimum(large, N_BUCKETS - 1)
    return jnp.where(n < max_exact, n, large)


def dsa_attention(cq, ckv, k_idx, w_idx, q_norm_g, kv_norm_g, w_uq, w_uk, w_uv, w_qidx, rel_bias):
    B, L, _ = cq.shape
    topk = min(TOPK_MAX, L // 4)
    nb = L // Q_BLOCK
    cq = rms_norm(cq, q_norm_g)
    ckv = rms_norm(ckv, kv_norm_g)
    q = jnp.einsum('blr,rhd->blhd', cq, w_uq)
    q_lat = jnp.einsum('blhd,hrd->blhr', q, w_uk)
    q_idx = jnp.einsum('blr,rhd->blhd', cq, w_qidx)
    w_idx = w_idx * (H_I ** -0.5 * D_I ** -0.5)
    pos = jnp.arange(L, dtype=jnp.int32)

    def to_blocks(a):
        return jnp.moveaxis(a.reshape((B, nb, Q_BLOCK) + a.shape[2:]), 1, 0)

    def block(args):
        qi, wi, ql, t = args
        s = jax.nn.relu(jnp.einsum('bqhd,bsd->bqhs', qi, k_idx))
        score = jnp.einsum('bqhs,bqh->bqs', s, wi).astype(jnp.float32)
        causal = pos[None, :] <= t[:, None]
        score = jnp.where(causal[None], score, -jnp.inf)
        _, sel = lax.top_k(score, topk)
        kv_sel = jax.vmap(lambda kv, i: kv[i])(ckv, sel)
        logits = jnp.einsum('bqhr,bqkr->bqhk', ql, kv_sel).astype(jnp.float32) * (DH_A ** -0.5)
        dist = t[None, :, None] - sel
        bias = rel_bias[rel_bucket(jnp.maximum(dist, 0))]
        logits = logits + jnp.moveaxis(bias, -1, 2).astype(jnp.float32)
        logits = jnp.where((dist >= 0)[:, :, None, :], logits, -jnp.inf)
        prob = jax.nn.softmax(logits, axis=-1).astype(kv_sel.dtype)
        return jnp.einsum('bqhk,bqkr->bqhr', prob, kv_sel)

    o_lat = lax.map(block, (to_blocks(q_idx), to_blocks(w_idx), to_blocks(q_lat),
                            pos.reshape(nb, Q_BLOCK)))
    o_lat = jnp.moveaxis(o_lat, 0, 1).reshape(B, L, H_A, R_KV)
    out = jnp.einsum('blhr,hrd->blhd', o_lat, w_uv)
    return out.reshape(B, L, H_A * DH_A)


def rwkv7_time_mix(p, mu, w0, w_up, a0, a_up, g_up, k_k, k_a, r_k, lnx_w, lnx_b):
    B, L, _ = p.shape
    p = token_shift(p, mu)
    r, k, v, wd, ad, gd = jnp.split(
        p, [W_B, 2 * W_B, 3 * W_B, 3 * W_B + LORA_W, 3 * W_B + LORA_W + LORA_A], axis=-1)
    w = -jax.nn.softplus(-(w0 + jnp.tanh(wd) @ w_up)) - 0.5
    decay = jnp.exp(-jnp.exp(w.astype(jnp.float32)))
    a = jax.nn.sigmoid(a0 + ad @ a_up)
    g = jax.nn.sigmoid(gd) @ g_up
    hs = lambda t: t.reshape(B, L, H_B, N_B)
    kk = hs(k * k_k).astype(jnp.float32)
    kk = kk * lax.rsqrt(jnp.maximum(jnp.sum(kk * kk, -1, keepdims=True), 1e-24))
    k = k * (1 + (a - 1) * k_a)
    r_h, k_h, v_h, a_h, d_h = hs(r), hs(k), hs(v), hs(a), hs(decay)

    def step(S, inp):
        rt, dt, kt, vt, kkt, at = inp
        sa = jnp.einsum('bhij,bhj->bhi', S, -kkt)
        S = (S * dt[:, :, None, :] + sa[..., None] * (kkt * at)[:, :, None, :]
             + vt[..., None] * kt[:, :, None, :])
        return S, jnp.einsum('bhij,bhj->bhi', S, rt)

    tm = lambda t: jnp.moveaxis(t.astype(jnp.float32), 1, 0)
    S0 = jnp.zeros((B, H_B, N_B, N_B), jnp.float32)
    _, y = lax.scan(step, S0, (tm(r_h), tm(d_h), tm(k_h), tm(v_h), tm(kk), tm(a_h)))
    y = jnp.moveaxis(y, 0, 1)
    mean = jnp.mean(y, -1, keepdims=True)
    var = jnp.mean(jnp.square(y - mean), -1, keepdims=True)
    y = ((y - mean) * lax.rsqrt(var + GN_EPS)).reshape(B, L, W_B) * lnx_w + lnx_b
    bonus = jnp.sum((r_h * k_h * r_k).astype(jnp.float32), -1, keepdims=True) * v_h
    y = y + bonus.reshape(B, L, W_B)
    return (y * g).astype(p.dtype)


def hgrn2(p, lb, onorm_g):
    B, L, _ = p.shape
    q, f, i, g = jnp.split(p, 4, axis=-1)
    q = jax.nn.silu(q)
    ff = f.astype(jnp.float32)
    log_f = jnp.logaddexp(jnp.log(lb), jnp.log1p(-lb) + jax.nn.log_sigmoid(ff))
    k = (1 - lb) * jax.nn.sigmoid(-ff)
    nc = L // CHUNK

    def chunks(t, d):
        t = t.astype(jnp.float32).reshape(B, nc, CHUNK, H_C, d)
        return jnp.transpose(t, (1, 0, 3, 2, 4))

    causal = jnp.tril(jnp.ones((CHUNK, CHUNK), bool))

    def chunk_step(S, inp):
        qc, kc, vc, lfc = inp
        b = jnp.cumsum(lfc, axis=2)
        diff = b[:, :, :, None, :] - b[:, :, None, :, :]
        dec = jnp.exp(jnp.where(causal[:, :, None], diff, -jnp.inf))
        A = jnp.einsum('bhtd,bhsd,bhtsd->bhts', qc, kc, dec)
        o = (jnp.einsum('bhts,bhsv->bhtv', A, vc)
             + jnp.einsum('bhtd,bhdv->bhtv', qc * jnp.exp(b), S))
        b_end = b[:, :, -1:, :]
        S = (jnp.exp(b_end[:, :, 0, :])[..., None] * S
             + jnp.einsum('bhsd,bhsv->bhdv', kc * jnp.exp(b_end - b), vc))
        return S, o

    S0 = jnp.zeros((B, H_C, DK_C, DV_C), jnp.float32)
    _, o = lax.scan(chunk_step, S0, (chunks(q, DK_C), chunks(k, DK_C), chunks(i, DV_C), chunks(log_f, DK_C)))
    o = jnp.transpose(o, (1, 0, 3, 2, 4)).reshape(B, L, H_C, DV_C)
    o = o * lax.rsqrt(jnp.mean(o * o, -1, keepdims=True) + EPS)
    o = o.reshape(B, L, W_C) * onorm_g * jax.nn.silu(g.astype(jnp.float32))
    return o.astype(p.dtype)


def setup_inputs(seed: int = 0) -> dict:
    key = jax.random.key(seed)
    ks = iter(jax.random.split(key, 40))
    nrm = lambda shape, scale: jax.random.normal(next(ks), shape, jnp.float32) * scale
    gain = lambda shape: 1.0 + nrm(shape, 0.02)
    L = DEPTH
    return {
        "x": nrm((BATCH, SEQ, D_MODEL), 1.0),
        "c": nrm((BATCH, D_MODEL), 1.0),
        "rel_bias": nrm((N_BUCKETS, H_A), 0.5),
        "hgrn_lb": nrm((L, W_C), 1.0),
        "ada_w": nrm((L, D_MODEL, 6 * D_MODEL), 0.2 * D_MODEL ** -0.5),
        "ada_b": nrm((L, 6 * D_MODEL), 0.02),
        "norm_g": gain((L, 4, D_MODEL)),
        "w_in": nrm((L, D_MODEL, N_IN), D_MODEL ** -0.5),
        "w_out": nrm((L, D_MIX, D_MODEL), D_MIX ** -0.5),
        "mla_q_norm": gain((L, R_Q)),
        "mla_kv_norm": gain((L, R_KV)),
        "w_uq": nrm((L, R_Q, H_A, DH_A), R_Q ** -0.5),
        "w_uk": nrm((L, H_A, R_KV, DH_A), R_KV ** -0.5),
        "w_uv": nrm((L, H_A, R_KV, DH_A), R_KV ** -0.5),
        "w_qidx": nrm((L, R_Q, H_I, D_I), R_Q ** -0.5),
        "rwkv_mu": jax.random.uniform(next(ks), (L, B_COLS), jnp.float32),
        "rwkv_w0": jax.random.uniform(next(ks), (L, W_B), jnp.float32, -6.0, -1.0),
        "rwkv_w_up": nrm((L, LORA_W, W_B), 0.5 * LORA_W ** -0.5),
        "rwkv_a0": nrm((L, W_B), 0.5),
        "rwkv_a_up": nrm((L, LORA_A, W_B), 0.5 * LORA_A ** -0.5),
        "rwkv_g_up": nrm((L, LORA_G, W_B), LORA_G ** -0.5),
        "rwkv_k_k": 0.85 + nrm((L, W_B), 0.1),
        "rwkv_k_a": 1.0 + nrm((L, W_B), 0.1),
        "rwkv_r_k": nrm((L, H_B, N_B), 0.1),
        "rwkv_lnx_w": gain((L, W_B)),
        "rwkv_lnx_b": nrm((L, W_B), 0.02),
        "hgrn_onorm": gain((L, W_C)),
        "w_ff1": nrm((L, D_MODEL, D_FF), D_MODEL ** -0.5),
        "w_ff2": nrm((L, D_FF, D_MODEL), D_FF ** -0.5),
    }


def reference(x, c, rel_bias, hgrn_lb, ada_w, ada_b, norm_g, w_in, w_out, mla_q_norm, mla_kv_norm,
              w_uq, w_uk, w_uv, w_qidx, rwkv_mu, rwkv_w0, rwkv_w_up, rwkv_a0, rwkv_a_up, rwkv_g_up,
              rwkv_k_k, rwkv_k_a, rwkv_r_k, rwkv_lnx_w, rwkv_lnx_b, hgrn_onorm, w_ff1, w_ff2):
    lb_all = jnp.cumsum(jax.nn.softmax(hgrn_lb.astype(jnp.float32), axis=0), axis=0)
    lb_all = lb_all - lb_all[0]
    c_act = jax.nn.silu(c)
    for l in range(DEPTH):
        mod = c_act @ ada_w[l] + ada_b[l]
        sh_m, sc_m, g_m, sh_f, sc_f, g_f = [m[:, None, :] for m in jnp.split(mod, 6, axis=-1)]
        h = rms_norm(x, norm_g[l, 0]) * (1 + sc_m) + sh_m
        p = h @ w_in[l]
        pa, pb, pc = jnp.split(p, [A_COLS, A_COLS + B_COLS], axis=-1)
        cq, ckv, k_idx, w_idx = jnp.split(pa, [R_Q, R_Q + R_KV, R_Q + R_KV + D_I], axis=-1)
        y_a = dsa_attention(cq, ckv, k_idx, w_idx, mla_q_norm[l], mla_kv_norm[l], w_uq[l], w_uk[l],
                            w_uv[l], w_qidx[l], rel_bias)
        y_b = rwkv7_time_mix(pb, rwkv_mu[l], rwkv_w0[l], rwkv_w_up[l], rwkv_a0[l], rwkv_a_up[l],
                             rwkv_g_up[l], rwkv_k_k[l], rwkv_k_a[l], rwkv_r_k[l], rwkv_lnx_w[l],
                             rwkv_lnx_b[l])
        y_c = hgrn2(pc, lb_all[l], hgrn_onorm[l])
        y = jnp.concatenate([y_a, y_b, y_c], axis=-1) @ w_out[l]
        x = x + g_m * rms_norm(y, norm_g[l, 1])
        h = rms_norm(x, norm_g[l, 2]) * (1 + sc_f) + sh_f
        y = jnp.square(jax.nn.relu(h @ w_ff1[l])) @ w_ff2[l]
        x = x + g_f * rms_norm(y, norm_g[l, 3])
    return x
```

```python
import numpy as np
from contextlib import ExitStack
import ml_dtypes
import concourse.bass as bass
import concourse.mybir as mybir
from concourse.bass_utils import run_bass_kernel_spmd

F32 = mybir.dt.float32
BF16 = mybir.dt.bfloat16
AF = mybir.ActivationFunctionType
ALU = mybir.AluOpType
AX = mybir.AxisListType

D = 4096
SEQ = 8192
NB = 2
DFF = 16384
EPS = 1e-6


class Buf:
    __slots__ = ("name", "last_w", "readers")

    def __init__(self, name=""):
        self.name = name
        self.last_w = None
        self.readers = []


class Prog:
    COMPUTE = ("pe", "act", "dve", "pool")
    QUEUES = {"sp": 24, "actq": 8, "poolq": 8}
    Q2ENG = {"sp": "sp", "actq": "act", "poolq": "pool"}

    def __init__(self, nc):
        self.nc = nc
        self.streams = {e: [] for e in ("pe", "act", "dve", "pool", "sp")}
        self.nops = {e: 0 for e in self.COMPUTE}
        self.marked = {e: set() for e in self.COMPUTE}
        self.dma_n = {q: 0 for q in self.QUEUES}
        self.dma_val = {}

    def _deps(self, eng, reads, writes):
        deps = []
        for b in reads:
            if b.last_w is not None:
                deps.append((b.last_w, True))
        for b in writes:
            if b.last_w is not None:
                deps.append((b.last_w, True))
            for r in b.readers:
                deps.append((r, False))
        out = {}
        for d, strong in deps:
            if d[0] == "c" and d[1] == eng:
                if eng == "pe":
                    continue
            out[d] = True
        return list(out.keys())

    def _finish(self, tok, reads, writes):
        for b in writes:
            b.last_w = tok
            b.readers = []
        for b in reads:
            if b not in writes:
                b.readers.append(tok)

    def op(self, eng, fn, reads=(), writes=()):
        deps = self._deps(eng, reads, writes)
        idx = self.nops[eng]
        self.nops[eng] += 1
        tok = ("c", eng, idx)
        for d in deps:
            if d[0] == "c":
                self.marked[d[1]].add(d[2])
        self.streams[eng].append(("op", fn, deps, tok))
        self._finish(tok, reads, writes)
        return tok

    def dma(self, q, fn, reads=(), writes=()):
        eng = self.Q2ENG[q]
        deps = self._deps("dma", reads, writes)
        n = self.dma_n[q]
        self.dma_n[q] += 1
        slot = n % self.QUEUES[q]
        prev = self.dma_val.get((q, slot), 0)
        val = prev + 16
        self.dma_val[(q, slot)] = val
        tok = ("d", q, slot, val)
        if prev > 0:
            deps.append(("d", q, slot, prev))
        for d in deps:
            if d[0] == "c":
                self.marked[d[1]].add(d[2])
        self.streams[eng].append(("dma", fn, deps, tok))
        self._finish(tok, reads, writes)
        return tok

    def emit(self):
        nc = self.nc
        with ExitStack() as es:
            csem = {e: es.enter_context(nc.semaphore("s_" + e)) for e in self.COMPUTE}
            dsem = {}
            for q, n in self.QUEUES.items():
                for s in range(n):
                    dsem[(q, s)] = es.enter_context(nc.semaphore("d_%s_%d" % (q, s)))
            cnt = {}
            for e in self.COMPUTE:
                c = 0
                m = self.marked[e]
                arr = np.zeros(self.nops[e] + 1, dtype=np.int64)
                for i in range(self.nops[e]):
                    if i in m:
                        c += 1
                    arr[i] = c
                cnt[e] = arr
            final_dma = dict(self.dma_val)
            block = es.enter_context(nc.Block())

            def run_stream(engname, engobj):
                known_c = {e: 0 for e in self.COMPUTE}
                known_d = {}
                for kind, fn, deps, tok in self.streams[engname]:
                    need_c = {}
                    need_d = {}
                    for d in deps:
                        if d[0] == "c":
                            v = int(cnt[d[1]][d[2]])
                            if v > known_c[d[1]] and v > need_c.get(d[1], 0):
                                need_c[d[1]] = v
                        else:
                            key = (d[1], d[2])
                            if d[3] > known_d.get(key, 0) and d[3] > need_d.get(key, 0):
                                need_d[key] = d[3]
                    for e, v in need_c.items():
                        engobj.wait_ge(csem[e], v)
                        known_c[e] = v
                    for key, v in need_d.items():
                        engobj.wait_ge(dsem[key], v)
                        known_d[key] = v
                    ins = fn(engobj)
                    if kind == "op":
                        if tok[2] in self.marked[tok[1]]:
                            ins.then_inc(csem[tok[1]], 1)
                    else:
                        ins.then_inc(dsem[(tok[1], tok[2])], 16)
                if engname == "sp":
                    for key, v in final_dma.items():
                        if v > known_d.get(key, 0):
                            engobj.wait_ge(dsem[key], v)

            @block.tensor
            def _(e):
                run_stream("pe", e)

            @block.vector
            def _(e):
                run_stream("dve", e)

            @block.scalar
            def _(e):
                run_stream("act", e)

            @block.gpsimd
            def _(e):
                run_stream("pool", e)

            @block.sync
            def _(e):
                run_stream("sp", e)


class K:
    def __init__(self):
        self.nc = bass.Bass("TRN2", target_bir_lowering=False)
        self.P = Prog(self.nc)
        self.es = ExitStack()
        self.n = 0

    def din(self, name, shape, dt=F32):
        return self.nc.dram_tensor(name, list(shape), dt, kind="ExternalInput").ap()

    def dout(self, name, shape, dt=F32):
        return self.nc.dram_tensor(name, list(shape), dt, kind="ExternalOutput").ap()

    def dscr(self, name, shape, dt=F32):
        return self.nc.dram_tensor(name, list(shape), dt, kind="Internal").ap()

    def sb(self, shape, dt=F32, name=None):
        self.n += 1
        t = self.es.enter_context(self.nc.sbuf_tensor(name or ("sb%d" % self.n), list(shape), dt))
        return t, Buf(name or "")

    def ps(self, shape, dt=F32, name=None):
        self.n += 1
        t = self.es.enter_context(self.nc.psum_tensor(name or ("ps%d" % self.n), list(shape), dt))
        return t, Buf(name or "")

    def mm(self, out, lhsT, rhs, start, stop, reads, writes):
        self.P.op("pe", lambda e: e.matmul(out, lhsT=lhsT, rhs=rhs, start=start, stop=stop),
                  reads=reads, writes=writes)

    def tr(self, out, in_, ident, reads, writes):
        self.P.op("pe", lambda e: e.transpose(out=out, in_=in_, identity=ident),
                  reads=reads, writes=writes)

    def act(self, out, in_, func, reads, writes, bias=None, scale=None, accum_out=None, eng="act"):
        kw = {}
        if bias is not None:
            kw["bias"] = bias
        if scale is not None:
            kw["scale"] = scale
        if accum_out is not None:
            kw["accum_out"] = accum_out
        self.P.op("act", lambda e: e.activation(out=out, in_=in_, func=func, **kw),
                  reads=reads, writes=writes)

    def tt(self, out, in0, in1, op, reads, writes, eng="dve"):
        self.P.op(eng, lambda e: e.tensor_tensor(out=out, in0=in0, in1=in1, op=op),
                  reads=reads, writes=writes)

    def ts(self, out, in0, s1, s2, op0, op1, reads, writes, eng="dve", accum_out=None):
        if op1 is None:
            self.P.op(eng, lambda e: e.tensor_scalar(out=out, in0=in0, scalar1=s1, scalar2=None, op0=op0),
                      reads=reads, writes=writes)
        elif accum_out is not None:
            self.P.op(eng, lambda e: e.tensor_scalar(out=out, in0=in0, scalar1=s1, scalar2=s2, op0=op0,
                                                     op1=op1, accum_out=accum_out),
                      reads=reads, writes=writes)
        else:
            self.P.op(eng, lambda e: e.tensor_scalar(out=out, in0=in0, scalar1=s1, scalar2=s2, op0=op0, op1=op1),
                      reads=reads, writes=writes)

    def stt(self, out, in0, scalar, in1, op0, op1, reads, writes, accum_out=None):
        if accum_out is None:
            self.P.op("dve", lambda e: e.scalar_tensor_tensor(out=out, in0=in0, scalar=scalar, in1=in1,
                                                              op0=op0, op1=op1),
                      reads=reads, writes=writes)
        else:
            self.P.op("dve", lambda e: e.scalar_tensor_tensor(out=out, in0=in0, scalar=scalar, in1=in1,
                                                              op0=op0, op1=op1, accum_out=accum_out),
                      reads=reads, writes=writes)

    def cp(self, out, in_, reads, writes, eng="dve"):
        self.P.op(eng, lambda e: e.tensor_copy(out=out, in_=in_), reads=reads, writes=writes)

    def recip(self, out, in_, reads, writes):
        self.P.op("dve", lambda e: e.reciprocal(out=out, in_=in_), reads=reads, writes=writes)

    def memset(self, ap, v, writes, eng="pool"):
        self.P.op(eng, lambda e: e.memset(ap, v), writes=writes)

    def dma(self, out, in_, reads, writes, q="sp", **kw):
        self.P.dma(q, lambda e: e.dma_start(out=out, in_=in_, **kw), reads=reads, writes=writes)

    def rstd(self, ssq, n, eps, tmpb=None):
        t, b = ssq
        self.ts(t, t, 1.0 / n, eps, ALU.mult, ALU.add, [b], [b])
        self.act(t, t, AF.Sqrt, [b], [b])
        self.recip(t, t, [b], [b])

    def finish(self):
        self.P.emit()
        self.es.close()
        return self.nc


def build_mod():
    k = K()
    cT = k.din("cT", [128, 32, 2])
    aw = k.din("aw", [2, 6, 4096, 512])
    ab = k.din("ab", [2, 6, 512])
    ng = k.din("ng", [2, 4, 512])
    out = k.dout("modv", [2, 2, 6, 512])
    ct, b_ct = k.sb([128, 32, 2])
    ca, b_ca = k.sb([128, 32, 2])
    sg, b_sg = k.sb([128, 32, 2])
    wb = [k.sb([128, 16, 512]) for _ in range(3)]
    abt, b_ab = k.sb([2, 2, 6, 512])
    ngt, b_ng = k.sb([2, 2, 4, 512])
    modt, b_mod = k.sb([2, 6, 512])
    res, b_res = k.sb([2, 2, 6, 512])
    pm = [k.ps([2, 512]) for _ in range(2)]
    k.dma(ct[:], cT, [], [b_ct])
    for b in range(2):
        k.dma(abt[b:b + 1], ab[None], [], [b_ab], q="actq")
        k.dma(ngt[b:b + 1], ng[None], [], [b_ng], q="actq")
    k.act(sg[:], ct[:], AF.Sigmoid, [b_ct], [b_sg])
    k.tt(ca[:], ct[:], sg[:], ALU.mult, [b_ct, b_sg], [b_ca])
    it = 0
    for l in range(2):
        for j in range(6):
            pt, b_pt = pm[(l * 6 + j) % 2]
            for hf in range(2):
                wt, b_wt = wb[it % 3]
                it += 1
                src = aw[l, j, hf * 2048:(hf + 1) * 2048, :].rearrange("(c p) n -> p c n", p=128)
                k.dma(wt[:], src, [], [b_wt])
                for c in range(16):
                    kk = hf * 16 + c
                    k.mm(pt[:], ca[:, kk, :], wt[:, c, :], kk == 0, kk == 31, [b_ca, b_wt], [b_pt])
            k.tt(modt[:, j, :], pt[:], abt[:, l, j, :], ALU.add, [b_pt, b_ab], [b_mod])
        k.stt(res[:, l, 0, :], modt[:, 1, :], 1.0, ngt[:, l, 0, :], ALU.add, ALU.mult, [b_mod, b_ng], [b_res])
        k.cp(res[:, l, 1, :], modt[:, 0, :], [b_mod], [b_res])
        k.tt(res[:, l, 2, :], modt[:, 2, :], ngt[:, l, 1, :], ALU.mult, [b_mod, b_ng], [b_res])
        k.stt(res[:, l, 3, :], modt[:, 4, :], 1.0, ngt[:, l, 2, :], ALU.add, ALU.mult, [b_mod, b_ng], [b_res])
        k.cp(res[:, l, 4, :], modt[:, 3, :], [b_mod], [b_res])
        k.tt(res[:, l, 5, :], modt[:, 5, :], ngt[:, l, 3, :], ALU.mult, [b_mod, b_ng], [b_res])
    k.dma(out.rearrange("l b v n -> b l v n"), res[:], [b_res], [])
    return k.finish()


def run_mod(inp):
    c = np.asarray(inp["c"], np.float32)
    cT = np.ascontiguousarray(c.reshape(2, 32, 128).transpose(2, 1, 0))
    ada_w = inp["ada_w"]
    ada_b = np.asarray(inp["ada_b"], np.float32)
    norm_g = np.asarray(inp["norm_g"], np.float32)
    maps = []
    for ci in range(8):
        sl = slice(ci * 512, (ci + 1) * 512)
        aw = np.ascontiguousarray(ada_w.reshape(2, 4096, 6, 4096)[:, :, :, sl].transpose(0, 2, 1, 3))
        ab = np.ascontiguousarray(ada_b.reshape(2, 6, 4096)[:, :, sl])
        ng = np.ascontiguousarray(norm_g[:, :, sl])
        maps.append({"cT": cT, "aw": aw, "ab": ab, "ng": ng})
    nc = _get_nc('mod', build_mod)
    res = run_bass_kernel_spmd(nc, maps, core_ids=list(range(8)))
    modv = np.concatenate([r["modv"] for r in res.results], axis=-1)
    return modv


TG = 256
NT = TG // 128
KC = 8
NWB = 4


def build_l2(ntok=2048, stop=None):
    k = K()
    x = k.din("x", [ntok, D])
    y = k.din("y", [ntok, D], BF16)
    modr = k.din("modr", [6, D])
    modc = k.din("modc", [6, 128, 32])
    w_out = k.din("w_out", [D, D], BF16)
    w1 = k.din("w1", [D, DFF], BF16)
    w2 = k.din("w2", [DFF, D], BF16)
    idf_d = k.din("identf", [128, 128])
    xo = k.dout("xo", [ntok, D])

    idf, b_idf = k.sb([128, 128])
    idb, b_idb = k.sb([128, 128], BF16)
    mc, b_mc = k.sb([128, 6, 32])
    actT, b_actT = k.sb([128, 32, TG], BF16)
    hid, b_hid = k.sb([128, 128, TG], BF16)
    o1 = [k.sb([128, D]) for _ in range(NT)]
    tx, b_tx = k.sb([128, D])
    rowb, b_rowb = k.sb([128, D])
    yb, b_yb = k.sb([128, D], BF16)
    wbf = [k.sb([128, KC, 512], BF16) for _ in range(NWB)]
    rl, b_rl = k.sb([128, 4, TG])
    st, b_st = k.sb([128, 4])
    acc = [k.ps([128, 512]) for _ in range(NT)]
    accF, b_accF = k.ps([128, 4, 512])
    ptb, b_ptb = k.ps([128, 4, 128], BF16)
    ptf, b_ptf = k.ps([128, 4, 128])

    k.dma(idf[:], idf_d, [], [b_idf])
    k.dma(mc[:], modc.rearrange("v p c -> p v c"), [], [b_mc])
    k.cp(idb[:], idf[:], [b_idf], [b_idb])
    wi = [0]

    def load_w(src):
        i = wi[0] % NWB
        wi[0] += 1
        wb_, b_wb = wbf[i]
        k.dma(wb_[:], src, [], [b_wb])
        return wb_, b_wb

    def sumsq(src, b_src, col):
        k.act(yb[:], src, AF.Square, [b_src], [b_yb, b_st], accum_out=st[:, col:col + 1])

    def rstd_col(col):
        t = st[:, col:col + 1]
        k.ts(t, t, 1.0 / D, EPS, ALU.mult, ALU.add, [b_st], [b_st])
        k.act(t, t, AF.Sqrt, [b_st], [b_st])
        k.recip(t, t, [b_st], [b_st])

    for ps_ in range(ntok // TG):
        t0 = ps_ * TG
        b_xo = [Buf() for _ in range(NT)]
        for mt in range(NT):
            r0 = t0 + mt * 128
            k.dma(yb[:], y[r0:r0 + 128, :], [], [b_yb])
            for g in range(8):
                for c in range(4):
                    k.tr(ptb[:, c, :], yb[:, (g * 4 + c) * 128:(g * 4 + c + 1) * 128], idb[:], [b_yb, b_idb], [b_ptb])
                k.cp(actT[:, g * 4:(g + 1) * 4, mt * 128:(mt + 1) * 128], ptb[:], [b_ptb], [b_actT])
        for nb in range(8):
            for kt in range(32 // KC):
                src = w_out[kt * KC * 128:(kt + 1) * KC * 128, nb * 512:(nb + 1) * 512].rearrange("(c p) n -> p c n", p=128)
                wb_, b_wb = load_w(src)
                for mt in range(NT):
                    a, b_a = acc[mt]
                    for c in range(KC):
                        kk = kt * KC + c
                        k.mm(a[:], actT[:, kk, mt * 128:(mt + 1) * 128], wb_[:, c, :], kk == 0, kk == 31, [b_actT, b_wb], [b_a])
            for mt in range(NT):
                a, b_a = acc[mt]
                o, b_o = o1[mt]
                k.act(o[:, nb * 512:(nb + 1) * 512], a[:], AF.Copy, [b_a], [b_o])
        for mt in range(NT):
            r0 = t0 + mt * 128
            o, b_o = o1[mt]
            sumsq(o[:], b_o, 0)
            rstd_col(0)
            k.dma(tx[:], x[r0:r0 + 128, :], [], [b_tx])
            k.dma(rowb[:], modr[2, :].partition_broadcast(128), [], [b_rowb], q="actq")
            k.stt(o[:], o[:], st[:, 0:1], rowb[:], ALU.mult, ALU.mult, [b_o, b_st, b_rowb], [b_o])
            k.tt(tx[:], tx[:], o[:], ALU.add, [b_tx, b_o], [b_tx], eng="pool")
            k.dma(xo[r0:r0 + 128, :], tx[:], [b_tx], [b_xo[mt]])
            sumsq(tx[:], b_tx, 1)
            rstd_col(1)
            k.act(o[:], tx[:], AF.Identity, [b_tx, b_st], [b_o], scale=st[:, 1:2])
            for g in range(8):
                for c in range(4):
                    k.tr(ptf[:, c, :], o[:, (g * 4 + c) * 128:(g * 4 + c + 1) * 128], idf[:], [b_o, b_idf], [b_ptf])
                for c in range(4):
                    kk = g * 4 + c
                    k.act(actT[:, kk, mt * 128:(mt + 1) * 128], ptf[:, c, :], AF.Identity, [b_ptf, b_mc], [b_actT],
                          bias=mc[:, 4, kk:kk + 1], scale=mc[:, 3, kk:kk + 1])
        if stop == 'C':
            continue
        for g in range(DFF // 512):
            for kt in range(32 // KC):
                src = w1[kt * KC * 128:(kt + 1) * KC * 128, g * 512:(g + 1) * 512].rearrange("(c p) n -> p c n", p=128)
                wb_, b_wb = load_w(src)
                for fb in range(4):
                    for c in range(KC):
                        kk = kt * KC + c
                        k.mm(accF[:, fb, 0:TG], wb_[:, c, fb * 128:(fb + 1) * 128], actT[:, kk, :], kk == 0, kk == 31,
                             [b_actT, b_wb], [b_accF])
            k.act(rl[:], accF[:, :, 0:TG], AF.Relu, [b_accF], [b_rl])
            k.tt(hid[:, g * 4:(g + 1) * 4, :], rl[:], rl[:], ALU.mult, [b_rl], [b_hid])
        for db in range(8):
            for ft in range(128 // KC):
                src = w2[ft * KC * 128:(ft + 1) * KC * 128, db * 512:(db + 1) * 512].rearrange("(c p) n -> p c n", p=128)
                wb_, b_wb = load_w(src)
                for mt in range(NT):
                    a, b_a = acc[mt]
                    for c in range(KC):
                        kk = ft * KC + c
                        k.mm(a[:], hid[:, kk, mt * 128:(mt + 1) * 128], wb_[:, c, :], kk == 0, kk == 127, [b_hid, b_wb], [b_a])
            for mt in range(NT):
                a, b_a = acc[mt]
                o, b_o = o1[mt]
                k.act(o[:, db * 512:(db + 1) * 512], a[:], AF.Copy, [b_a], [b_o])
        for mt in range(NT):
            r0 = t0 + mt * 128
            o, b_o = o1[mt]
            sumsq(o[:], b_o, 2)
            rstd_col(2)
            k.dma(tx[:], xo[r0:r0 + 128, :], [b_xo[mt]], [b_tx])
            k.dma(rowb[:], modr[5, :].partition_broadcast(128), [], [b_rowb], q="actq")
            k.stt(o[:], o[:], st[:, 2:3], rowb[:], ALU.mult, ALU.mult, [b_o, b_st, b_rowb], [b_o])
            k.tt(tx[:], tx[:], o[:], ALU.add, [b_tx, b_o], [b_tx], eng="pool")
            k.dma(xo[r0:r0 + 128, :], tx[:], [b_tx], [b_xo[mt]])
    return k.finish()


_IDENT = np.eye(128, dtype=np.float32)


def run_l2(x, ybf, modv_l, w_out, w1, w2):
    nc = _get_nc('l2', build_l2)
    maps = []
    for ci in range(8):
        b, q = ci // 4, ci % 4
        sl = slice(q * 2048, (q + 1) * 2048)
        mr = np.ascontiguousarray(modv_l[b])
        mcm = np.ascontiguousarray(mr.reshape(6, 32, 128).transpose(0, 2, 1))
        maps.append({"x": np.ascontiguousarray(x[b, sl]), "y": np.ascontiguousarray(ybf[b, sl]),
                     "modr": mr, "modc": mcm, "w_out": w_out, "w1": w1, "w2": w2, "identf": _IDENT})
    res = run_bass_kernel_spmd(nc, maps, core_ids=list(range(8)))
    out = np.empty((2, 8192, 4096), np.float32)
    for ci in range(8):
        b, q = ci // 4, ci % 4
        out[b, q * 2048:(q + 1) * 2048] = res.results[ci]["xo"]
    return out


class HTMaker:
    def __init__(self, k, modc_d, idf_d, inplace=False):
        self.k = k
        self.idf, self.b_idf = k.sb([128, 128])
        self.idb, self.b_idb = k.sb([128, 128], BF16)
        self.mc, self.b_mc = k.sb([128, 6, 32])
        self.tx, self.b_tx = k.sb([128, D])
        if inplace:
            self.xn, self.b_xn = self.tx, self.b_tx
        else:
            self.xn, self.b_xn = k.sb([128, D])
        self.junk, self.b_junk = k.sb([128, D], BF16)
        self.st, self.b_st = k.sb([128, 2])
        self.ptf, self.b_ptf = k.ps([128, 4, 128])
        k.dma(self.idf[:], idf_d, [], [self.b_idf])
        k.dma(self.mc[:], modc_d.rearrange("v p c -> p v c"), [], [self.b_mc])
        k.cp(self.idb[:], self.idf[:], [self.b_idf], [self.b_idb])

    def emit(self, src, dst_fn, b_dst, ai=0, bi=1):
        k = self
        kk_ = self.k
        kk_.dma(self.tx[:], src, [], [self.b_tx])
        kk_.act(self.junk[:], self.tx[:], AF.Square, [self.b_tx], [self.b_junk, self.b_st],
                accum_out=self.st[:, 0:1])
        t = self.st[:, 0:1]
        kk_.ts(t, t, 1.0 / D, EPS, ALU.mult, ALU.add, [self.b_st], [self.b_st])
        kk_.act(t, t, AF.Sqrt, [self.b_st], [self.b_st])
        kk_.recip(t, t, [self.b_st], [self.b_st])
        kk_.act(self.xn[:], self.tx[:], AF.Identity, [self.b_tx, self.b_st], [self.b_xn], scale=self.st[:, 0:1])
        for g in range(8):
            for c in range(4):
                q = g * 4 + c
                kk_.tr(self.ptf[:, c, :], self.xn[:, q * 128:(q + 1) * 128], self.idf[:],
                       [self.b_xn, self.b_idf], [self.b_ptf])
            for c in range(4):
                q = g * 4 + c
                kk_.act(dst_fn(q), self.ptf[:, c, :], AF.Identity, [self.b_ptf, self.b_mc], [b_dst],
                        bias=self.mc[:, bi, q:q + 1], scale=self.mc[:, ai, q:q + 1])


class WLoader:
    def __init__(self, k, ncol=128, nbuf=2, nst=None):
        self.k = k
        self.ncol = ncol
        self.st = [k.sb([128, 32, ncol]) for _ in range(nst or nbuf)]
        self.bf = [k.sb([128, 32, ncol], BF16) for _ in range(nbuf)]
        self.i = 0

    def load(self, W, c0, n):
        k = self.k
        ws, b_ws = self.st[self.i % len(self.st)]
        wb, b_wb = self.bf[self.i % len(self.bf)]
        self.i += 1
        k.dma(ws[:, :, 0:n], W[:, c0:c0 + n].rearrange("(c p) n -> p c n", p=128), [], [b_ws])
        k.cp(wb[:, :, 0:n], ws[:, :, 0:n], [b_ws], [b_wb], eng="pool")
        return wb, b_wb


TB = 512
CH = 64


def gemm_fm(k, wb, b_wb, ncols_off, hT, b_hT, out, b_out, M=128):
    for kk in range(32):
        k.mm(out, wb[:, kk, ncols_off:ncols_off + M], hT[:, kk, :], kk == 0, kk == 31, [b_wb, b_hT], [b_out])


def build_1c(layer):
    HC = 3
    k = K()
    x = k.din("x", [SEQ, D])
    modc = k.din("modc", [6, 128, 32])
    idf_d = k.din("identf", [128, 128])
    w = k.din("w", [D, 4 * HC * 128])
    lbraw = k.din("lbraw", [128, HC, 2])
    onorm = k.din("onorm", [HC * 128])
    cmask_d = k.din("cmask", [64, 64])
    smask_d = k.din("smask", [128, TB])
    yo = k.dout("yc", [SEQ, HC * 128], BF16)

    hm = HTMaker(k, modc, idf_d)
    wl = WLoader(k, 128, 2)
    hT, b_hT = k.sb([128, 32, TB], BF16)
    cmask, b_cm = k.sb([64, 64])
    smask, b_sm = k.sb([128, TB])
    lbt, b_lb = k.sb([128, HC, 2])
    lbw, b_lbw = k.sb([128, 6, HC])
    lb, b_lbv = k.sb([128, HC])
    oml, b_oml = k.sb([128, HC])
    onb, b_onb = k.sb([64, HC * 128])
    k.dma(cmask[:], cmask_d, [], [b_cm])
    k.dma(smask[:], smask_d, [], [b_sm])
    k.dma(lbt[:], lbraw, [], [b_lb])
    k.dma(onb[:], onorm.partition_broadcast(64), [], [b_onb])
    m_ = lbw[:, 0, :]
    k.tt(m_, lbt[:, :, 0], lbt[:, :, 1], ALU.max, [b_lb], [b_lbw])
    k.tt(lbw[:, 1, :], lbt[:, :, 0], m_, ALU.subtract, [b_lb, b_lbw], [b_lbw])
    k.tt(lbw[:, 2, :], lbt[:, :, 1], m_, ALU.subtract, [b_lb, b_lbw], [b_lbw])
    k.act(lbw[:, 1, :], lbw[:, 1, :], AF.Exp, [b_lbw], [b_lbw])
    k.act(lbw[:, 2, :], lbw[:, 2, :], AF.Exp, [b_lbw], [b_lbw])
    k.tt(lbw[:, 3, :], lbw[:, 1, :], lbw[:, 2, :], ALU.add, [b_lbw], [b_lbw])
    k.recip(lbw[:, 3, :], lbw[:, 3, :], [b_lbw], [b_lbw])
    k.tt(lbw[:, 4, :], lbw[:, 1, :], lbw[:, 3, :], ALU.mult, [b_lbw], [b_lbw])
    k.tt(lbw[:, 5, :], lbw[:, 2, :], lbw[:, 3, :], ALU.mult, [b_lbw], [b_lbw])
    if layer == 0:
        k.tt(lb[:], lbw[:, 4, :], lbw[:, 4, :], ALU.subtract, [b_lbw], [b_lbv])
    else:
        k.tt(lb[:], lbw[:, 4, :], lbw[:, 5, :], ALU.add, [b_lbw], [b_lbv])
        k.tt(lb[:], lb[:], lbw[:, 4, :], ALU.subtract, [b_lbw, b_lbv], [b_lbv])
    k.ts(oml[:], lb[:], -1.0, 1.0, ALU.mult, ALU.add, [b_lbv], [b_oml])

    pq = [k.ps([128, TB]) for _ in range(2)]
    pv, b_pv = k.ps([64, 4, 128])
    pAT, b_pAT = k.ps([64, HC, 64])
    po, b_po = k.ps([64, HC, 128])
    pS, b_pS = k.ps([128, HC, 128])
    pkt, b_pkt = k.ps([64, HC, 128], BF16)

    qs, b_qs = k.sb([128, TB])
    sg, b_sg = k.sb([128, TB])
    sgn, b_sgn = k.sb([128, TB])
    lf, b_lf = k.sb([128, TB])
    bb, b_bb = k.sb([128, TB])
    enb, b_enb = k.sb([128, TB])
    eb, b_eb = k.sb([128, HC, TB])
    qt, b_qt = k.sb([128, HC, TB], BF16)
    kt, b_kt = k.sb([128, HC, TB], BF16)
    V, b_V = k.sb([64, 8, HC * 128], BF16)
    gw, b_gw = k.sb([64, 8, HC * 128])
    gs, b_gs = k.sb([64, 4, 128])
    S, b_S = k.sb([128, HC, 128])
    Sb, b_Sb = k.sb([128, HC, 128], BF16)
    t1, b_t1 = k.sb([128, HC, 128])
    ATs, b_ATs = k.sb([64, HC, 64], BF16)
    kts, b_kts = k.sb([64, HC, 128], BF16)
    st, b_st = k.sb([64, HC])
    junk, b_junk = k.sb([64, 128])
    yt = [k.sb([64, HC * 128], BF16) for _ in range(2)]
    k.memset(S[:], 0.0, [b_S])
    k.memset(Sb[:], 0.0, [b_Sb])
    pqi = 0
    for tb in range(SEQ // TB):
        t0 = tb * TB
        for mt in range(TB // 128):
            hm.emit(x[t0 + mt * 128:t0 + (mt + 1) * 128, :],
                    lambda q, mt=mt: hT[:, q, mt * 128:(mt + 1) * 128], b_hT)
        for h in range(HC):
            wb, b_wb = wl.load(w, h * 128, 128)
            p_, b_p = pq[pqi % 2]; pqi += 1
            gemm_fm(k, wb, b_wb, 0, hT, b_hT, p_[:], b_p)
            k.act(qs[:], p_[:], AF.Silu, [b_p], [b_qs])
            wb, b_wb = wl.load(w, (HC + h) * 128, 128)
            p_, b_p = pq[pqi % 2]; pqi += 1
            gemm_fm(k, wb, b_wb, 0, hT, b_hT, p_[:], b_p)
            k.act(sg[:], p_[:], AF.Sigmoid, [b_p], [b_sg])
            k.act(sgn[:], p_[:], AF.Sigmoid, [b_p], [b_sgn], scale=-1.0)
            k.ts(sg[:], sg[:], oml[:, h:h + 1], lb[:, h:h + 1], ALU.mult, ALU.add, [b_sg, b_oml, b_lbv], [b_sg])
            k.act(lf[:], sg[:], AF.Ln, [b_sg], [b_lf])
            k.ts(sgn[:], sgn[:], oml[:, h:h + 1], None, ALU.mult, None, [b_sgn, b_oml], [b_sgn])
            k.P.op("dve", lambda e: e.tensor_tensor_scan(out=bb[:], data0=smask[:], data1=lf[:], initial=0.0,
                                                        op0=ALU.mult, op1=ALU.add),
                   reads=[b_sm, b_lf], writes=[b_bb])
            k.act(eb[:, h, :], bb[:], AF.Exp, [b_bb], [b_eb])
            k.act(enb[:], bb[:], AF.Exp, [b_bb], [b_enb], scale=-1.0)
            k.tt(qt[:, h, :], qs[:], eb[:, h, :], ALU.mult, [b_qs, b_eb], [b_qt])
            k.tt(kt[:, h, :], sgn[:], enb[:], ALU.mult, [b_sgn, b_enb], [b_kt])
        for j in range(2 * HC):
            wb, b_wb = wl.load(w, (2 * HC + j) * 128, 128)
            for c4 in range(2):
                for c in range(4):
                    cc = c4 * 4 + c
                    for kk in range(32):
                        k.mm(pv[:, c, :], hT[:, kk, cc * 64:(cc + 1) * 64], wb[:, kk, :], kk == 0, kk == 31,
                             [b_hT, b_wb], [b_pv])
                if j < HC:
                    k.act(V[:, c4 * 4:(c4 + 1) * 4, j * 128:(j + 1) * 128], pv[:], AF.Copy, [b_pv], [b_V])
                else:
                    hh = j - HC
                    k.act(gs[:], pv[:], AF.Silu, [b_pv], [b_gs])
                    for c in range(4):
                        k.tt(gw[:, c4 * 4 + c, hh * 128:(hh + 1) * 128], gs[:, c, :], onb[:, hh * 128:(hh + 1) * 128],
                             ALU.mult, [b_gs, b_onb], [b_gw])
        for c in range(8):
            cs = slice(c * 64, (c + 1) * 64)
            y_, b_y = yt[c % 2]
            for h in range(HC):
                hs = slice(h * 128, (h + 1) * 128)
                k.mm(pAT[:, h, :], kt[:, h, cs], qt[:, h, cs], True, True, [b_kt, b_qt], [b_pAT])
                k.tt(ATs[:, h, :], pAT[:, h, :], cmask[:], ALU.mult, [b_pAT, b_cm], [b_ATs])
                k.mm(po[:, h, :], ATs[:, h, :], V[:, c, hs], True, False, [b_ATs, b_V], [b_po])
                k.mm(po[:, h, :], qt[:, h, cs], Sb[:, h, :], False, True, [b_qt, b_Sb], [b_po])
                k.tr(pkt[:, h, :], kt[:, h, cs], hm.idb[:], [b_kt, hm.b_idb], [b_pkt])
                k.act(kts[:, h, :], pkt[:, h, :], AF.Copy, [b_pkt], [b_kts])
                k.mm(pS[:, h, :], kts[:, h, :], V[:, c, hs], True, True, [b_kts, b_V], [b_pS])
                ec = eb[:, h, c * 64 + 63:c * 64 + 64]
                k.tt(t1[:, h, :], pS[:, h, :], S[:, h, :], ALU.add, [b_pS, b_S], [b_t1])
                k.ts(S[:, h, :], t1[:, h, :], ec, None, ALU.mult, None, [b_t1, b_eb], [b_S])
                k.act(Sb[:, h, :], t1[:, h, :], AF.Identity, [b_t1, b_eb], [b_Sb], scale=ec)
                k.act(junk[:], po[:, h, :], AF.Square, [b_po], [b_junk, b_st], accum_out=st[:, h:h + 1])
            k.ts(st[:], st[:], 1.0 / 128, EPS, ALU.mult, ALU.add, [b_st], [b_st])
            k.act(st[:], st[:], AF.Sqrt, [b_st], [b_st])
            k.recip(st[:], st[:], [b_st], [b_st])
            for h in range(HC):
                hs = slice(h * 128, (h + 1) * 128)
                k.stt(y_[:, hs], po[:, h, :], st[:, h:h + 1], gw[:, c, hs], ALU.mult, ALU.mult,
                      [b_po, b_st, b_gw], [b_y])
            k.dma(yo[t0 + c * 64:t0 + (c + 1) * 64, :], y_[:], [b_y], [], q="actq")
    return k.finish()


_CMASK = np.triu(np.ones((64, 64), np.float32))
_SMASK = np.ones((128, TB), np.float32)
_SMASK[:, ::CH] = 0.0


def run_1c(layer, x, modv_l, w_in_l, hgrn_lb, hgrn_onorm_l):
    A_COLS, B_COLS = 1104, 5344
    c0 = A_COLS + B_COLS
    nc = _get_nc(('1c', layer), lambda: build_1c(layer))
    maps = []
    for ci in range(8):
        b, g = ci // 4, ci % 4
        hs = slice(g * 384, (g + 1) * 384)
        wc = w_in_l[:, c0:]
        wsl = np.ascontiguousarray(np.concatenate([wc[:, j * 1536:(j + 1) * 1536][:, hs] for j in range(4)], axis=1))
        mr = np.ascontiguousarray(modv_l[b])
        mcm = np.ascontiguousarray(mr.reshape(6, 32, 128).transpose(0, 2, 1))
        lbr = np.ascontiguousarray(hgrn_lb[:, hs].reshape(2, 3, 128).transpose(2, 1, 0))
        maps.append({"x": np.ascontiguousarray(x[b]), "modc": mcm, "identf": _IDENT, "w": wsl,
                     "lbraw": lbr, "onorm": np.ascontiguousarray(hgrn_onorm_l[hs]),
                     "cmask": _CMASK, "smask": _SMASK})
    res = run_bass_kernel_spmd(nc, maps, core_ids=list(range(8)))
    out = np.empty((2, 8192, 1536), ml_dtypes.bfloat16)
    for ci in range(8):
        b, g = ci // 4, ci % 4
        out[b, :, g * 384:(g + 1) * 384] = res.results[ci]["yc"]
    return out


TBB = 256
GN_EPS = 64e-5


def build_1b(seq=SEQ):
    HB = 6
    NCH = TBB // CH
    k = K()
    x = k.din("x", [seq, D])
    modc = k.din("modc", [6, 128, 32])
    idf_d = k.din("identf", [128, 128])
    wrkv = k.din("wrkv", [D, 3 * HB * 64])
    wlo = k.din("wlo", [D, 736])
    mu_rkv_d = k.din("mu_rkv", [64, 3 * HB])
    mu_wa_d = k.din("mu_wa", [128, 2])
    mu_g_d = k.din("mu_g", [120, 4])
    hp_d = k.din("hp", [64, 5, HB])
    wup_d = k.din("wup", [128, HB * 64])
    aup_d = k.din("aup", [128, HB * 64])
    gup_d = k.din("gup", [120, 4, HB * 64])
    lnw_d = k.din("lnw", [HB * 64])
    lnb_d = k.din("lnb", [HB * 64])
    smask_d = k.din("smask", [64, TBB])
    m5_d = k.din("m5", [64, 3, 5, 64])
    yo = k.dout("yb", [seq, HB * 64], BF16)

    hm = HTMaker(k, modc, idf_d)
    wl = WLoader(k, 128, 2)
    hT, b_hT = k.sb([128, 32, TBB], BF16)
    smask, b_sm = k.sb([64, TBB])
    m5, b_m5 = k.sb([64, 3, 5, 64])
    mu_rkv, b_mur = k.sb([64, 3 * HB])
    mu_wa, b_muw = k.sb([128, 2])
    mu_g, b_mug = k.sb([120, 4])
    hp, b_hp = k.sb([64, 5, HB])
    wupf, b_wupf = k.sb([128, HB * 64])
    aupf, b_aupf = k.sb([128, HB * 64])
    gupf, b_gupf = k.sb([120, 4, HB * 64])
    wup, b_wup = k.sb([128, HB * 64], BF16)
    aup, b_aup = k.sb([128, HB * 64], BF16)
    gup, b_gup = k.sb([120, 4, HB * 64], BF16)
    lnw, b_lnw = k.sb([64, HB * 64])
    lnb, b_lnb = k.sb([64, HB * 64])
    ones, b_ones = k.sb([64, 64])
    for dst, src, bd in ((smask, smask_d, b_sm), (m5, m5_d, b_m5), (mu_rkv, mu_rkv_d, b_mur), (mu_wa, mu_wa_d, b_muw),
                         (mu_g, mu_g_d, b_mug), (hp, hp_d, b_hp), (wupf, wup_d, b_wupf), (aupf, aup_d, b_aupf),
                         (gupf, gup_d, b_gupf)):
        k.dma(dst[:], src, [], [bd], q="actq")
    k.dma(lnw[:], lnw_d.partition_broadcast(64), [], [b_lnw], q="actq")
    k.dma(lnb[:], lnb_d.partition_broadcast(64), [], [b_lnb], q="actq")
    k.cp(wup[:], wupf[:], [b_wupf], [b_wup])
    k.cp(aup[:], aupf[:], [b_aupf], [b_aup])
    k.cp(gup[:], gupf[:], [b_gupf], [b_gup])
    k.memset(ones[:], 1.0, [b_ones])

    pq, b_pq = k.ps([128, 512])
    PA, b_PA = k.ps([64, 16, 64])
    PN, b_PN = k.ps([64, 8, 64])
    PXW, b_PX = k.ps([64, 8, 64])
    b_PW = b_PX
    PYH, b_PY = k.ps([64, 8, 64])
    b_PH = b_PY
    PT, b_PT = k.ps([64, 8, 64])

    R_, b_R = k.sb([128, TBB + 1])
    carry, b_carry = k.sb([128, 3 * HB + 6])
    rT, b_rT = k.sb([64, HB, TBB])
    kT, b_kT = k.sb([64, HB, TBB])
    vT, b_vT = k.sb([64, HB, TBB])
    At, b_At = k.sb([64, HB, TBB])
    Bt, b_Bt = k.sb([64, HB, TBB])
    ebC, b_ebC = k.sb([64, HB, NCH])
    lsh, b_lsh = k.sb([128, TBB])
    tw, b_tw = k.sb([128, TBB], BF16)
    ta, b_ta = k.sb([128, TBB], BF16)
    tg, b_tg = k.sb([120, 4, TBB], BF16)
    tmp = [k.sb([64, TBB]) for _ in range(8)]
    Vt, b_Vt = k.sb([64, NCH, HB, 64])
    gt, b_gt = k.sb([64, NCH, HB * 64])
    rk, b_rk = k.sb([64, NCH, HB])
    H, b_H = k.sb([64, HB, 64])
    mats, b_mats = k.sb([64, 3, 5, 64])
    Nsb, b_N = k.sb([64, 3, 2, 64])
    X, b_X = k.sb([64, 3, 64])
    W, b_W = k.sb([64, 3, 64])
    U, b_U = k.sb([64, 3, 64])
    BKt, b_BKt = k.sb([64, 3, 2, 64])
    ysb, b_ysb = k.sb([64, HB, 64])
    t2, b_t2 = k.sb([64, HB, 64])
    stt_, b_stt = k.sb([64, HB, 8])
    mv, b_mv = k.sb([64, HB, 2])
    yt = [k.sb([64, HB * 64], BF16) for _ in range(2)]
    k.memset(carry[:], 0.0, [b_carry])
    k.memset(H[:], 0.0, [b_H])
    idf, b_idf = hm.idf, hm.b_idf

    def shifted(M, col, mu_ap, b_mu, dst, b_dst):
        k.act(R_[0:M, 1:TBB + 1], pq[0:M, 0:TBB], AF.Copy, [b_pq], [b_R])
        k.cp(R_[0:M, 0:1], carry[0:M, col:col + 1], [b_carry], [b_R])
        k.cp(carry[0:M, col:col + 1], R_[0:M, TBB:TBB + 1], [b_R], [b_carry])
        t_, b_t = tmp[7]
        k.tt(lsh[0:M, :], R_[0:M, 0:TBB], R_[0:M, 1:TBB + 1], ALU.subtract, [b_R], [b_lsh])
        k.stt(dst, lsh[0:M, :], mu_ap, R_[0:M, 1:TBB + 1], ALU.mult, ALU.add, [b_lsh, b_mu, b_R], [b_dst])

    def gemm(wb, b_wb, off, M):
        for kk in range(32):
            k.mm(pq[0:M, 0:TBB], wb[:, kk, off:off + M], hT[:, kk, :], kk == 0, kk == 31, [b_wb, b_hT], [b_pq])

    yi = 0
    for tb in range(seq // TBB):
        t0 = tb * TBB
        for mt in range(TBB // 128):
            hm.emit(x[t0 + mt * 128:t0 + (mt + 1) * 128, :],
                    lambda q, mt=mt: hT[:, q, mt * 128:(mt + 1) * 128], b_hT)
        for j in range(3 * HB // 2):
            wb, b_wb = wl.load(wrkv, j * 128, 128)
            for s in range(2):
                col = j * 2 + s
                which, h = col // HB, col % HB
                dstt, b_d = ((rT, b_rT), (kT, b_kT), (vT, b_vT))[which]
                gemm(wb, b_wb, s * 64, 64)
                shifted(64, col, mu_rkv[:, col:col + 1], b_mur, dstt[:, h, :], b_d)
        wb, b_wb = wl.load(wlo, 0, 128)
        gemm(wb, b_wb, 0, 128)
        shifted(128, 3 * HB + 0, mu_wa[:, 0:1], b_muw, lsh[:, :], b_lsh)
        k.act(tw[:], lsh[:], AF.Tanh, [b_lsh], [b_tw])
        wb, b_wb = wl.load(wlo, 128, 128)
        gemm(wb, b_wb, 0, 128)
        shifted(128, 3 * HB + 1, mu_wa[:, 1:2], b_muw, lsh[:, :], b_lsh)
        k.act(ta[:], lsh[:], AF.Copy, [b_lsh], [b_ta])
        for j in range(4):
            wb, b_wb = wl.load(wlo, 256 + j * 120, 120)
            gemm(wb, b_wb, 0, 120)
            shifted(120, 3 * HB + 2 + j, mu_g[:, j:j + 1], b_mug, lsh[0:120, :], b_lsh)
            k.act(tg[:, j, :], lsh[0:120, :], AF.Sigmoid, [b_lsh], [b_tg])
        for h in range(HB):
            hs = slice(h * 64, (h + 1) * 64)
            (sgz, b_sgz), (av, b_av), (kkv, b_kkv), (bv, b_bv), (e1, b_e1), (e2, b_e2), (e3, b_e3), (tq, b_tq) = tmp
            k.mm(pq[0:64, 0:TBB], wup[:, hs], tw[:], True, True, [b_wup, b_tw], [b_pq])
            k.act(sgz[:], pq[0:64, 0:TBB], AF.Sigmoid, [b_pq, b_hp], [b_sgz], bias=hp[:, 0, h:h + 1])
            k.ts(sgz[:], sgz[:], -float(np.exp(-0.5)), None, ALU.mult, None, [b_sgz], [b_sgz])
            k.P.op("dve", lambda e, bv=bv, sgz=sgz: e.tensor_tensor_scan(out=bv[:], data0=smask[:], data1=sgz[:],
                                                                       initial=0.0, op0=ALU.mult, op1=ALU.add),
                   reads=[b_sm, b_sgz], writes=[b_bv])
            k.act(e1[:], bv[:], AF.Exp, [b_bv], [b_e1])
            k.act(e2[:], bv[:], AF.Exp, [b_bv], [b_e2], scale=-1.0)
            k.tt(sgz[:], bv[:], sgz[:], ALU.subtract, [b_bv, b_sgz], [b_sgz])
            k.act(e3[:], sgz[:], AF.Exp, [b_sgz], [b_e3])
            for c in range(NCH):
                k.cp(ebC[:, h, c:c + 1], e1[:, c * CH + CH - 1:c * CH + CH], [b_e1], [b_ebC])
            k.mm(pq[0:64, 0:TBB], aup[:, hs], ta[:], True, True, [b_aup, b_ta], [b_pq])
            k.act(av[:], pq[0:64, 0:TBB], AF.Sigmoid, [b_pq, b_hp], [b_av], bias=hp[:, 1, h:h + 1])
            k.ts(kkv[:], kT[:, h, :], hp[:, 2, h:h + 1], None, ALU.mult, None, [b_kT, b_hp], [b_kkv])
            k.tt(tq[:], kkv[:], kkv[:], ALU.mult, [b_kkv], [b_tq])
            k.mm(pq[0:64, 0:TBB], ones[:], tq[:], True, True, [b_ones, b_tq], [b_pq])
            k.ts(tq[:], pq[0:64, 0:TBB], 1e-24, None, ALU.max, None, [b_pq], [b_tq])
            k.act(tq[:], tq[:], AF.Sqrt, [b_tq], [b_tq])
            k.recip(tq[:], tq[:], [b_tq], [b_tq])
            k.tt(kkv[:], kkv[:], tq[:], ALU.mult, [b_kkv, b_tq], [b_kkv])
            k.stt(At[:, h, :], kkv[:], -1.0, e3[:], ALU.mult, ALU.mult, [b_kkv, b_e3], [b_At])
            k.tt(tq[:], kkv[:], av[:], ALU.mult, [b_kkv, b_av], [b_tq])
            k.tt(Bt[:, h, :], tq[:], e2[:], ALU.mult, [b_tq, b_e2], [b_Bt])
            k.ts(tq[:], av[:], -1.0, hp[:, 3, h:h + 1], ALU.add, ALU.mult, [b_av, b_hp], [b_tq])
            k.stt(kT[:, h, :], tq[:], 1.0, kT[:, h, :], ALU.add, ALU.mult, [b_tq, b_kT], [b_kT])
            k.stt(tq[:], rT[:, h, :], hp[:, 4, h:h + 1], kT[:, h, :], ALU.mult, ALU.mult, [b_rT, b_hp, b_kT], [b_tq])
            for c in range(NCH):
                k.mm(PT[:, c, 0:1], tq[:, c * CH:(c + 1) * CH], ones[:, 0:1], True, True, [b_tq, b_ones], [b_PT])
            for c in range(NCH):
                k.cp(rk[:, c, h:h + 1], PT[:, c, 0:1], [b_PT], [b_rk])
            k.tt(kT[:, h, :], kT[:, h, :], e2[:], ALU.mult, [b_kT, b_e2], [b_kT])
            k.tt(rT[:, h, :], rT[:, h, :], e1[:], ALU.mult, [b_rT, b_e1], [b_rT])
            for c in range(NCH):
                k.tr(PT[:, c, :], vT[:, h, c * CH:(c + 1) * CH], idf[0:64, 0:64], [b_vT, b_idf], [b_PT])
            k.cp(Vt[:, :, h, :], PT[:, 0:NCH, :], [b_PT], [b_Vt])
        for c in range(NCH):
            for j in range(4):
                k.mm(pq[0:64, 0:HB * 64], tg[:, j, c * CH:(c + 1) * CH], gup[:, j, :], j == 0, j == 3,
                     [b_tg, b_gup], [b_pq])
            k.act(gt[:, c, :], pq[0:64, 0:HB * 64], AF.Copy, [b_pq], [b_gt])
        for c in range(NCH):
            cs = slice(c * CH, (c + 1) * CH)
            for grp in range(2):
                hh = [grp * 3 + i for i in range(3)]
                for i, h in enumerate(hh):
                    k.mm(PA[:, i * 5 + 0, :], Bt[:, h, cs], At[:, h, cs], True, True, [b_Bt, b_At], [b_PA])
                    k.mm(PA[:, i * 5 + 1, :], kT[:, h, cs], At[:, h, cs], True, True, [b_kT, b_At], [b_PA])
                    k.mm(PA[:, i * 5 + 2, :], Bt[:, h, cs], rT[:, h, cs], True, True, [b_Bt, b_rT], [b_PA])
                    k.mm(PA[:, i * 5 + 3, :], kT[:, h, cs], rT[:, h, cs], True, True, [b_kT, b_rT], [b_PA])
                    k.mm(PA[:, i * 5 + 4, :], At[:, h, cs], Bt[:, h, cs], True, True, [b_Bt, b_At], [b_PA])
                k.tt(mats[:].rearrange("p a b c -> p (a b) c"), PA[:, 0:15, :], m5[:].rearrange("p a b c -> p (a b) c"), ALU.mult, [b_PA, b_m5], [b_mats])
                k.cp(Nsb[:, :, 0, :], mats[:, :, 4, :], [b_mats], [b_N])
                k.cp(Nsb[:, :, 1, :], mats[:, :, 0, :], [b_mats], [b_N])
                for i in range(3):
                    k.tt(X[:, i, :], mats[:, i, 0, :], idf[0:64, 0:64], ALU.add, [b_mats, b_idf], [b_X])
                for step in range(5):
                    for i in range(3):
                        k.mm(PN[:, i * 2, :], Nsb[:, i, 1, :], Nsb[:, i, 0, :], True, True, [b_N], [b_PN])
                        if step < 4:
                            k.mm(PN[:, i * 2 + 1, :], Nsb[:, i, 0, :], Nsb[:, i, 1, :], True, True, [b_N], [b_PN])
                    k.cp(Nsb[:].rearrange("p a b c -> p (a b) c"), PN[:, 0:6, :], [b_PN], [b_N])
                    for i in range(3):
                        k.mm(PXW[:, i, :], Nsb[:, i, 0, :], X[:, i, :], True, True, [b_N, b_X], [b_PX])
                    k.tt(X[:], X[:], PXW[:, 0:3, :], ALU.add, [b_X, b_PX], [b_X])
                for i, h in enumerate(hh):
                    k.mm(PXW[:, 3 + i, :], At[:, h, cs], H[:, h, :], True, False, [b_At, b_H], [b_PW])
                    k.mm(PXW[:, 3 + i, :], mats[:, i, 1, :], Vt[:, c, h, :], False, True, [b_mats, b_Vt], [b_PW])
                k.cp(W[:], PXW[:, 3:6, :], [b_PW], [b_W])
                for i in range(3):
                    k.mm(PXW[:, 3 + i, :], X[:, i, :], W[:, i, :], True, True, [b_X, b_W], [b_PW])
                k.cp(U[:], PXW[:, 3:6, :], [b_PW], [b_U])
                for i, h in enumerate(hh):
                    k.mm(PYH[:, i, :], rT[:, h, cs], H[:, h, :], True, False, [b_rT, b_H], [b_PY])
                    k.mm(PYH[:, i, :], mats[:, i, 2, :], U[:, i, :], False, False, [b_mats, b_U], [b_PY])
                    k.mm(PYH[:, i, :], mats[:, i, 3, :], Vt[:, c, h, :], False, True, [b_mats, b_Vt], [b_PY])
                k.cp(ysb[:, grp * 3:(grp + 1) * 3, :], PYH[:, 0:3, :], [b_PY], [b_ysb])
                for i, h in enumerate(hh):
                    k.tr(PT[:, i * 2, :], Bt[:, h, cs], idf[0:64, 0:64], [b_Bt, b_idf], [b_PT])
                    k.tr(PT[:, i * 2 + 1, :], kT[:, h, cs], idf[0:64, 0:64], [b_kT, b_idf], [b_PT])
                k.cp(BKt[:].rearrange("p a b c -> p (a b) c"), PT[:, 0:6, :], [b_PT], [b_BKt])
                for i, h in enumerate(hh):
                    k.mm(PYH[:, 3 + i, :], BKt[:, i, 0, :], U[:, i, :], True, False, [b_BKt, b_U], [b_PH])
                    k.mm(PYH[:, 3 + i, :], BKt[:, i, 1, :], Vt[:, c, h, :], False, True, [b_BKt, b_Vt], [b_PH])
                k.tt(H[:, grp * 3:(grp + 1) * 3, :], H[:, grp * 3:(grp + 1) * 3, :], PYH[:, 3:6, :], ALU.add,
                     [b_H, b_PH], [b_H])
                for i, h in enumerate(hh):
                    k.ts(H[:, h, :], H[:, h, :], ebC[:, h, c:c + 1], None, ALU.mult, None, [b_H, b_ebC], [b_H])
            for h in range(HB):
                k.P.op("dve", lambda e, h=h: e.bn_stats(out=stt_[:, h, 0:6], in_=ysb[:, h, :]),
                       reads=[b_ysb], writes=[b_stt])
                k.P.op("dve", lambda e, h=h: e.bn_aggr(out=mv[:, h, :], in_=stt_[:, h, 0:6]),
                       reads=[b_stt], writes=[b_mv])
            k.ts(mv[:, :, 1], mv[:, :, 1], 1.0, GN_EPS, ALU.mult, ALU.add, [b_mv], [b_mv])
            k.act(mv[:, :, 1], mv[:, :, 1], AF.Sqrt, [b_mv], [b_mv])
            k.recip(mv[:, :, 1], mv[:, :, 1], [b_mv], [b_mv])
            for h in range(HB):
                k.ts(t2[:, h, :], ysb[:, h, :], mv[:, h, 0:1], mv[:, h, 1:2], ALU.subtract, ALU.mult,
                     [b_ysb, b_mv], [b_t2])
            t2f = t2[:].rearrange("p a b -> p (a b)")
            k.tt(t2f, t2f, lnw[:], ALU.mult, [b_t2, b_lnw], [b_t2])
            k.tt(t2f, t2f, lnb[:], ALU.add, [b_t2, b_lnb], [b_t2])
            for h in range(HB):
                k.stt(t2[:, h, :], Vt[:, c, h, :], rk[:, c, h:h + 1], t2[:, h, :], ALU.mult, ALU.add,
                      [b_Vt, b_rk, b_t2], [b_t2])
            y_, b_y = yt[yi % 2]
            yi += 1
            k.tt(y_[:], t2f, gt[:, c, :], ALU.mult, [b_t2, b_gt], [b_y])
            k.dma(yo[t0 + c * CH:t0 + (c + 1) * CH, :], y_[:], [b_y], [], q="actq")
    return k.finish()


_SMASKB = np.ones((64, TBB), np.float32)
_SMASKB[:, ::CH] = 0.0
_M5 = np.zeros((64, 3, 5, 64), np.float32)
_su = np.triu(np.ones((64, 64), np.float32), 1)
_iu = np.triu(np.ones((64, 64), np.float32), 0)
_M5[:, :, 0] = _su[:, None]
_M5[:, :, 1] = _su[:, None]
_M5[:, :, 2] = _iu[:, None]
_M5[:, :, 3] = _iu[:, None]
_M5[:, :, 4] = _su.T[:, None]


def run_1b(inp, layer, x, modv_l, seq=SEQ, runner=None, cores=range(8)):
    A_COLS = 1104
    W_B = 1536
    w_in_l = inp["w_in"][layer]
    mu = np.asarray(inp["rwkv_mu"][layer], np.float32)
    nc = _get_nc(('1b', seq), lambda: build_1b(seq))
    maps = []
    fm = lambda v: np.ascontiguousarray(np.asarray(v, np.float32).reshape(6, 64).T)
    for ci in cores:
        b, g = ci // 4, ci % 4
        hs = slice(g * 384, (g + 1) * 384)
        wB = w_in_l[:, A_COLS:A_COLS + 5344]
        wrkv = np.ascontiguousarray(np.concatenate([wB[:, j * W_B:(j + 1) * W_B][:, hs] for j in range(3)], axis=1))
        wlo = np.ascontiguousarray(wB[:, 3 * W_B:])
        mu_rkv = np.concatenate([mu[j * W_B:(j + 1) * W_B][hs] for j in range(3)]).reshape(18, 64).T
        mu_l = mu[3 * W_B:]
        mu_wa = np.stack([mu_l[0:128], mu_l[128:256]], axis=1)
        mu_g = mu_l[256:].reshape(4, 120).T
        hp = np.stack([fm(inp["rwkv_w0"][layer][hs]), fm(inp["rwkv_a0"][layer][hs]), fm(inp["rwkv_k_k"][layer][hs]),
                       fm(inp["rwkv_k_a"][layer][hs]), fm(inp["rwkv_r_k"][layer].reshape(-1)[hs])], axis=1)
        mr = np.ascontiguousarray(modv_l[b])
        mcm = np.ascontiguousarray(mr.reshape(6, 32, 128).transpose(0, 2, 1))
        maps.append({"x": np.ascontiguousarray(x[b, :seq]), "modc": mcm, "identf": _IDENT, "wrkv": wrkv, "wlo": wlo,
                     "mu_rkv": np.ascontiguousarray(mu_rkv), "mu_wa": np.ascontiguousarray(mu_wa),
                     "mu_g": np.ascontiguousarray(mu_g), "hp": np.ascontiguousarray(hp),
                     "wup": np.ascontiguousarray(inp["rwkv_w_up"][layer][:, hs]),
                     "aup": np.ascontiguousarray(inp["rwkv_a_up"][layer][:, hs]),
                     "gup": np.ascontiguousarray(inp["rwkv_g_up"][layer][:, hs].reshape(4, 120, 384).transpose(1, 0, 2)),
                     "lnw": np.ascontiguousarray(inp["rwkv_lnx_w"][layer][hs]),
                     "lnb": np.ascontiguousarray(inp["rwkv_lnx_b"][layer][hs]),
                     "smask": _SMASKB, "m5": _M5})
    if runner is not None:
        return runner(nc, maps)
    res = run_bass_kernel_spmd(nc, maps, core_ids=list(range(8)))
    out = np.empty((2, 8192, 1536), ml_dtypes.bfloat16)
    for ci in range(8):
        b, g = ci // 4, ci % 4
        out[b, :, g * 384:(g + 1) * 384] = res.results[ci]["yb"]
    return out


def build_1ak(ntok=2048):
    k = K()
    x = k.din("x", [ntok, D])
    modc = k.din("modc", [6, 128, 32])
    idf_d = k.din("identf", [128, 128])
    wkv = k.din("wkv", [D, 320])
    kvg = k.din("kvg", [256])
    o_ct = k.dout("ckvnT", [128, 2, ntok], BF16)
    o_v = k.dout("V", [ntok, 256], BF16)
    o_ki = k.dout("kidxT", [64, ntok], BF16)
    hm = HTMaker(k, modc, idf_d)
    wl = WLoader(k, 128, 2)
    hT, b_hT = k.sb([128, 32, TB], BF16)
    wres, b_wres = k.sb([128, 32, 320], BF16)
    gb, b_gb = k.sb([128, 256])
    k.dma(gb[:], kvg.partition_broadcast(128), [], [b_gb], q="actq")
    for j, (c0, n) in enumerate(((0, 128), (128, 128), (256, 64))):
        wb, b_wb = wl.load(wkv, c0, n)
        k.cp(wres[:, :, c0:c0 + n], wb[:, :, 0:n], [b_wb], [b_wres], eng="pool")
    pkv = [k.ps([128, 512]) for _ in range(2)]
    pki, b_pki = k.ps([64, 512])
    ptb, b_ptb = k.ps([128, 2, 128], BF16)
    st, b_st = k.sb([128, 1])
    junk, b_junk = k.sb([128, 256])
    vt = [k.sb([128, 256], BF16) for _ in range(2)]
    ct = [k.sb([128, 2, 128], BF16) for _ in range(2)]
    kit = [k.sb([64, TB], BF16) for _ in range(2)]
    n_ = 0
    for tb in range(ntok // TB):
        t0 = tb * TB
        for mt in range(TB // 128):
            hm.emit(x[t0 + mt * 128:t0 + (mt + 1) * 128, :],
                    lambda q, mt=mt: hT[:, q, mt * 128:(mt + 1) * 128], b_hT)
        for mt in range(TB // 128):
            p_, b_p = pkv[n_ % 2]
            v_, b_v = vt[n_ % 2]
            c_, b_c = ct[n_ % 2]
            n_ += 1
            for kk in range(32):
                k.mm(p_[:, 0:256], hT[:, kk, mt * 128:(mt + 1) * 128], wres[:, kk, 0:256], kk == 0, kk == 31,
                     [b_hT, b_wres], [b_p])
            k.act(junk[:], p_[:, 0:256], AF.Square, [b_p], [b_junk, b_st], accum_out=st[:, 0:1])
            k.ts(st[:], st[:], 1.0 / 256, EPS, ALU.mult, ALU.add, [b_st], [b_st])
            k.act(st[:], st[:], AF.Sqrt, [b_st], [b_st])
            k.recip(st[:], st[:], [b_st], [b_st])
            k.stt(v_[:], p_[:, 0:256], st[:, 0:1], gb[:], ALU.mult, ALU.mult, [b_p, b_st, b_gb], [b_v])
            r0 = t0 + mt * 128
            k.dma(o_v[r0:r0 + 128, :], v_[:], [b_v], [], q="actq")
            for rc in range(2):
                k.tr(ptb[:, rc, :], v_[:, rc * 128:(rc + 1) * 128], hm.idb[:], [b_v, hm.b_idb], [b_ptb])
            k.cp(c_[:], ptb[:], [b_ptb], [b_c])
            k.dma(o_ct[:, :, r0:r0 + 128], c_[:], [b_c], [], q="actq")
        ki_, b_ki = kit[tb % 2]
        for kk in range(32):
            k.mm(pki[:], wres[:, kk, 256:320], hT[:, kk, :], kk == 0, kk == 31, [b_hT, b_wres], [b_pki])
        k.act(ki_[:], pki[:], AF.Copy, [b_pki], [b_ki])
        k.dma(o_ki[:, t0:t0 + TB], ki_[:], [b_ki], [], q="actq")
    return k.finish()


def run_1ak(layer, x, modv_l, w_in_l, kvg):
    nc = _get_nc('1ak', build_1ak)
    maps = []
    wkv = np.ascontiguousarray(w_in_l[:, 768:768 + 320])
    for ci in range(8):
        b, q = ci // 4, ci % 4
        mr = np.ascontiguousarray(modv_l[b])
        mcm = np.ascontiguousarray(mr.reshape(6, 32, 128).transpose(0, 2, 1))
        maps.append({"x": np.ascontiguousarray(x[b, q * 2048:(q + 1) * 2048]), "modc": mcm, "identf": _IDENT,
                     "wkv": wkv, "kvg": np.ascontiguousarray(kvg)})
    res = run_bass_kernel_spmd(nc, maps, core_ids=list(range(8)))
    ckvnT = np.empty((2, 128, 2, 8192), ml_dtypes.bfloat16)
    V = np.empty((2, 8192, 256), ml_dtypes.bfloat16)
    kidxT = np.empty((2, 64, 8192), ml_dtypes.bfloat16)
    for ci in range(8):
        b, q = ci // 4, ci % 4
        sl = slice(q * 2048, (q + 1) * 2048)
        ckvnT[b, :, :, sl] = res.results[ci]["ckvnT"]
        V[b, sl] = res.results[ci]["V"]
        kidxT[b, :, sl] = res.results[ci]["kidxT"]
    return ckvnT, V, kidxT


NEG_SEL = -3.0e38
NEG_CAUSAL = -1.0e30
NEG_LOGIT = -30000.0


def build_1aq(nq=16, seq=SEQ):
    k = K()
    xq = k.din("xq", [nq * 128, D])
    modc = k.din("modc", [6, 128, 32])
    idf_d = k.din("identf", [128, 128])
    wq = k.din("wq", [D, 784])
    qg_d = k.din("qg", [128, 6])
    wqidx_d = k.din("wqidx", [128, 6, 1024])
    wuq_d = k.din("wuq", [128, 6, 1024])
    wukT_d = k.din("wukT", [128, 8, 256])
    wuv_d = k.din("wuv", [128, 2, 8, 128])
    rb_d = k.din("rb", [32, 8])
    ohv_d = k.din("ohv", [32, 384])
    kvg_d = k.din("kvg", [256])
    ckT_d = k.din("ckvnT", [128, 2, seq], BF16)
    V_d = k.din("V", [seq, 256], BF16)
    ki_d = k.din("kidxT", [64, seq], BF16)
    tpos_d = k.din("tposrel", [128, 1])
    iota_d = k.din("iota", [128, 512])
    sel_d = k.din("sel", [128, 10])
    J_d = k.din("J", [128, 128])
    ya = k.dout("ya", [nq * 128, 1024], BF16)
    vd = k.dscr("vd", [8, 384])
    b_vd = Buf()

    hm = HTMaker(k, modc, idf_d, inplace=True)
    wl = WLoader(k, 128, 1, nst=1)
    stg, b_stg = wl.st[0]
    stgf = stg[:].rearrange("p a b -> p (a b)")
    hqT, b_hqT = k.sb([128, 32, 128], BF16)
    qg, b_qg = k.sb([128, 6])
    wqidx, b_wqidx = k.sb([128, 6, 1024], BF16)
    wuq, b_wuq = k.sb([128, 6, 1024], BF16)
    wukT, b_wukT = k.sb([128, 8, 256], BF16)
    wuv, b_wuv = k.sb([128, 2, 8, 128], BF16)
    tpos, b_tpos = k.sb([128, 1])
    sel, b_sel = k.sb([128, 10])
    Jf, b_Jf = k.sb([128, 128])
    Jb, b_Jb = k.sb([128, 128], BF16)
    I4, b_I4 = k.sb([128, 4, 128], BF16)
    onesb, b_onesb = k.sb([128, 128], BF16)
    cb, b_cb = k.sb([128, 512])
    cbb, b_cbb = k.sb([128, 512], BF16)
    rb, b_rb = k.sb([32, 8])
    rb31, b_rb31 = k.sb([32, 8])
    ohv, b_ohv = k.sb([32, 384])
    vts, b_vts = k.sb([8, 384])
    r1, b_r1 = k.sb([1, 512])
    fac, b_fac = k.sb([1, 4])
    TBsel, b_TBsel = k.sb([128, 5, 1024], BF16)
    score, b_score = k.sb([128, seq])
    assert seq >= 3584 or True
    TBr = [(score[:, 0:1024], b_score), (score[:, 1024:2048], b_score)] if seq >= 4096 else [k.sb([128, 1024]) for _ in range(2)]
    if seq >= 4096:
        TBt, b_TBt = score[:, 2048:3072], b_score
        iota, b_iota = score[:, 3072:3584], b_score
    else:
        TBt, b_TBt = k.sb([128, 1024])
        iota, b_iota = k.sb([128, 512])

    pA, b_pA = k.ps([128, 512])
    pw = [k.ps([128, 512]) for _ in range(2)]
    pO = [k.ps([128, 512]) for _ in range(2)]
    pZ, b_pZ = k.ps([128, 512])

    selb, b_selb = k.sb([128, seq], BF16)
    cqs, b_cqs = k.sb([128, 784])
    widx, b_widx = k.sb([128, 16])
    cqT, b_cqT = k.sb([128, 6, 128], BF16)
    qiT, b_qiT = k.sb([64, 16, 128], BF16)
    qT, b_qT = k.sb([128, 8, 128], BF16)
    qlT, b_qlT = k.sb([128, 2, 8, 128], BF16)
    sq, b_sq = k.sb([128, 2, 1024], BF16)
    srow, b_srow = k.sb([1, 1024])
    nsr, b_nsr = k.sb([1, 1024], BF16)
    st, b_st = k.sb([128, 1])
    m8, b_m8 = k.sb([128, 8])
    kit = [k.sb([64, 512], BF16) for _ in range(2)]
    ckt = [k.sb([128, 2, 512], BF16) for _ in range(2)]
    Vt = [k.sb([128, 4, 256], BF16) for _ in range(2)]
    rl = [k.sb([128, 512]) for _ in range(2)]
    PT = [k.sb([128, 512], BF16) for _ in range(2)]
    rZ, b_rZ = k.sb([128, 512])
    olT, b_olT = k.sb([128, 2, 512], BF16)
    yat, b_yat = k.sb([128, 1024], BF16)

    for dst, src, bd in ((qg, qg_d, b_qg), (tpos, tpos_d, b_tpos), (sel, sel_d, b_sel),
                         (Jf, J_d, b_Jf), (rb, rb_d, b_rb), (ohv, ohv_d, b_ohv)):
        k.dma(dst[:], src, [], [bd], q="actq")
    k.dma(iota[:] if seq < 4096 else iota, iota_d, [], [b_iota], q="actq")
    k.dma(rb31[:], rb_d[31, :].partition_broadcast(32), [], [b_rb31], q="actq")
    k.cp(Jb[:], Jf[:], [b_Jf], [b_Jb])
    for c in range(4):
        k.cp(I4[:, c, :], hm.idf[:], [hm.b_idf], [b_I4])
    k.memset(onesb[:], 1.0, [b_onesb])
    for half in range(2):
        k.dma(stgf[:, 0:3072], wqidx_d[:, half * 3:(half + 1) * 3, :].rearrange("p a b -> p (a b)"), [], [b_stg])
        k.cp(wqidx[:, half * 3:(half + 1) * 3, :].rearrange("p a b -> p (a b)"), stgf[:, 0:3072], [b_stg], [b_wqidx], eng="pool")
    for half in range(2):
        k.dma(stgf[:, 0:3072], wuq_d[:, half * 3:(half + 1) * 3, :].rearrange("p a b -> p (a b)"), [], [b_stg])
        k.cp(wuq[:, half * 3:(half + 1) * 3, :].rearrange("p a b -> p (a b)"), stgf[:, 0:3072], [b_stg], [b_wuq], eng="pool")
    k.dma(stgf[:, 0:2048], wukT_d.rearrange("p a b -> p (a b)"), [], [b_stg])
    k.cp(wukT[:].rearrange("p a b -> p (a b)"), stgf[:, 0:2048], [b_stg], [b_wukT], eng="pool")
    k.dma(stgf[:, 0:2048], wuv_d.rearrange("p a b c -> p (a b c)"), [], [b_stg])
    k.cp(wuv[:].rearrange("p a b c -> p (a b c)"), stgf[:, 0:2048], [b_stg], [b_wuv], eng="pool")
    k.ts(cb[:], iota[:] if seq < 4096 else iota, tpos[:, 0:1], NEG_CAUSAL, ALU.is_gt, ALU.mult, [b_iota, b_tpos], [b_cb])
    k.ts(cbb[:], iota[:] if seq < 4096 else iota, tpos[:, 0:1], NEG_LOGIT, ALU.is_gt, ALU.mult, [b_iota, b_tpos], [b_cbb])
    k.dma(r1[:, 0:256], kvg_d[None, :], [], [b_r1], q="actq")
    k.P.op("dve", lambda e: e.tensor_reduce(out=fac[:, 0:1], in_=r1[:, 0:256], axis=AX.X, op=ALU.max,
                                            apply_absolute_value=True), reads=[b_r1], writes=[b_fac])
    k.dma(r1[:, 256:512], rb_d.rearrange("a b -> (a b)")[None, :], [], [b_r1], q="actq")
    k.P.op("dve", lambda e: e.tensor_reduce(out=fac[:, 1:2], in_=r1[:, 256:512], axis=AX.X, op=ALU.max,
                                            apply_absolute_value=True), reads=[b_r1], writes=[b_fac])
    k.ts(fac[:, 2:3], fac[:, 0:1], -16.0, None, ALU.mult, None, [b_fac], [b_fac])
    k.ts(fac[:, 3:4], fac[:, 1:2], -2.0, None, ALU.mult, None, [b_fac], [b_fac])
    k.tt(rb[:], rb[:], rb31[:], ALU.subtract, [b_rb, b_rb31], [b_rb])
    k.mm(pA[0:8, 0:384], rb[:], ohv[:], True, True, [b_rb, b_ohv], [b_pA])
    k.cp(vts[:], pA[0:8, 0:384], [b_pA], [b_vts])
    k.dma(vd, vts[:], [b_vts], [b_vd])
    for dl in range(2):
        t_, b_t = TBr[dl]
        src = bass.AP(tensor=vd.tensor, offset=128 * dl + 1, ap=[[1, 128], [384, 8], [1, 128]])
        k.dma(t_.rearrange("p (a b) -> p a b", a=8) if seq >= 4096 else t_[:].rearrange("p (a b) -> p a b", a=8), src, [b_vd], [b_t])
    for j in range(5):
        k.ts(TBt[:] if seq < 4096 else TBt, TBr[0][0][:] if seq < 4096 else TBr[0][0], sel[:, 2 * j:2 * j + 1], None, ALU.mult, None, [TBr[0][1], b_sel], [b_TBt])
        k.stt(TBsel[:, j, :], TBr[1][0][:] if seq < 4096 else TBr[1][0], sel[:, 2 * j + 1:2 * j + 2], TBt[:] if seq < 4096 else TBt, ALU.mult, ALU.add,
              [TBr[1][1], b_sel, b_TBt], [b_TBsel])

    inv_sqrt_d = float(128 ** -0.5)
    for qi in range(nq):
        Lk = 512 * (qi + 1)
        nkb = 4 * (qi + 1)
        hm.emit(xq[qi * 128:(qi + 1) * 128, :], lambda q: hqT[:, q, :], b_hqT)
        for j in range(7):
            n = 128 if j < 6 else 16
            wb, b_wb = wl.load(wq, j * 128, n)
            for kk in range(32):
                k.mm(pA[:, 0:n], hqT[:, kk, :], wb[:, kk, 0:n], kk == 0, kk == 31, [b_hqT, b_wb], [b_pA])
            k.act(cqs[:, j * 128:j * 128 + n], pA[:, 0:n], AF.Copy, [b_pA], [b_cqs])
        k.act(hm.junk[:, 0:768], cqs[:, 0:768], AF.Square, [b_cqs], [hm.b_junk, b_st], accum_out=st[:, 0:1])
        k.ts(st[:], st[:], 1.0 / 768, EPS, ALU.mult, ALU.add, [b_st], [b_st])
        k.act(st[:], st[:], AF.Sqrt, [b_st], [b_st])
        k.recip(st[:], st[:], [b_st], [b_st])
        k.ts(widx[:], cqs[:, 768:784], 1.0 / 32.0, None, ALU.mult, None, [b_cqs], [b_widx])
        k.ts(cqs[:, 0:768], cqs[:, 0:768], st[:, 0:1], None, ALU.mult, None, [b_cqs, b_st], [b_cqs])
        for c0, nchunk in ((0, 4), (4, 2)):
            for c in range(nchunk):
                k.tr(hm.ptf[:, c, :], cqs[:, (c0 + c) * 128:(c0 + c + 1) * 128], hm.idf[:], [b_cqs, hm.b_idf], [hm.b_ptf])
            for c in range(nchunk):
                k.act(cqT[:, c0 + c, :], hm.ptf[:, c, :], AF.Identity, [hm.b_ptf, b_qg], [b_cqT],
                      scale=qg[:, c0 + c:c0 + c + 1])
        for h4 in range(4):
            for hl in range(4):
                h = h4 * 4 + hl
                for rc in range(6):
                    k.mm(pA[0:64, hl * 128:(hl + 1) * 128], wqidx[:, rc, h * 64:(h + 1) * 64], cqT[:, rc, :],
                         rc == 0, rc == 5, [b_wqidx, b_cqT], [b_pA])
            k.act(qiT[:, h4 * 4:(h4 + 1) * 4, :].rearrange("p a b -> p (a b)"), pA[0:64, :], AF.Copy, [b_pA], [b_qiT])
        for h4 in range(2):
            for hl in range(4):
                h = h4 * 4 + hl
                for rc in range(6):
                    k.mm(pA[:, hl * 128:(hl + 1) * 128], wuq[:, rc, h * 128:(h + 1) * 128], cqT[:, rc, :],
                         rc == 0, rc == 5, [b_wuq, b_cqT], [b_pA])
            k.act(qT[:, h4 * 4:(h4 + 1) * 4, :].rearrange("p a b -> p (a b)"), pA[:], AF.Copy, [b_pA], [b_qT])
        for rc2 in range(2):
            for h4 in range(2):
                for hl in range(4):
                    h = h4 * 4 + hl
                    k.mm(pA[:, hl * 128:(hl + 1) * 128], wukT[:, h, rc2 * 128:(rc2 + 1) * 128], qT[:, h, :],
                         True, True, [b_wukT, b_qT], [b_pA])
                k.act(qlT[:, rc2, h4 * 4:(h4 + 1) * 4, :].rearrange("p a b -> p (a b)"), pA[:], AF.Copy,
                      [b_pA], [b_qlT], scale=inv_sqrt_d)
        qlf = qlT[:].rearrange("p a b c -> p a (b c)")
        k.tt(sq[:], qlf, qlf, ALU.mult, [b_qlT], [b_sq])
        for hp in range(2):
            for rc2 in range(2):
                k.mm(pA[:], onesb[:], sq[:, rc2, hp * 512:(hp + 1) * 512], rc2 == 0, rc2 == 1, [b_onesb, b_sq], [b_pA])
            k.act(srow[:, hp * 512:(hp + 1) * 512], pA[0:1, :], AF.Sqrt, [b_pA], [b_srow])
        k.ts(nsr[:], srow[:], fac[:, 2:3], fac[:, 3:4], ALU.mult, ALU.add, [b_srow, b_fac], [b_nsr])
        ih = 0
        for kt in range(qi + 1):
            ki_, b_ki = kit[kt % 2]
            k.dma(ki_[:], ki_d[:, kt * 512:(kt + 1) * 512], [], [b_ki])
            cs = slice(kt * 512, (kt + 1) * 512)
            for h in range(16):
                p_, b_p = pw[ih % 2]
                r_, b_r = rl[ih % 2]
                ih += 1
                k.mm(p_[:], qiT[:, h, :], ki_[:], True, True, [b_qiT, b_ki], [b_p])
                k.act(r_[:], p_[:], AF.Relu, [b_p], [b_r])
                if h == 0:
                    k.ts(score[:, cs], r_[:], widx[:, 0:1], None, ALU.mult, None, [b_r, b_widx], [b_score])
                else:
                    k.stt(score[:, cs], r_[:], widx[:, h:h + 1], score[:, cs], ALU.mult, ALU.add,
                          [b_r, b_widx, b_score], [b_score])
        last = slice(Lk - 512, Lk)
        k.tt(score[:, last], score[:, last], cb[:], ALU.add, [b_score, b_cb], [b_score])
        for r in range(32):
            k.P.op("dve", lambda e, Lk=Lk: e.max(out=m8[:], in_=score[:, 0:Lk]), reads=[b_score], writes=[b_m8])
            k.P.op("dve", lambda e, Lk=Lk: e.match_replace(out=score[:, 0:Lk], in_to_replace=m8[:],
                                                          in_values=score[:, 0:Lk], imm_value=NEG_SEL),
                   reads=[b_score, b_m8], writes=[b_score])
        k.ts(selb[:, 0:Lk], score[:, 0:Lk], -2.0e38, NEG_LOGIT, ALU.is_gt, ALU.mult, [b_score], [b_selb])
        k.tt(selb[:, last], selb[:, last], cbb[:], ALU.add, [b_selb, b_cbb], [b_selb])
        it = 0
        for hp in range(2):
            qlh = [qlT[:, rc2, hp * 4:(hp + 1) * 4, :].rearrange("p a b -> p (a b)") for rc2 in range(2)]
            for kt in range(qi + 1):
                c_, b_c = ckt[it % 2]
                v_, b_v = Vt[it % 2]
                it += 1
                k.dma(c_[:], ckT_d[:, :, kt * 512:(kt + 1) * 512], [], [b_c])
                k.dma(v_[:], V_d[kt * 512:(kt + 1) * 512, :].rearrange("(c p) r -> p c r", p=128), [], [b_v])
                for kb4 in range(4):
                    kb = kt * 4 + kb4
                    pl, b_pl = pw[kb % 2]
                    pt_, b_pt = PT[kb % 2]
                    ks = slice(kb4 * 128, (kb4 + 1) * 128)
                    k.mm(pl[:], c_[:, 0, ks], qlh[0], True, False, [b_c, b_qlT], [b_pl])
                    k.mm(pl[:], c_[:, 1, ks], qlh[1], False, False, [b_c, b_qlT], [b_pl])
                    k.mm(pl[:], selb[:, kb * 128:(kb + 1) * 128], I4[:].rearrange("p a b -> p (a b)"), False, False,
                         [b_selb, b_I4], [b_pl])
                    j = kb - (4 * qi - 1)
                    if 0 <= j <= 4:
                        k.mm(pl[:], Jb[:], TBsel[:, j, hp * 512:(hp + 1) * 512], False, False, [b_Jb, b_TBsel], [b_pl])
                    k.mm(pl[:], onesb[0:1, :], nsr[0:1, hp * 512:(hp + 1) * 512], False, True, [b_onesb, b_nsr], [b_pl])
                    k.act(pt_[:], pl[:], AF.Exp, [b_pl], [b_pt])
                    first, lastkb = kb == 0, kb == nkb - 1
                    k.mm(pO[0][0][:], v_[:, kb4, 0:128], pt_[:], first, lastkb, [b_v, b_pt], [pO[0][1]])
                    k.mm(pO[1][0][:], v_[:, kb4, 128:256], pt_[:], first, lastkb, [b_v, b_pt], [pO[1][1]])
                    k.mm(pZ[:], onesb[:], pt_[:], first, lastkb, [b_onesb, b_pt], [b_pZ])
            k.recip(rZ[:], pZ[:], [b_pZ], [b_rZ])
            for rc in range(2):
                k.tt(olT[:, rc, :], pO[rc][0][:], rZ[:], ALU.mult, [pO[rc][1], b_rZ], [b_olT])
            for hl in range(4):
                h = hp * 4 + hl
                for rc in range(2):
                    k.mm(pA[:, hl * 128:(hl + 1) * 128], olT[:, rc, hl * 128:(hl + 1) * 128], wuv[:, rc, h, :],
                         rc == 0, rc == 1, [b_olT, b_wuv], [b_pA])
            k.act(yat[:, hp * 512:(hp + 1) * 512], pA[:], AF.Copy, [b_pA], [b_yat])
        k.dma(ya[qi * 128:(qi + 1) * 128, :], yat[:], [b_yat], [], q="actq")
    return k.finish()


def _rel_bucket_np(n):
    n = np.asarray(n, np.int32)
    nf = np.maximum(n, 1).astype(np.float32)
    large = 16 + (np.log(nf / np.float32(16)) / np.float32(np.log(128 / 16)) * np.float32(16)).astype(np.int32)
    large = np.minimum(large, 31)
    return np.where(n < 16, n, large)


_OHV = np.zeros((32, 384), np.float32)
for _n in range(256):
    _OHV[int(_rel_bucket_np(_n)), _n + 128] = 1.0
_J = np.ascontiguousarray(np.eye(128, dtype=np.float32)[::-1])
_IOTA = np.ascontiguousarray(np.broadcast_to(np.arange(512, dtype=np.float32)[None], (128, 512)))


def run_1aq(inp, layer, x, modv_l, ckvnT, V, kidxT, nq=16, runner=None, cores=range(8)):
    l = layer
    seq = nq * 512
    w_in_l = inp["w_in"][l]
    nc = _get_nc(('1aq', nq, seq), lambda: build_1aq(nq, seq))
    wq = np.ascontiguousarray(np.concatenate([w_in_l[:, 0:768], w_in_l[:, 1088:1104]], axis=1))
    qg = np.ascontiguousarray(np.asarray(inp["mla_q_norm"][l], np.float32).reshape(6, 128).T)
    wqidx = np.ascontiguousarray(np.asarray(inp["w_qidx"][l]).reshape(6, 128, 1024).transpose(1, 0, 2))
    wuq = np.ascontiguousarray(np.asarray(inp["w_uq"][l]).reshape(6, 128, 1024).transpose(1, 0, 2))
    wukT = np.ascontiguousarray(np.asarray(inp["w_uk"][l]).transpose(2, 0, 1))
    wuv = np.ascontiguousarray(np.asarray(inp["w_uv"][l]).reshape(8, 2, 128, 128).transpose(2, 1, 0, 3))
    maps = []
    for ci in cores:
        b, g = ci // 4, ci % 4
        mr = np.ascontiguousarray(modv_l[b])
        mcm = np.ascontiguousarray(mr.reshape(6, 32, 128).transpose(0, 2, 1))
        xq = np.ascontiguousarray(x[b].reshape(64, 128, D)[g::4][:nq].reshape(nq * 128, D))
        sel = np.zeros((128, 10), np.float32)
        for j in range(5):
            dl = g + 1 - j
            if dl in (0, 1):
                sel[:, 2 * j + dl] = 1.0
        tpos = (g * 128 + np.arange(128, dtype=np.float32))[:, None]
        maps.append({"xq": xq, "modc": mcm, "identf": _IDENT, "wq": wq, "qg": qg, "wqidx": wqidx, "wuq": wuq,
                     "wukT": wukT, "wuv": wuv, "rb": np.ascontiguousarray(inp["rel_bias"]), "ohv": _OHV,
                     "kvg": np.ascontiguousarray(inp["mla_kv_norm"][l]),
                     "ckvnT": np.ascontiguousarray(ckvnT[b][:, :, :seq]), "V": np.ascontiguousarray(V[b][:seq]),
                     "kidxT": np.ascontiguousarray(kidxT[b][:, :seq]),
                     "tposrel": np.ascontiguousarray(tpos), "iota": _IOTA, "sel": sel, "J": _J})
    if runner is not None:
        return runner(nc, maps)
    res = run_bass_kernel_spmd(nc, maps, core_ids=list(range(8)))
    out = np.empty((2, 64, 128, 1024), ml_dtypes.bfloat16)
    for ci in range(8):
        b, g = ci // 4, ci % 4
        out[b, g::4] = res.results[ci]["ya"].reshape(16, 128, 1024)
    return out.reshape(2, 8192, 1024)


def build_wc(rows, cols):
    k = K()
    w = k.din("w", [rows, cols])
    o = k.dout("wb", [rows, cols], BF16)
    CW = 2048
    st = [k.sb([128, CW]) for _ in range(3)]
    bf = [k.sb([128, CW], BF16) for _ in range(3)]
    engs = ["pool", "dve", "act"]
    i = 0
    for r in range(rows // 128):
        for c in range(cols // CW):
            s_, b_s = st[i % 3]
            o_, b_o = bf[i % 3]
            k.dma(s_[:], w[r * 128:(r + 1) * 128, c * CW:(c + 1) * CW], [], [b_s])
            e = engs[i % 3]
            if e == "act":
                k.act(o_[:], s_[:], AF.Copy, [b_s], [b_o])
            else:
                k.cp(o_[:], s_[:], [b_s], [b_o], eng=e)
            k.dma(o[r * 128:(r + 1) * 128, c * CW:(c + 1) * CW], o_[:], [b_o], [], q="actq")
            i += 1
    return k.finish()


_NC_CACHE = {}


def _get_nc(key, fn):
    if key not in _NC_CACHE:
        _NC_CACHE[key] = fn()
    return _NC_CACHE[key]


def run_wc(w):
    R, C = w.shape
    rs = R // 8
    nc = _get_nc(("wc", rs, C), lambda: build_wc(rs, C))
    maps = [{"w": np.ascontiguousarray(w[ci * rs:(ci + 1) * rs])} for ci in range(8)]
    res = run_bass_kernel_spmd(nc, maps, core_ids=list(range(8)))
    return np.concatenate([r["wb"] for r in res.results], axis=0)


def kernel(**inp):
    x = np.ascontiguousarray(np.asarray(inp["x"], np.float32))
    modv = run_mod(inp)
    for l in range(2):
        w_in_l = inp["w_in"][l]
        ckvnT, V, kidxT = run_1ak(l, x, modv[l], w_in_l, inp["mla_kv_norm"][l])
        ya = run_1aq(inp, l, x, modv[l], ckvnT, V, kidxT)
        yb = run_1b(inp, l, x, modv[l])
        yc = run_1c(l, x, modv[l], w_in_l, inp["hgrn_lb"], inp["hgrn_onorm"][l])
        y = np.ascontiguousarray(np.concatenate([ya, yb, yc], axis=-1))
        wo = run_wc(np.asarray(inp["w_out"][l]))
        w1 = run_wc(np.asarray(inp["w_ff1"][l]))
        w2 = run_wc(np.asarray(inp["w_ff2"][l]))
        x = run_l2(x, y, modv[l], wo, w1, w2)
    return x
```

```python
import numpy as np
from contextlib import ExitStack
import ml_dtypes
import concourse.bass as bass
import concourse.mybir as mybir
from concourse.bass_utils import run_bass_kernel_spmd

F32 = mybir.dt.float32
BF16 = mybir.dt.bfloat16
AF = mybir.ActivationFunctionType
ALU = mybir.AluOpType
AX = mybir.AxisListType

D = 4096
SEQ = 8192
NB = 2
DFF = 16384
EPS = 1e-6


class Buf:
    __slots__ = ("name", "last_w", "readers")

    def __init__(self, name=""):
        self.name = name
        self.last_w = None
        self.readers = []


class Prog:
    COMPUTE = ("pe", "act", "dve", "pool")
    QUEUES = {"sp": 24, "actq": 8, "poolq": 8}
    Q2ENG = {"sp": "sp", "actq": "act", "poolq": "pool"}

    def __init__(self, nc):
        self.nc = nc
        self.streams = {e: [] for e in ("pe", "act", "dve", "pool", "sp")}
        self.nops = {e: 0 for e in self.COMPUTE}
        self.marked = {e: set() for e in self.COMPUTE}
        self.dma_n = {q: 0 for q in self.QUEUES}
        self.dma_val = {}

    def _deps(self, eng, reads, writes):
        deps = []
        for b in reads:
            if b.last_w is not None:
                deps.append((b.last_w, True))
        for b in writes:
            if b.last_w is not None:
                deps.append((b.last_w, True))
            for r in b.readers:
                deps.append((r, False))
        out = {}
        for d, strong in deps:
            if d[0] == "c" and d[1] == eng:
                if eng == "pe":
                    continue
            out[d] = True
        return list(out.keys())

    def _finish(self, tok, reads, writes):
        for b in writes:
            b.last_w = tok
            b.readers = []
        for b in reads:
            if b not in writes:
                b.readers.append(tok)

    def op(self, eng, fn, reads=(), writes=()):
        deps = self._deps(eng, reads, writes)
        idx = self.nops[eng]
        self.nops[eng] += 1
        tok = ("c", eng, idx)
        for d in deps:
            if d[0] == "c":
                self.marked[d[1]].add(d[2])
        self.streams[eng].append(("op", fn, deps, tok))
        self._finish(tok, reads, writes)
        return tok

    def dma(self, q, fn, reads=(), writes=()):
        eng = self.Q2ENG[q]
        deps = self._deps("dma", reads, writes)
        n = self.dma_n[q]
        self.dma_n[q] += 1
        slot = n % self.QUEUES[q]
        prev = self.dma_val.get((q, slot), 0)
        val = prev + 16
        self.dma_val[(q, slot)] = val
        tok = ("d", q, slot, val)
        if prev > 0:
            deps.append(("d", q, slot, prev))
        for d in deps:
            if d[0] == "c":
                self.marked[d[1]].add(d[2])
        self.streams[eng].append(("dma", fn, deps, tok))
        self._finish(tok, reads, writes)
        return tok

    def emit(self):
        nc = self.nc
        with ExitStack() as es:
            csem = {e: es.enter_context(nc.semaphore("s_" + e)) for e in self.COMPUTE}
            dsem = {}
            for q, n in self.QUEUES.items():
                for s in range(n):
                    dsem[(q, s)] = es.enter_context(nc.semaphore("d_%s_%d" % (q, s)))
            cnt = {}
            for e in self.COMPUTE:
                c = 0
                m = self.marked[e]
                arr = np.zeros(self.nops[e] + 1, dtype=np.int64)
                for i in range(self.nops[e]):
                    if i in m:
                        c += 1
                    arr[i] = c
                cnt[e] = arr
            final_dma = dict(self.dma_val)
            block = es.enter_context(nc.Block())

            def run_stream(engname, engobj):
                known_c = {e: 0 for e in self.COMPUTE}
                known_d = {}
                for kind, fn, deps, tok in self.streams[engname]:
                    need_c = {}
                    need_d = {}
                    for d in deps:
                        if d[0] == "c":
                            v = int(cnt[d[1]][d[2]])
                            if v > known_c[d[1]] and v > need_c.get(d[1], 0):
                                need_c[d[1]] = v
                        else:
                            key = (d[1], d[2])
                            if d[3] > known_d.get(key, 0) and d[3] > need_d.get(key, 0):
                                need_d[key] = d[3]
                    for e, v in need_c.items():
                        engobj.wait_ge(csem[e], v)
                        known_c[e] = v
                    for key, v in need_d.items():
                        engobj.wait_ge(dsem[key], v)
                        known_d[key] = v
                    ins = fn(engobj)
                    if kind == "op":
                        if tok[2] in self.marked[tok[1]]:
                            ins.then_inc(csem[tok[1]], 1)
                    else:
                        ins.then_inc(dsem[(tok[1], tok[2])], 16)
                if engname == "sp":
                    for key, v in final_dma.items():
                        if v > known_d.get(key, 0):
                            engobj.wait_ge(dsem[key], v)

            @block.tensor
            def _(e):
                run_stream("pe", e)

            @block.vector
            def _(e):
                run_stream("dve", e)

            @block.scalar
            def _(e):
                run_stream("act", e)

            @block.gpsimd
            def _(e):
                run_stream("pool", e)

            @block.sync
            def _(e):
                run_stream("sp", e)


class K:
    def __init__(self):
        self.nc = bass.Bass("TRN2", target_bir_lowering=False)
        self.P = Prog(self.nc)
        self.es = ExitStack()
        self.n = 0

    def din(self, name, shape, dt=F32):
        return self.nc.dram_tensor(name, list(shape), dt, kind="ExternalInput").ap()

    def dout(self, name, shape, dt=F32):
        return self.nc.dram_tensor(name, list(shape), dt, kind="ExternalOutput").ap()

    def dscr(self, name, shape, dt=F32):
        return self.nc.dram_tensor(name, list(shape), dt, kind="Internal").ap()

    def sb(self, shape, dt=F32, name=None):
        self.n += 1
        t = self.es.enter_context(self.nc.sbuf_tensor(name or ("sb%d" % self.n), list(shape), dt))
        return t, Buf(name or "")

    def ps(self, shape, dt=F32, name=None):
        self.n += 1
        t = self.es.enter_context(self.nc.psum_tensor(name or ("ps%d" % self.n), list(shape), dt))
        return t, Buf(name or "")

    def mm(self, out, lhsT, rhs, start, stop, reads, writes):
        self.P.op("pe", lambda e: e.matmul(out, lhsT=lhsT, rhs=rhs, start=start, stop=stop),
                  reads=reads, writes=writes)

    def tr(self, out, in_, ident, reads, writes):
        self.P.op("pe", lambda e: e.transpose(out=out, in_=in_, identity=ident),
                  reads=reads, writes=writes)

    def act(self, out, in_, func, reads, writes, bias=None, scale=None, accum_out=None, eng="act"):
        kw = {}
        if bias is not None:
            kw["bias"] = bias
        if scale is not None:
            kw["scale"] = scale
        if accum_out is not None:
            kw["accum_out"] = accum_out
        self.P.op("act", lambda e: e.activation(out=out, in_=in_, func=func, **kw),
                  reads=reads, writes=writes)

    def tt(self, out, in0, in1, op, reads, writes, eng="dve"):
        self.P.op(eng, lambda e: e.tensor_tensor(out=out, in0=in0, in1=in1, op=op),
                  reads=reads, writes=writes)

    def ts(self, out, in0, s1, s2, op0, op1, reads, writes, eng="dve", accum_out=None):
        if op1 is None:
            self.P.op(eng, lambda e: e.tensor_scalar(out=out, in0=in0, scalar1=s1, scalar2=None, op0=op0),
                      reads=reads, writes=writes)
        elif accum_out is not None:
            self.P.op(eng, lambda e: e.tensor_scalar(out=out, in0=in0, scalar1=s1, scalar2=s2, op0=op0,
                                                     op1=op1, accum_out=accum_out),
                      reads=reads, writes=writes)
        else:
            self.P.op(eng, lambda e: e.tensor_scalar(out=out, in0=in0, scalar1=s1, scalar2=s2, op0=op0, op1=op1),
                      reads=reads, writes=writes)

    def stt(self, out, in0, scalar, in1, op0, op1, reads, writes, accum_out=None):
        if accum_out is None:
            self.P.op("dve", lambda e: e.scalar_tensor_tensor(out=out, in0=in0, scalar=scalar, in1=in1,
                                                              op0=op0, op1=op1),
                      reads=reads, writes=writes)
        else:
            self.P.op("dve", lambda e: e.scalar_tensor_tensor(out=out, in0=in0, scalar=scalar, in1=in1,
                                                              op0=op0, op1=op1, accum_out=accum_out),
                      reads=reads, writes=writes)

    def cp(self, out, in_, reads, writes, eng="dve"):
        self.P.op(eng, lambda e: e.tensor_copy(out=out, in_=in_), reads=reads, writes=writes)

    def recip(self, out, in_, reads, writes):
        self.P.op("dve", lambda e: e.reciprocal(out=out, in_=in_), reads=reads, writes=writes)

    def memset(self, ap, v, writes, eng="pool"):
        self.P.op(eng, lambda e: e.memset(ap, v), writes=writes)

    def dma(self, out, in_, reads, writes, q="sp", **kw):
        self.P.dma(q, lambda e: e.dma_start(out=out, in_=in_, **kw), reads=reads, writes=writes)

    def rstd(self, ssq, n, eps, tmpb=None):
        t, b = ssq
        self.ts(t, t, 1.0 / n, eps, ALU.mult, ALU.add, [b], [b])
        self.act(t, t, AF.Sqrt, [b], [b])
        self.recip(t, t, [b], [b])

    def finish(self):
        self.P.emit()
        self.es.close()
        return self.nc


def build_mod():
    k = K()
    cT = k.din("cT", [128, 32, 2])
    aw = k.din("aw", [2, 6, 4096, 512])
    ab = k.din("ab", [2, 6, 512])
    ng = k.din("ng", [2, 4, 512])
    out = k.dout("modv", [2, 2, 6, 512])
    ct, b_ct = k.sb([128, 32, 2])
    ca, b_ca = k.sb([128, 32, 2])
    sg, b_sg = k.sb([128, 32, 2])
    wb = [k.sb([128, 16, 512]) for _ in range(3)]
    abt, b_ab = k.sb([2, 2, 6, 512])
    ngt, b_ng = k.sb([2, 2, 4, 512])
    modt, b_mod = k.sb([2, 6, 512])
    res, b_res = k.sb([2, 2, 6, 512])
    pm = [k.ps([2, 512]) for _ in range(2)]
    k.dma(ct[:], cT, [], [b_ct])
    for b in range(2):
        k.dma(abt[b:b + 1], ab[None], [], [b_ab], q="actq")
        k.dma(ngt[b:b + 1], ng[None], [], [b_ng], q="actq")
    k.act(sg[:], ct[:], AF.Sigmoid, [b_ct], [b_sg])
    k.tt(ca[:], ct[:], sg[:], ALU.mult, [b_ct, b_sg], [b_ca])
    it = 0
    for l in range(2):
        for j in range(6):
            pt, b_pt = pm[(l * 6 + j) % 2]
            for hf in range(2):
                wt, b_wt = wb[it % 3]
                it += 1
                src = aw[l, j, hf * 2048:(hf + 1) * 2048, :].rearrange("(c p) n -> p c n", p=128)
                k.dma(wt[:], src, [], [b_wt])
                for c in range(16):
                    kk = hf * 16 + c
                    k.mm(pt[:], ca[:, kk, :], wt[:, c, :], kk == 0, kk == 31, [b_ca, b_wt], [b_pt])
            k.tt(modt[:, j, :], pt[:], abt[:, l, j, :], ALU.add, [b_pt, b_ab], [b_mod])
        k.stt(res[:, l, 0, :], modt[:, 1, :], 1.0, ngt[:, l, 0, :], ALU.add, ALU.mult, [b_mod, b_ng], [b_res])
        k.cp(res[:, l, 1, :], modt[:, 0, :], [b_mod], [b_res])
        k.tt(res[:, l, 2, :], modt[:, 2, :], ngt[:, l, 1, :], ALU.mult, [b_mod, b_ng], [b_res])
        k.stt(res[:, l, 3, :], modt[:, 4, :], 1.0, ngt[:, l, 2, :], ALU.add, ALU.mult, [b_mod, b_ng], [b_res])
        k.cp(res[:, l, 4, :], modt[:, 3, :], [b_mod], [b_res])
        k.tt(res[:, l, 5, :], modt[:, 5, :], ngt[:, l, 3, :], ALU.mult, [b_mod, b_ng], [b_res])
    k.dma(out.rearrange("l b v n -> b l v n"), res[:], [b_res], [])
    return k.finish()


def run_mod(inp):
    c = np.asarray(inp["c"], np.float32)
    cT = np.ascontiguousarray(c.reshape(2, 32, 128).transpose(2, 1, 0))
    ada_w = inp["ada_w"]
    ada_b = np.asarray(inp["ada_b"], np.float32)
    norm_g = np.asarray(inp["norm_g"], np.float32)
    maps = []
    for ci in range(8):
        sl = slice(ci * 512, (ci + 1) * 512)
        aw = np.ascontiguousarray(ada_w.reshape(2, 4096, 6, 4096)[:, :, :, sl].transpose(0, 2, 1, 3))
        ab = np.ascontiguousarray(ada_b.reshape(2, 6, 4096)[:, :, sl])
        ng = np.ascontiguousarray(norm_g[:, :, sl])
        maps.append({"cT": cT, "aw": aw, "ab": ab, "ng": ng})
    nc = _get_nc('mod', build_mod)
    res = run_bass_kernel_spmd(nc, maps, core_ids=list(range(8)))
    modv = np.concatenate([r["modv"] for r in res.results], axis=-1)
    return modv


TG = 256
NT = TG // 128
KC = 8
NWB = 4


def build_l2(ntok=2048, stop=None):
    k = K()
    x = k.din("x", [ntok, D])
    y = k.din("y", [ntok, D], BF16)
    modr = k.din("modr", [6, D])
    modc = k.din("modc", [6, 128, 32])
    w_out = k.din("w_out", [D, D], BF16)
    w1 = k.din("w1", [D, DFF], BF16)
    w2 = k.din("w2", [DFF, D], BF16)
    idf_d = k.din("identf", [128, 128])
    xo = k.dout("xo", [ntok, D])

    idf, b_idf = k.sb([128, 128])
    idb, b_idb = k.sb([128, 128], BF16)
    mc, b_mc = k.sb([128, 6, 32])
    actT, b_actT = k.sb([128, 32, TG], BF16)
    hid, b_hid = k.sb([128, 128, TG], BF16)
    o1 = [k.sb([128, D]) for _ in range(NT)]
    tx, b_tx = k.sb([128, D])
    rowb, b_rowb = k.sb([128, D])
    yb, b_yb = k.sb([128, D], BF16)
    wbf = [k.sb([128, KC, 512], BF16) for _ in range(NWB)]
    rl, b_rl = k.sb([128, 4, TG])
    st, b_st = k.sb([128, 4])
    acc = [k.ps([128, 512]) for _ in range(NT)]
    accF, b_accF = k.ps([128, 4, 512])
    ptb, b_ptb = k.ps([128, 4, 128], BF16)
    ptf, b_ptf = k.ps([128, 4, 128])

    k.dma(idf[:], idf_d, [], [b_idf])
    k.dma(mc[:], modc.rearrange("v p c -> p v c"), [], [b_mc])
    k.cp(idb[:], idf[:], [b_idf], [b_idb])
    wi = [0]

    def load_w(src):
        i = wi[0] % NWB
        wi[0] += 1
        wb_, b_wb = wbf[i]
        k.dma(wb_[:], src, [], [b_wb])
        return wb_, b_wb

    def sumsq(src, b_src, col):
        k.act(yb[:], src, AF.Square, [b_src], [b_yb, b_st], accum_out=st[:, col:col + 1])

    def rstd_col(col):
        t = st[:, col:col + 1]
        k.ts(t, t, 1.0 / D, EPS, ALU.mult, ALU.add, [b_st], [b_st])
        k.act(t, t, AF.Sqrt, [b_st], [b_st])
        k.recip(t, t, [b_st], [b_st])

    for ps_ in range(ntok // TG):
        t0 = ps_ * TG
        b_xo = [Buf() for _ in range(NT)]
        for mt in range(NT):
            r0 = t0 + mt * 128
            k.dma(yb[:], y[r0:r0 + 128, :], [], [b_yb])
            for g in range(8):
                for c in range(4):
                    k.tr(ptb[:, c, :], yb[:, (g * 4 + c) * 128:(g * 4 + c + 1) * 128], idb[:], [b_yb, b_idb], [b_ptb])
                k.cp(actT[:, g * 4:(g + 1) * 4, mt * 128:(mt + 1) * 128], ptb[:], [b_ptb], [b_actT])
        for nb in range(8):
            for kt in range(32 // KC):
                src = w_out[kt * KC * 128:(kt + 1) * KC * 128, nb * 512:(nb + 1) * 512].rearrange("(c p) n -> p c n", p=128)
                wb_, b_wb = load_w(src)
                for mt in range(NT):
                    a, b_a = acc[mt]
                    for c in range(KC):
                        kk = kt * KC + c
                        k.mm(a[:], actT[:, kk, mt * 128:(mt + 1) * 128], wb_[:, c, :], kk == 0, kk == 31, [b_actT, b_wb], [b_a])
            for mt in range(NT):
                a, b_a = acc[mt]
                o, b_o = o1[mt]
                k.act(o[:, nb * 512:(nb + 1) * 512], a[:], AF.Copy, [b_a], [b_o])
        for mt in range(NT):
            r0 = t0 + mt * 128
            o, b_o = o1[mt]
            sumsq(o[:], b_o, 0)
            rstd_col(0)
            k.dma(tx[:], x[r0:r0 + 128, :], [], [b_tx])
            k.dma(rowb[:], modr[2, :].partition_broadcast(128), [], [b_rowb], q="actq")
            k.stt(o[:], o[:], st[:, 0:1], rowb[:], ALU.mult, ALU.mult, [b_o, b_st, b_rowb], [b_o])
            k.tt(tx[:], tx[:], o[:], ALU.add, [b_tx, b_o], [b_tx], eng="pool")
            k.dma(xo[r0:r0 + 128, :], tx[:], [b_tx], [b_xo[mt]])
            sumsq(tx[:], b_tx, 1)
            rstd_col(1)
            k.act(o[:], tx[:], AF.Identity, [b_tx, b_st], [b_o], scale=st[:, 1:2])
            for g in range(8):
                for c in range(4):
                    k.tr(ptf[:, c, :], o[:, (g * 4 + c) * 128:(g * 4 + c + 1) * 128], idf[:], [b_o, b_idf], [b_ptf])
                for c in range(4):
                    kk = g * 4 + c
                    k.act(actT[:, kk, mt * 128:(mt + 1) * 128], ptf[:, c, :], AF.Identity, [b_ptf, b_mc], [b_actT],
                          bias=mc[:, 4, kk:kk + 1], scale=mc[:, 3, kk:kk + 1])
        if stop == 'C':
            continue
        for g in range(DFF // 512):
            for kt in range(32 // KC):
                src = w1[kt * KC * 128:(kt + 1) * KC * 128, g * 512:(g + 1) * 512].rearrange("(c p) n -> p c n", p=128)
                wb_, b_wb = load_w(src)
                for fb in range(4):
                    for c in range(KC):
                        kk = kt * KC + c
                        k.mm(accF[:, fb, 0:TG], wb_[:, c, fb * 128:(fb + 1) * 128], actT[:, kk, :], kk == 0, kk == 31,
                             [b_actT, b_wb], [b_accF])
            k.act(rl[:], accF[:, :, 0:TG], AF.Relu, [b_accF], [b_rl])
            k.tt(hid[:, g * 4:(g + 1) * 4, :], rl[:], rl[:], ALU.mult, [b_rl], [b_hid])
        for db in range(8):
            for ft in range(128 // KC):
                src = w2[ft * KC * 128:(ft + 1) * KC * 128, db * 512:(db + 1) * 512].rearrange("(c p) n -> p c n", p=128)
                wb_, b_wb = load_w(src)
                for mt in range(NT):
                    a, b_a = acc[mt]
                    for c in range(KC):
                        kk = ft * KC + c
                        k.mm(a[:], hid[:, kk, mt * 128:(mt + 1) * 128], wb_[:, c, :], kk == 0, kk == 127, [b_hid, b_wb], [b_a])
            for mt in range(NT):
                a, b_a = acc[mt]
                o, b_o = o1[mt]
                k.act(o[:, db * 512:(db + 1) * 512], a[:], AF.Copy, [b_a], [b_o])
        for mt in range(NT):
            r0 = t0 + mt * 128
            o, b_o = o1[mt]
            sumsq(o[:], b_o, 2)
            rstd_col(2)
            k.dma(tx[:], xo[r0:r0 + 128, :], [b_xo[mt]], [b_tx])
            k.dma(rowb[:], modr[5, :].partition_broadcast(128), [], [b_rowb], q="actq")
            k.stt(o[:], o[:], st[:, 2:3], rowb[:], ALU.mult, ALU.mult, [b_o, b_st, b_rowb], [b_o])
            k.tt(tx[:], tx[:], o[:], ALU.add, [b_tx, b_o], [b_tx], eng="pool")
            k.dma(xo[r0:r0 + 128, :], tx[:], [b_tx], [b_xo[mt]])
    return k.finish()


_IDENT = np.eye(128, dtype=np.float32)


def run_l2(x, ybf, modv_l, w_out, w1, w2):
    nc = _get_nc('l2', build_l2)
    maps = []
    for ci in range(8):
        b, q = ci // 4, ci % 4
        sl = slice(q * 2048, (q + 1) * 2048)
        mr = np.ascontiguousarray(modv_l[b])
        mcm = np.ascontiguousarray(mr.reshape(6, 32, 128).transpose(0, 2, 1))
        maps.append({"x": np.ascontiguousarray(x[b, sl]), "y": np.ascontiguousarray(ybf[b, sl]),
                     "modr": mr, "modc": mcm, "w_out": w_out, "w1": w1, "w2": w2, "identf": _IDENT})
    res = run_bass_kernel_spmd(nc, maps, core_ids=list(range(8)))
    out = np.empty((2, 8192, 4096), np.float32)
    for ci in range(8):
        b, q = ci // 4, ci % 4
        out[b, q * 2048:(q + 1) * 2048] = res.results[ci]["xo"]
    return out


class HTMaker:
    def __init__(self, k, modc_d, idf_d, inplace=False):
        self.k = k
        self.idf, self.b_idf = k.sb([128, 128])
        self.idb, self.b_idb = k.sb([128, 128], BF16)
        self.mc, self.b_mc = k.sb([128, 6, 32])
        self.tx, self.b_tx = k.sb([128, D])
        if inplace:
            self.xn, self.b_xn = self.tx, self.b_tx
        else:
            self.xn, self.b_xn = k.sb([128, D])
        self.junk, self.b_junk = k.sb([128, D], BF16)
        self.st, self.b_st = k.sb([128, 2])
        self.ptf, self.b_ptf = k.ps([128, 4, 128])
        k.dma(self.idf[:], idf_d, [], [self.b_idf])
        k.dma(self.mc[:], modc_d.rearrange("v p c -> p v c"), [], [self.b_mc])
        k.cp(self.idb[:], self.idf[:], [self.b_idf], [self.b_idb])

    def emit(self, src, dst_fn, b_dst, ai=0, bi=1):
        k = self
        kk_ = self.k
        kk_.dma(self.tx[:], src, [], [self.b_tx])
        kk_.act(self.junk[:], self.tx[:], AF.Square, [self.b_tx], [self.b_junk, self.b_st],
                accum_out=self.st[:, 0:1])
        t = self.st[:, 0:1]
        kk_.ts(t, t, 1.0 / D, EPS, ALU.mult, ALU.add, [self.b_st], [self.b_st])
        kk_.act(t, t, AF.Sqrt, [self.b_st], [self.b_st])
        kk_.recip(t, t, [self.b_st], [self.b_st])
        kk_.act(self.xn[:], self.tx[:], AF.Identity, [self.b_tx, self.b_st], [self.b_xn], scale=self.st[:, 0:1])
        for g in range(8):
            for c in range(4):
                q = g * 4 + c
                kk_.tr(self.ptf[:, c, :], self.xn[:, q * 128:(q + 1) * 128], self.idf[:],
                       [self.b_xn, self.b_idf], [self.b_ptf])
            for c in range(4):
                q = g * 4 + c
                kk_.act(dst_fn(q), self.ptf[:, c, :], AF.Identity, [self.b_ptf, self.b_mc], [b_dst],
                        bias=self.mc[:, bi, q:q + 1], scale=self.mc[:, ai, q:q + 1])


class WLoader:
    def __init__(self, k, ncol=128, nbuf=2, nst=None):
        self.k = k
        self.ncol = ncol
        self.st = [k.sb([128, 32, ncol]) for _ in range(nst or nbuf)]
        self.bf = [k.sb([128, 32, ncol], BF16) for _ in range(nbuf)]
        self.i = 0

    def load(self, W, c0, n):
        k = self.k
        ws, b_ws = self.st[self.i % len(self.st)]
        wb, b_wb = self.bf[self.i % len(self.bf)]
        self.i += 1
        if W.dtype == BF16:
            k.dma(wb[:, :, 0:n], W[:, c0:c0 + n].rearrange("(c p) n -> p c n", p=128), [], [b_wb])
            return wb, b_wb
        k.dma(ws[:, :, 0:n], W[:, c0:c0 + n].rearrange("(c p) n -> p c n", p=128), [], [b_ws])
        k.cp(wb[:, :, 0:n], ws[:, :, 0:n], [b_ws], [b_wb], eng="pool")
        return wb, b_wb


TB = 512
CH = 64


def gemm_fm(k, wb, b_wb, ncols_off, hT, b_hT, out, b_out, M=128):
    for kk in range(32):
        k.mm(out, wb[:, kk, ncols_off:ncols_off + M], hT[:, kk, :], kk == 0, kk == 31, [b_wb, b_hT], [b_out])


def build_1c(layer):
    HC = 3
    k = K()
    x = k.din("x", [SEQ, D])
    modc = k.din("modc", [6, 128, 32])
    idf_d = k.din("identf", [128, 128])
    w = k.din("w", [D, 4 * HC * 128], BF16)
    lbraw = k.din("lbraw", [128, HC, 2])
    onorm = k.din("onorm", [HC * 128])
    cmask_d = k.din("cmask", [64, 64])
    smask_d = k.din("smask", [128, TB])
    yo = k.dout("yc", [SEQ, HC * 128], BF16)

    hm = HTMaker(k, modc, idf_d)
    wl = WLoader(k, 128, 2)
    hT, b_hT = k.sb([128, 32, TB], BF16)
    cmask, b_cm = k.sb([64, 64])
    smask, b_sm = k.sb([128, TB])
    lbt, b_lb = k.sb([128, HC, 2])
    lbw, b_lbw = k.sb([128, 6, HC])
    lb, b_lbv = k.sb([128, HC])
    oml, b_oml = k.sb([128, HC])
    onb, b_onb = k.sb([64, HC * 128])
    k.dma(cmask[:], cmask_d, [], [b_cm])
    k.dma(smask[:], smask_d, [], [b_sm])
    k.dma(lbt[:], lbraw, [], [b_lb])
    k.dma(onb[:], onorm.partition_broadcast(64), [], [b_onb])
    m_ = lbw[:, 0, :]
    k.tt(m_, lbt[:, :, 0], lbt[:, :, 1], ALU.max, [b_lb], [b_lbw])
    k.tt(lbw[:, 1, :], lbt[:, :, 0], m_, ALU.subtract, [b_lb, b_lbw], [b_lbw])
    k.tt(lbw[:, 2, :], lbt[:, :, 1], m_, ALU.subtract, [b_lb, b_lbw], [b_lbw])
    k.act(lbw[:, 1, :], lbw[:, 1, :], AF.Exp, [b_lbw], [b_lbw])
    k.act(lbw[:, 2, :], lbw[:, 2, :], AF.Exp, [b_lbw], [b_lbw])
    k.tt(lbw[:, 3, :], lbw[:, 1, :], lbw[:, 2, :], ALU.add, [b_lbw], [b_lbw])
    k.recip(lbw[:, 3, :], lbw[:, 3, :], [b_lbw], [b_lbw])
    k.tt(lbw[:, 4, :], lbw[:, 1, :], lbw[:, 3, :], ALU.mult, [b_lbw], [b_lbw])
    k.tt(lbw[:, 5, :], lbw[:, 2, :], lbw[:, 3, :], ALU.mult, [b_lbw], [b_lbw])
    if layer == 0:
        k.tt(lb[:], lbw[:, 4, :], lbw[:, 4, :], ALU.subtract, [b_lbw], [b_lbv])
    else:
        k.tt(lb[:], lbw[:, 4, :], lbw[:, 5, :], ALU.add, [b_lbw], [b_lbv])
        k.tt(lb[:], lb[:], lbw[:, 4, :], ALU.subtract, [b_lbw, b_lbv], [b_lbv])
    k.ts(oml[:], lb[:], -1.0, 1.0, ALU.mult, ALU.add, [b_lbv], [b_oml])

    pq = [k.ps([128, TB]) for _ in range(2)]
    pv, b_pv = k.ps([64, 4, 128])
    pAT, b_pAT = k.ps([64, HC, 64])
    po, b_po = k.ps([64, HC, 128])
    pS, b_pS = k.ps([128, HC, 128])
    pkt, b_pkt = k.ps([64, HC, 128], BF16)

    qs, b_qs = k.sb([128, TB])
    sg, b_sg = k.sb([128, TB])
    sgn, b_sgn = k.sb([128, TB])
    lf, b_lf = k.sb([128, TB])
    bb, b_bb = k.sb([128, TB])
    enb, b_enb = k.sb([128, TB])
    eb, b_eb = k.sb([128, HC, TB])
    qt, b_qt = k.sb([128, HC, TB], BF16)
    kt, b_kt = k.sb([128, HC, TB], BF16)
    V, b_V = k.sb([64, 8, HC * 128], BF16)
    gw, b_gw = k.sb([64, 8, HC * 128])
    gs, b_gs = k.sb([64, 4, 128])
    S, b_S = k.sb([128, HC, 128])
    Sb, b_Sb = k.sb([128, HC, 128], BF16)
    t1, b_t1 = k.sb([128, HC, 128])
    ATs, b_ATs = k.sb([64, HC, 64], BF16)
    kts, b_kts = k.sb([64, HC, 128], BF16)
    st, b_st = k.sb([64, HC])
    junk, b_junk = k.sb([64, 128])
    yt = [k.sb([64, HC * 128], BF16) for _ in range(2)]
    k.memset(S[:], 0.0, [b_S])
    k.memset(Sb[:], 0.0, [b_Sb])
    pqi = 0
    for tb in range(SEQ // TB):
        t0 = tb * TB
        for mt in range(TB // 128):
            hm.emit(x[t0 + mt * 128:t0 + (mt + 1) * 128, :],
                    lambda q, mt=mt: hT[:, q, mt * 128:(mt + 1) * 128], b_hT)
        for h in range(HC):
            wb, b_wb = wl.load(w, h * 128, 128)
            p_, b_p = pq[pqi % 2]; pqi += 1
            gemm_fm(k, wb, b_wb, 0, hT, b_hT, p_[:], b_p)
            k.act(qs[:], p_[:], AF.Silu, [b_p], [b_qs])
            wb, b_wb = wl.load(w, (HC + h) * 128, 128)
            p_, b_p = pq[pqi % 2]; pqi += 1
            gemm_fm(k, wb, b_wb, 0, hT, b_hT, p_[:], b_p)
            k.act(sg[:], p_[:], AF.Sigmoid, [b_p], [b_sg])
            k.act(sgn[:], p_[:], AF.Sigmoid, [b_p], [b_sgn], scale=-1.0)
            k.ts(sg[:], sg[:], oml[:, h:h + 1], lb[:, h:h + 1], ALU.mult, ALU.add, [b_sg, b_oml, b_lbv], [b_sg])
            k.act(lf[:], sg[:], AF.Ln, [b_sg], [b_lf])
            k.ts(sgn[:], sgn[:], oml[:, h:h + 1], None, ALU.mult, None, [b_sgn, b_oml], [b_sgn])
            k.P.op("dve", lambda e: e.tensor_tensor_scan(out=bb[:], data0=smask[:], data1=lf[:], initial=0.0,
                                                        op0=ALU.mult, op1=ALU.add),
                   reads=[b_sm, b_lf], writes=[b_bb])
            k.act(eb[:, h, :], bb[:], AF.Exp, [b_bb], [b_eb])
            k.act(enb[:], bb[:], AF.Exp, [b_bb], [b_enb], scale=-1.0)
            k.tt(qt[:, h, :], qs[:], eb[:, h, :], ALU.mult, [b_qs, b_eb], [b_qt])
            k.tt(kt[:, h, :], sgn[:], enb[:], ALU.mult, [b_sgn, b_enb], [b_kt])
        for j in range(2 * HC):
            wb, b_wb = wl.load(w, (2 * HC + j) * 128, 128)
            for c4 in range(2):
                for c in range(4):
                    cc = c4 * 4 + c
                    for kk in range(32):
                        k.mm(pv[:, c, :], hT[:, kk, cc * 64:(cc + 1) * 64], wb[:, kk, :], kk == 0, kk == 31,
                             [b_hT, b_wb], [b_pv])
                if j < HC:
                    k.act(V[:, c4 * 4:(c4 + 1) * 4, j * 128:(j + 1) * 128], pv[:], AF.Copy, [b_pv], [b_V])
                else:
                    hh = j - HC
                    k.act(gs[:], pv[:], AF.Silu, [b_pv], [b_gs])
                    for c in range(4):
                        k.tt(gw[:, c4 * 4 + c, hh * 128:(hh + 1) * 128], gs[:, c, :], onb[:, hh * 128:(hh + 1) * 128],
                             ALU.mult, [b_gs, b_onb], [b_gw])
        for c in range(8):
            cs = slice(c * 64, (c + 1) * 64)
            y_, b_y = yt[c % 2]
            for h in range(HC):
                hs = slice(h * 128, (h + 1) * 128)
                k.mm(pAT[:, h, :], kt[:, h, cs], qt[:, h, cs], True, True, [b_kt, b_qt], [b_pAT])
                k.tt(ATs[:, h, :], pAT[:, h, :], cmask[:], ALU.mult, [b_pAT, b_cm], [b_ATs])
                k.mm(po[:, h, :], ATs[:, h, :], V[:, c, hs], True, False, [b_ATs, b_V], [b_po])
                k.mm(po[:, h, :], qt[:, h, cs], Sb[:, h, :], False, True, [b_qt, b_Sb], [b_po])
                k.tr(pkt[:, h, :], kt[:, h, cs], hm.idb[:], [b_kt, hm.b_idb], [b_pkt])
                k.act(kts[:, h, :], pkt[:, h, :], AF.Copy, [b_pkt], [b_kts])
                k.mm(pS[:, h, :], kts[:, h, :], V[:, c, hs], True, True, [b_kts, b_V], [b_pS])
                ec = eb[:, h, c * 64 + 63:c * 64 + 64]
                k.tt(t1[:, h, :], pS[:, h, :], S[:, h, :], ALU.add, [b_pS, b_S], [b_t1])
                k.ts(S[:, h, :], t1[:, h, :], ec, None, ALU.mult, None, [b_t1, b_eb], [b_S])
                k.act(Sb[:, h, :], t1[:, h, :], AF.Identity, [b_t1, b_eb], [b_Sb], scale=ec)
                k.act(junk[:], po[:, h, :], AF.Square, [b_po], [b_junk, b_st], accum_out=st[:, h:h + 1])
            k.ts(st[:], st[:], 1.0 / 128, EPS, ALU.mult, ALU.add, [b_st], [b_st])
            k.act(st[:], st[:], AF.Sqrt, [b_st], [b_st])
            k.recip(st[:], st[:], [b_st], [b_st])
            for h in range(HC):
                hs = slice(h * 128, (h + 1) * 128)
                k.stt(y_[:, hs], po[:, h, :], st[:, h:h + 1], gw[:, c, hs], ALU.mult, ALU.mult,
                      [b_po, b_st, b_gw], [b_y])
            k.dma(yo[t0 + c * 64:t0 + (c + 1) * 64, :], y_[:], [b_y], [], q="actq")
    return k.finish()


_CMASK = np.triu(np.ones((64, 64), np.float32))
_SMASK = np.ones((128, TB), np.float32)
_SMASK[:, ::CH] = 0.0


def run_1c(layer, x, modv_l, w_in_l, hgrn_lb, hgrn_onorm_l):
    A_COLS, B_COLS = 1104, 5344
    c0 = A_COLS + B_COLS
    if w_in_l.dtype == np.float32:
        w_in_l = run_wc(np.asarray(w_in_l))
    nc = _get_nc(('1c', layer), lambda: build_1c(layer))
    maps = []
    for ci in range(8):
        b, g = ci // 4, ci % 4
        hs = slice(g * 384, (g + 1) * 384)
        wc = w_in_l[:, c0:]
        wsl = np.ascontiguousarray(np.concatenate([wc[:, j * 1536:(j + 1) * 1536][:, hs] for j in range(4)], axis=1))
        mr = np.ascontiguousarray(modv_l[b])
        mcm = np.ascontiguousarray(mr.reshape(6, 32, 128).transpose(0, 2, 1))
        lbr = np.ascontiguousarray(hgrn_lb[:, hs].reshape(2, 3, 128).transpose(2, 1, 0))
        maps.append({"x": np.ascontiguousarray(x[b]), "modc": mcm, "identf": _IDENT, "w": wsl,
                     "lbraw": lbr, "onorm": np.ascontiguousarray(hgrn_onorm_l[hs]),
                     "cmask": _CMASK, "smask": _SMASK})
    res = run_bass_kernel_spmd(nc, maps, core_ids=list(range(8)))
    out = np.empty((2, 8192, 1536), ml_dtypes.bfloat16)
    for ci in range(8):
        b, g = ci // 4, ci % 4
        out[b, :, g * 384:(g + 1) * 384] = res.results[ci]["yc"]
    return out


TBB = 256
GN_EPS = 64e-5


def build_1b(seq=SEQ):
    HB = 6
    NCH = TBB // CH
    k = K()
    x = k.din("x", [seq, D])
    modc = k.din("modc", [6, 128, 32])
    idf_d = k.din("identf", [128, 128])
    wrkv = k.din("wrkv", [D, 3 * HB * 64], BF16)
    wlo = k.din("wlo", [D, 736], BF16)
    mu_rkv_d = k.din("mu_rkv", [64, 3 * HB])
    mu_wa_d = k.din("mu_wa", [128, 2])
    mu_g_d = k.din("mu_g", [120, 4])
    hp_d = k.din("hp", [64, 5, HB])
    wup_d = k.din("wup", [128, HB * 64])
    aup_d = k.din("aup", [128, HB * 64])
    gup_d = k.din("gup", [120, 4, HB * 64])
    lnw_d = k.din("lnw", [HB * 64])
    lnb_d = k.din("lnb", [HB * 64])
    smask_d = k.din("smask", [64, HB, TBB])
    m5_d = k.din("m5", [64, HB, 5, 64])
    yo = k.dout("yb", [seq, HB * 64], BF16)

    hm = HTMaker(k, modc, idf_d, inplace=True)
    wl = WLoader(k, 128, 2, nst=1)
    hT, b_hT = k.sb([128, 32, TBB], BF16)
    smask, b_sm = k.sb([64, HB, TBB])
    m5, b_m5 = k.sb([64, HB, 5, 64])
    mu_rkv, b_mur = k.sb([64, 3 * HB])
    mu_wa, b_muw = k.sb([128, 2])
    mu_g, b_mug = k.sb([120, 4])
    hp, b_hp = k.sb([64, 5, HB])
    wup, b_wup = k.sb([128, HB * 64], BF16)
    aup, b_aup = k.sb([128, HB * 64], BF16)
    gup, b_gup = k.sb([120, 4, HB * 64], BF16)
    lnw, b_lnw = k.sb([64, HB * 64])
    lnb, b_lnb = k.sb([64, HB * 64])
    ones, b_ones = k.sb([64, 64])
    stg, b_stg = wl.st[0]
    stgf = stg[:].rearrange("p a b -> p (a b)")
    for dst, src, bd in ((smask, smask_d, b_sm), (m5, m5_d, b_m5), (mu_rkv, mu_rkv_d, b_mur), (mu_wa, mu_wa_d, b_muw),
                         (mu_g, mu_g_d, b_mug), (hp, hp_d, b_hp)):
        k.dma(dst[:], src, [], [bd], q="actq")
    k.dma(lnw[:], lnw_d.partition_broadcast(64), [], [b_lnw], q="actq")
    k.dma(lnb[:], lnb_d.partition_broadcast(64), [], [b_lnb], q="actq")
    k.dma(stgf[:, 0:384], wup_d, [], [b_stg])
    k.cp(wup[:], stgf[:, 0:384], [b_stg], [b_wup])
    k.dma(stgf[:, 0:384], aup_d, [], [b_stg])
    k.cp(aup[:], stgf[:, 0:384], [b_stg], [b_aup])
    k.dma(stgf[0:120, 0:1536], gup_d.rearrange("p a b -> p (a b)"), [], [b_stg])
    k.cp(gup[:].rearrange("p a b -> p (a b)"), stgf[0:120, 0:1536], [b_stg], [b_gup])
    k.memset(ones[:], 1.0, [b_ones])

    pq, b_pq = k.ps([128, 512])
    PA, b_PA = k.ps([64, 32, 64])
    PN, b_PN = k.ps([64, 16, 64])
    PAf = PA[:].rearrange("p a b -> p (a b)")
    PNf = PN[:].rearrange("p a b -> p (a b)")

    R_, b_R = k.sb([128, TBB + 1])
    carry, b_carry = k.sb([128, 3 * HB + 6])
    rT, b_rT = k.sb([64, HB, TBB])
    kT, b_kT = k.sb([64, HB, TBB])
    vT, b_vT = k.sb([64, HB, TBB])
    At, b_At = k.sb([64, HB, TBB])
    Bt, b_Bt = k.sb([64, HB, TBB])
    ebC, b_ebC = k.sb([64, HB, NCH])
    lsh, b_lsh = k.sb([128, TBB])
    tw, b_tw = k.sb([128, TBB], BF16)
    ta, b_ta = k.sb([128, TBB], BF16)
    tg, b_tg = k.sb([120, 4, TBB], BF16)
    tmp = [k.sb([64, HB, TBB]) for _ in range(6)]
    Vt, b_Vt = k.sb([64, NCH, HB, 64])
    gt, b_gt = k.sb([64, NCH, HB * 64])
    rk, b_rk = k.sb([64, HB, NCH])
    H, b_H = k.sb([64, HB, 64])
    mats, b_mats = k.sb([64, HB, 5, 64])
    Nsb, b_N = k.sb([64, HB, 2, 64])
    X, b_X = k.sb([64, HB, 64])
    W, b_W = k.sb([64, HB, 64])
    U, b_U = k.sb([64, HB, 64])
    BKt, b_BKt = k.sb([64, HB, 2, 64])
    ysb, b_ysb = k.sb([64, HB, 64])
    t2, b_t2 = k.sb([64, HB, 64])
    stt_, b_stt = k.sb([64, HB, 8])
    mv, b_mv = k.sb([64, HB, 2])
    yt = [k.sb([64, HB * 64], BF16) for _ in range(2)]
    k.memset(carry[:], 0.0, [b_carry])
    k.memset(H[:], 0.0, [b_H])
    idf, b_idf = hm.idf, hm.b_idf
    i64 = idf[0:64, 0:64]
    FL = "p a b -> p (a b)"

    def bc(ap3):
        return ap3.to_broadcast([64, HB, TBB])

    def shifted(M, col, mu_ap, b_mu, dst, b_dst):
        k.act(R_[0:M, 1:TBB + 1], pq[0:M, 0:TBB], AF.Copy, [b_pq], [b_R])
        k.cp(R_[0:M, 0:1], carry[0:M, col:col + 1], [b_carry], [b_R])
        k.cp(carry[0:M, col:col + 1], R_[0:M, TBB:TBB + 1], [b_R], [b_carry])
        k.tt(lsh[0:M, :], R_[0:M, 0:TBB], R_[0:M, 1:TBB + 1], ALU.subtract, [b_R], [b_lsh])
        k.stt(dst, lsh[0:M, :], mu_ap, R_[0:M, 1:TBB + 1], ALU.mult, ALU.add, [b_lsh, b_mu, b_R], [b_dst])

    def gemm(wb, b_wb, off, M):
        for kk in range(32):
            k.mm(pq[0:M, 0:TBB], wb[:, kk, off:off + M], hT[:, kk, :], kk == 0, kk == 31, [b_wb, b_hT], [b_pq])

    (sgz, b_sgz), (av, b_av), (kkv, b_kkv), (bv, b_bv), (e1, b_e1), (tq, b_tq) = tmp
    yi = 0
    for tb in range(seq // TBB):
        t0 = tb * TBB
        for mt in range(TBB // 128):
            hm.emit(x[t0 + mt * 128:t0 + (mt + 1) * 128, :],
                    lambda q, mt=mt: hT[:, q, mt * 128:(mt + 1) * 128], b_hT)
        for j in range(3 * HB // 2):
            wb, b_wb = wl.load(wrkv, j * 128, 128)
            for s_ in range(2):
                col = j * 2 + s_
                which, h = col // HB, col % HB
                dstt, b_d = ((rT, b_rT), (kT, b_kT), (vT, b_vT))[which]
                gemm(wb, b_wb, s_ * 64, 64)
                shifted(64, col, mu_rkv[:, col:col + 1], b_mur, dstt[:, h, :], b_d)
        wb, b_wb = wl.load(wlo, 0, 128)
        gemm(wb, b_wb, 0, 128)
        shifted(128, 3 * HB + 0, mu_wa[:, 0:1], b_muw, lsh[:, :], b_lsh)
        k.act(tw[:], lsh[:], AF.Tanh, [b_lsh], [b_tw])
        wb, b_wb = wl.load(wlo, 128, 128)
        gemm(wb, b_wb, 0, 128)
        shifted(128, 3 * HB + 1, mu_wa[:, 1:2], b_muw, lsh[:, :], b_lsh)
        k.act(ta[:], lsh[:], AF.Copy, [b_lsh], [b_ta])
        for j in range(4):
            wb, b_wb = wl.load(wlo, 256 + j * 120, 120)
            gemm(wb, b_wb, 0, 120)
            shifted(120, 3 * HB + 2 + j, mu_g[:, j:j + 1], b_mug, lsh[0:120, :], b_lsh)
            k.act(tg[:, j, :], lsh[0:120, :], AF.Sigmoid, [b_lsh], [b_tg])
        for h in range(HB):
            hs = slice(h * 64, (h + 1) * 64)
            k.mm(pq[0:64, 0:TBB], wup[:, hs], tw[:], True, True, [b_wup, b_tw], [b_pq])
            k.act(sgz[:, h, :], pq[0:64, 0:TBB], AF.Sigmoid, [b_pq, b_hp], [b_sgz], bias=hp[:, 0, h:h + 1])
        for h in range(HB):
            hs = slice(h * 64, (h + 1) * 64)
            k.mm(pq[0:64, 0:TBB], aup[:, hs], ta[:], True, True, [b_aup, b_ta], [b_pq])
            k.act(av[:, h, :], pq[0:64, 0:TBB], AF.Sigmoid, [b_pq, b_hp], [b_av], bias=hp[:, 1, h:h + 1])
        k.ts(sgz[:], sgz[:], -float(np.exp(-0.5)), None, ALU.mult, None, [b_sgz], [b_sgz])
        k.P.op("dve", lambda e: e.tensor_tensor_scan(out=bv[:].rearrange(FL), data0=smask[:].rearrange(FL),
                                                    data1=sgz[:].rearrange(FL), initial=0.0,
                                                    op0=ALU.mult, op1=ALU.add),
               reads=[b_sm, b_sgz], writes=[b_bv])
        k.act(e1[:], bv[:], AF.Exp, [b_bv], [b_e1])
        k.cp(ebC[:], e1[:].rearrange("p h (c s) -> p h c s", s=CH)[:, :, :, CH - 1], [b_e1], [b_ebC])
        k.tt(rT[:], rT[:], e1[:], ALU.mult, [b_rT, b_e1], [b_rT])
        k.tt(sgz[:], bv[:], sgz[:], ALU.subtract, [b_bv, b_sgz], [b_sgz])
        k.act(sgz[:], sgz[:], AF.Exp, [b_sgz], [b_sgz])
        k.act(bv[:], bv[:], AF.Exp, [b_bv], [b_bv], scale=-1.0)
        k.tt(kkv[:], kT[:], bc(hp[:, 2, :, None]), ALU.mult, [b_kT, b_hp], [b_kkv])
        k.tt(tq[:], kkv[:], kkv[:], ALU.mult, [b_kkv], [b_tq])
        for h2 in range(HB // 2):
            k.mm(pq[0:64, :], ones[:], tq[:, 2 * h2:2 * h2 + 2, :].rearrange(FL), True, True, [b_ones, b_tq], [b_pq])
            k.ts(e1[:, 2 * h2:2 * h2 + 2, :].rearrange(FL), pq[0:64, :], 1e-24, None, ALU.max, None, [b_pq], [b_e1])
        k.act(e1[:], e1[:], AF.Sqrt, [b_e1], [b_e1])
        k.recip(e1[:], e1[:], [b_e1], [b_e1])
        k.tt(kkv[:], kkv[:], e1[:], ALU.mult, [b_kkv, b_e1], [b_kkv])
        k.stt(At[:], kkv[:], -1.0, sgz[:], ALU.mult, ALU.mult, [b_kkv, b_sgz], [b_At])
        k.tt(tq[:], kkv[:], av[:], ALU.mult, [b_kkv, b_av], [b_tq])
        k.tt(Bt[:], tq[:], bv[:], ALU.mult, [b_tq, b_bv], [b_Bt])
        k.stt(tq[:], av[:], -1.0, bc(hp[:, 3, :, None]), ALU.add, ALU.mult, [b_av, b_hp], [b_tq])
        k.stt(kT[:], tq[:], 1.0, kT[:], ALU.add, ALU.mult, [b_tq, b_kT], [b_kT])
        k.tt(kT[:], kT[:], bv[:], ALU.mult, [b_kT, b_bv], [b_kT])
        k.tt(tq[:], rT[:], bc(hp[:, 4, :, None]), ALU.mult, [b_rT, b_hp], [b_tq])
        k.tt(tq[:], tq[:], kT[:], ALU.mult, [b_tq, b_kT], [b_tq])
        for h in range(HB):
            for c in range(NCH):
                idx = h * NCH + c
                k.mm(PNf[:, idx:idx + 1], tq[:, h, c * CH:(c + 1) * CH], ones[:, 0:1], True, True,
                     [b_tq, b_ones], [b_PN])
        k.cp(rk[:].rearrange(FL), PNf[:, 0:HB * NCH], [b_PN], [b_rk])
        for r3 in range(2):
            for hl in range(3):
                for c in range(NCH):
                    k.tr(PN[:, hl * NCH + c, :], vT[:, r3 * 3 + hl, c * CH:(c + 1) * CH], i64, [b_vT, b_idf], [b_PN])
            k.cp(Vt[:, :, r3 * 3:(r3 + 1) * 3, :].rearrange("p c h v -> p h c v"),
                 PN[:, 0:3 * NCH, :].rearrange("p (h c) v -> p h c v", h=3), [b_PN], [b_Vt])
        for c in range(NCH):
            for j in range(4):
                k.mm(pq[0:64, 0:HB * 64], tg[:, j, c * CH:(c + 1) * CH], gup[:, j, :], j == 0, j == 3,
                     [b_tg, b_gup], [b_pq])
            k.act(gt[:, c, :], pq[0:64, 0:HB * 64], AF.Copy, [b_pq], [b_gt])
        for c in range(NCH):
            cs = slice(c * CH, (c + 1) * CH)
            for h in range(HB):
                k.mm(PA[:, h * 5 + 0, :], Bt[:, h, cs], At[:, h, cs], True, True, [b_Bt, b_At], [b_PA])
                k.mm(PA[:, h * 5 + 1, :], kT[:, h, cs], At[:, h, cs], True, True, [b_kT, b_At], [b_PA])
                k.mm(PA[:, h * 5 + 2, :], Bt[:, h, cs], rT[:, h, cs], True, True, [b_Bt, b_rT], [b_PA])
                k.mm(PA[:, h * 5 + 3, :], kT[:, h, cs], rT[:, h, cs], True, True, [b_kT, b_rT], [b_PA])
                k.mm(PA[:, h * 5 + 4, :], At[:, h, cs], Bt[:, h, cs], True, True, [b_Bt, b_At], [b_PA])
            k.tt(mats[:].rearrange("p a b c -> p (a b) c"), PA[:, 0:HB * 5, :], m5[:].rearrange("p a b c -> p (a b) c"),
                 ALU.mult, [b_PA, b_m5], [b_mats])
            k.cp(Nsb[:, :, 0, :], mats[:, :, 4, :], [b_mats], [b_N])
            k.cp(Nsb[:, :, 1, :], mats[:, :, 0, :], [b_mats], [b_N])
            k.tt(X[:], mats[:, :, 0, :], i64[:, None, :].to_broadcast([64, HB, 64]), ALU.add, [b_mats, b_idf], [b_X])
            for step in range(5):
                for h in range(HB):
                    k.mm(PN[:, h * 2, :], Nsb[:, h, 1, :], Nsb[:, h, 0, :], True, True, [b_N], [b_PN])
                    if step < 4:
                        k.mm(PN[:, h * 2 + 1, :], Nsb[:, h, 0, :], Nsb[:, h, 1, :], True, True, [b_N], [b_PN])
                k.cp(Nsb[:].rearrange("p a b c -> p (a b) c"), PN[:, 0:2 * HB, :], [b_PN], [b_N])
                for h in range(HB):
                    k.mm(PA[:, h, :], Nsb[:, h, 0, :], X[:, h, :], True, True, [b_N, b_X], [b_PA])
                k.tt(X[:], X[:], PA[:, 0:HB, :], ALU.add, [b_X, b_PA], [b_X])
            for h in range(HB):
                k.mm(PA[:, 8 + h, :], At[:, h, cs], H[:, h, :], True, False, [b_At, b_H], [b_PA])
                k.mm(PA[:, 8 + h, :], mats[:, h, 1, :], Vt[:, c, h, :], False, True, [b_mats, b_Vt], [b_PA])
            k.cp(W[:], PA[:, 8:8 + HB, :], [b_PA], [b_W])
            for h in range(HB):
                k.mm(PA[:, 8 + h, :], X[:, h, :], W[:, h, :], True, True, [b_X, b_W], [b_PA])
            k.cp(U[:], PA[:, 8:8 + HB, :], [b_PA], [b_U])
            for h in range(HB):
                k.mm(PA[:, 16 + h, :], rT[:, h, cs], H[:, h, :], True, False, [b_rT, b_H], [b_PA])
                k.mm(PA[:, 16 + h, :], mats[:, h, 2, :], U[:, h, :], False, False, [b_mats, b_U], [b_PA])
                k.mm(PA[:, 16 + h, :], mats[:, h, 3, :], Vt[:, c, h, :], False, True, [b_mats, b_Vt], [b_PA])
            k.cp(ysb[:], PA[:, 16:16 + HB, :], [b_PA], [b_ysb])
            for h in range(HB):
                k.tr(PN[:, h * 2, :], Bt[:, h, cs], i64, [b_Bt, b_idf], [b_PN])
                k.tr(PN[:, h * 2 + 1, :], kT[:, h, cs], i64, [b_kT, b_idf], [b_PN])
            k.cp(BKt[:].rearrange("p a b c -> p (a b) c"), PN[:, 0:2 * HB, :], [b_PN], [b_BKt])
            for h in range(HB):
                k.mm(PA[:, 24 + h, :], BKt[:, h, 0, :], U[:, h, :], True, False, [b_BKt, b_U], [b_PA])
                k.mm(PA[:, 24 + h, :], BKt[:, h, 1, :], Vt[:, c, h, :], False, True, [b_BKt, b_Vt], [b_PA])
            k.tt(H[:], H[:], PA[:, 24:24 + HB, :], ALU.add, [b_H, b_PA], [b_H])
            k.tt(H[:], H[:], ebC[:, :, c:c + 1].to_broadcast([64, HB, 64]), ALU.mult, [b_H, b_ebC], [b_H])
            for h in range(HB):
                k.P.op("dve", lambda e, h=h: e.bn_stats(out=stt_[:, h, 0:6], in_=ysb[:, h, :]),
                       reads=[b_ysb], writes=[b_stt])
                k.P.op("dve", lambda e, h=h: e.bn_aggr(out=mv[:, h, :], in_=stt_[:, h, 0:6]),
                       reads=[b_stt], writes=[b_mv])
            k.ts(mv[:, :, 1], mv[:, :, 1], 1.0, GN_EPS, ALU.mult, ALU.add, [b_mv], [b_mv])
            k.act(mv[:, :, 1], mv[:, :, 1], AF.Sqrt, [b_mv], [b_mv])
            k.recip(mv[:, :, 1], mv[:, :, 1], [b_mv], [b_mv])
            k.tt(t2[:], ysb[:], mv[:, :, 0:1].to_broadcast([64, HB, 64]), ALU.subtract, [b_ysb, b_mv], [b_t2])
            k.tt(t2[:], t2[:], mv[:, :, 1:2].to_broadcast([64, HB, 64]), ALU.mult, [b_t2, b_mv], [b_t2])
            t2f = t2[:].rearrange(FL)
            k.tt(t2f, t2f, lnw[:], ALU.mult, [b_t2, b_lnw], [b_t2])
            k.tt(t2f, t2f, lnb[:], ALU.add, [b_t2, b_lnb], [b_t2])
            k.tt(W[:], Vt[:, c, :, :], rk[:, :, c:c + 1].to_broadcast([64, HB, 64]), ALU.mult, [b_Vt, b_rk], [b_W])
            k.tt(t2[:], t2[:], W[:], ALU.add, [b_t2, b_W], [b_t2])
            y_, b_y = yt[yi % 2]
            yi += 1
            k.tt(y_[:], t2f, gt[:, c, :], ALU.mult, [b_t2, b_gt], [b_y])
            k.dma(yo[t0 + c * CH:t0 + (c + 1) * CH, :], y_[:], [b_y], [], q="actq")
    return k.finish()


_SMASKB = np.ones((64, 6, TBB), np.float32)
_SMASKB[:, :, ::CH] = 0.0
_M5 = np.zeros((64, 6, 5, 64), np.float32)
_su = np.triu(np.ones((64, 64), np.float32), 1)
_iu = np.triu(np.ones((64, 64), np.float32), 0)
_M5[:, :, 0] = _su[:, None]
_M5[:, :, 1] = _su[:, None]
_M5[:, :, 2] = _iu[:, None]
_M5[:, :, 3] = _iu[:, None]
_M5[:, :, 4] = _su.T[:, None]


def run_1b(inp, layer, x, modv_l, seq=SEQ, runner=None, cores=range(8), w_in_l=None):
    A_COLS = 1104
    W_B = 1536
    if w_in_l is None:
        w_in_l = run_wc(np.asarray(inp["w_in"][layer]))
    mu = np.asarray(inp["rwkv_mu"][layer], np.float32)
    nc = _get_nc(('1b', seq), lambda: build_1b(seq))
    maps = []
    fm = lambda v: np.ascontiguousarray(np.asarray(v, np.float32).reshape(6, 64).T)
    for ci in cores:
        b, g = ci // 4, ci % 4
        hs = slice(g * 384, (g + 1) * 384)
        wB = w_in_l[:, A_COLS:A_COLS + 5344]
        wrkv = np.ascontiguousarray(np.concatenate([wB[:, j * W_B:(j + 1) * W_B][:, hs] for j in range(3)], axis=1))
        wlo = np.ascontiguousarray(wB[:, 3 * W_B:])
        mu_rkv = np.concatenate([mu[j * W_B:(j + 1) * W_B][hs] for j in range(3)]).reshape(18, 64).T
        mu_l = mu[3 * W_B:]
        mu_wa = np.stack([mu_l[0:128], mu_l[128:256]], axis=1)
        mu_g = mu_l[256:].reshape(4, 120).T
        hp = np.stack([fm(inp["rwkv_w0"][layer][hs]), fm(inp["rwkv_a0"][layer][hs]), fm(inp["rwkv_k_k"][layer][hs]),
                       fm(inp["rwkv_k_a"][layer][hs]), fm(inp["rwkv_r_k"][layer].reshape(-1)[hs])], axis=1)
        mr = np.ascontiguousarray(modv_l[b])
        mcm = np.ascontiguousarray(mr.reshape(6, 32, 128).transpose(0, 2, 1))
        maps.append({"x": np.ascontiguousarray(x[b, :seq]), "modc": mcm, "identf": _IDENT, "wrkv": wrkv, "wlo": wlo,
                     "mu_rkv": np.ascontiguousarray(mu_rkv), "mu_wa": np.ascontiguousarray(mu_wa),
                     "mu_g": np.ascontiguousarray(mu_g), "hp": np.ascontiguousarray(hp),
                     "wup": np.ascontiguousarray(inp["rwkv_w_up"][layer][:, hs]),
                     "aup": np.ascontiguousarray(inp["rwkv_a_up"][layer][:, hs]),
                     "gup": np.ascontiguousarray(inp["rwkv_g_up"][layer][:, hs].reshape(4, 120, 384).transpose(1, 0, 2)),
                     "lnw": np.ascontiguousarray(inp["rwkv_lnx_w"][layer][hs]),
                     "lnb": np.ascontiguousarray(inp["rwkv_lnx_b"][layer][hs]),
                     "smask": _SMASKB, "m5": _M5})
    if runner is not None:
        return runner(nc, maps)
    res = run_bass_kernel_spmd(nc, maps, core_ids=list(range(8)))
    out = np.empty((2, 8192, 1536), ml_dtypes.bfloat16)
    for ci in range(8):
        b, g = ci // 4, ci % 4
        out[b, :, g * 384:(g + 1) * 384] = res.results[ci]["yb"]
    return out


def build_1ak(ntok=2048):
    k = K()
    x = k.din("x", [ntok, D])
    modc = k.din("modc", [6, 128, 32])
    idf_d = k.din("identf", [128, 128])
    wkv = k.din("wkv", [D, 320], BF16)
    kvg = k.din("kvg", [256])
    o_ct = k.dout("ckvnT", [128, 2, ntok], BF16)
    o_v = k.dout("V", [ntok, 256], BF16)
    o_ki = k.dout("kidxT", [64, ntok], BF16)
    hm = HTMaker(k, modc, idf_d)
    wl = WLoader(k, 128, 2)
    hT, b_hT = k.sb([128, 32, TB], BF16)
    wres, b_wres = k.sb([128, 32, 320], BF16)
    gb, b_gb = k.sb([128, 256])
    k.dma(gb[:], kvg.partition_broadcast(128), [], [b_gb], q="actq")
    for j, (c0, n) in enumerate(((0, 128), (128, 128), (256, 64))):
        wb, b_wb = wl.load(wkv, c0, n)
        k.cp(wres[:, :, c0:c0 + n], wb[:, :, 0:n], [b_wb], [b_wres], eng="pool")
    pkv = [k.ps([128, 512]) for _ in range(2)]
    pki, b_pki = k.ps([64, 512])
    ptb, b_ptb = k.ps([128, 2, 128], BF16)
    st, b_st = k.sb([128, 1])
    junk, b_junk = k.sb([128, 256])
    vt = [k.sb([128, 256], BF16) for _ in range(2)]
    ct = [k.sb([128, 2, 128], BF16) for _ in range(2)]
    kit = [k.sb([64, TB], BF16) for _ in range(2)]
    n_ = 0
    for tb in range(ntok // TB):
        t0 = tb * TB
        for mt in range(TB // 128):
            hm.emit(x[t0 + mt * 128:t0 + (mt + 1) * 128, :],
                    lambda q, mt=mt: hT[:, q, mt * 128:(mt + 1) * 128], b_hT)
        for mt in range(TB // 128):
            p_, b_p = pkv[n_ % 2]
            v_, b_v = vt[n_ % 2]
            c_, b_c = ct[n_ % 2]
            n_ += 1
            for kk in range(32):
                k.mm(p_[:, 0:256], hT[:, kk, mt * 128:(mt + 1) * 128], wres[:, kk, 0:256], kk == 0, kk == 31,
                     [b_hT, b_wres], [b_p])
            k.act(junk[:], p_[:, 0:256], AF.Square, [b_p], [b_junk, b_st], accum_out=st[:, 0:1])
            k.ts(st[:], st[:], 1.0 / 256, EPS, ALU.mult, ALU.add, [b_st], [b_st])
            k.act(st[:], st[:], AF.Sqrt, [b_st], [b_st])
            k.recip(st[:], st[:], [b_st], [b_st])
            k.stt(v_[:], p_[:, 0:256], st[:, 0:1], gb[:], ALU.mult, ALU.mult, [b_p, b_st, b_gb], [b_v])
            r0 = t0 + mt * 128
            k.dma(o_v[r0:r0 + 128, :], v_[:], [b_v], [], q="actq")
            for rc in range(2):
                k.tr(ptb[:, rc, :], v_[:, rc * 128:(rc + 1) * 128], hm.idb[:], [b_v, hm.b_idb], [b_ptb])
            k.cp(c_[:], ptb[:], [b_ptb], [b_c])
            k.dma(o_ct[:, :, r0:r0 + 128], c_[:], [b_c], [], q="actq")
        ki_, b_ki = kit[tb % 2]
        for kk in range(32):
            k.mm(pki[:], wres[:, kk, 256:320], hT[:, kk, :], kk == 0, kk == 31, [b_hT, b_wres], [b_pki])
        k.act(ki_[:], pki[:], AF.Copy, [b_pki], [b_ki])
        k.dma(o_ki[:, t0:t0 + TB], ki_[:], [b_ki], [], q="actq")
    return k.finish()


def run_1ak(layer, x, modv_l, w_in_l, kvg):
    if w_in_l.dtype == np.float32:
        w_in_l = run_wc(np.asarray(w_in_l))
    nc = _get_nc('1ak', build_1ak)
    maps = []
    wkv = np.ascontiguousarray(w_in_l[:, 768:768 + 320])
    for ci in range(8):
        b, q = ci // 4, ci % 4
        mr = np.ascontiguousarray(modv_l[b])
        mcm = np.ascontiguousarray(mr.reshape(6, 32, 128).transpose(0, 2, 1))
        maps.append({"x": np.ascontiguousarray(x[b, q * 2048:(q + 1) * 2048]), "modc": mcm, "identf": _IDENT,
                     "wkv": wkv, "kvg": np.ascontiguousarray(kvg)})
    res = run_bass_kernel_spmd(nc, maps, core_ids=list(range(8)))
    ckvnT = np.empty((2, 128, 2, 8192), ml_dtypes.bfloat16)
    V = np.empty((2, 8192, 256), ml_dtypes.bfloat16)
    kidxT = np.empty((2, 64, 8192), ml_dtypes.bfloat16)
    for ci in range(8):
        b, q = ci // 4, ci % 4
        sl = slice(q * 2048, (q + 1) * 2048)
        ckvnT[b, :, :, sl] = res.results[ci]["ckvnT"]
        V[b, sl] = res.results[ci]["V"]
        kidxT[b, :, sl] = res.results[ci]["kidxT"]
    return ckvnT, V, kidxT


NEG_SEL = -3.0e38
NEG_CAUSAL = -1.0e30
NEG_LOGIT = -30000.0


def build_1aq(nq=16, seq=SEQ):
    k = K()
    xq = k.din("xq", [nq * 128, D])
    modc = k.din("modc", [6, 128, 32])
    idf_d = k.din("identf", [128, 128])
    wq = k.din("wq", [D, 784], BF16)
    qg_d = k.din("qg", [128, 6])
    wqidx_d = k.din("wqidx", [128, 6, 1024])
    wuq_d = k.din("wuq", [128, 6, 1024])
    wukT_d = k.din("wukT", [128, 8, 256])
    wuv_d = k.din("wuv", [128, 2, 8, 128])
    rb_d = k.din("rb", [32, 8])
    ohv_d = k.din("ohv", [32, 384])
    kvg_d = k.din("kvg", [256])
    ckT_d = k.din("ckvnT", [128, 2, seq], BF16)
    V_d = k.din("V", [seq, 256], BF16)
    ki_d = k.din("kidxT", [64, seq], BF16)
    tpos_d = k.din("tposrel", [128, 1])
    iota_d = k.din("iota", [128, 512])
    sel_d = k.din("sel", [128, 10])
    J_d = k.din("J", [128, 128])
    ya = k.dout("ya", [nq * 128, 1024], BF16)
    vd = k.dscr("vd", [8, 384])
    b_vd = Buf()

    hm = HTMaker(k, modc, idf_d, inplace=True)
    wl = WLoader(k, 128, 1, nst=1)
    stg, b_stg = wl.st[0]
    stgf = stg[:].rearrange("p a b -> p (a b)")
    hqT, b_hqT = k.sb([128, 32, 128], BF16)
    qg, b_qg = k.sb([128, 6])
    wqidx, b_wqidx = k.sb([128, 6, 1024], BF16)
    wuq, b_wuq = k.sb([128, 6, 1024], BF16)
    wukT, b_wukT = k.sb([128, 8, 256], BF16)
    wuv, b_wuv = k.sb([128, 2, 8, 128], BF16)
    tpos, b_tpos = k.sb([128, 1])
    sel, b_sel = k.sb([128, 10])
    Jf, b_Jf = k.sb([128, 128])
    Jb, b_Jb = k.sb([128, 128], BF16)
    I4, b_I4 = k.sb([128, 4, 128], BF16)
    onesb, b_onesb = k.sb([128, 128], BF16)
    cb, b_cb = k.sb([128, 512])
    cbb, b_cbb = k.sb([128, 512], BF16)
    rb, b_rb = k.sb([32, 8])
    rb31, b_rb31 = k.sb([32, 8])
    ohv, b_ohv = k.sb([32, 384])
    vts, b_vts = k.sb([8, 384])
    r1, b_r1 = k.sb([1, 512])
    fac, b_fac = k.sb([1, 4])
    TBsel, b_TBsel = k.sb([128, 5, 1024], BF16)
    score, b_score = k.sb([128, seq])
    assert seq >= 3584 or True
    TBr = [(score[:, 0:1024], b_score), (score[:, 1024:2048], b_score)] if seq >= 4096 else [k.sb([128, 1024]) for _ in range(2)]
    if seq >= 4096:
        TBt, b_TBt = score[:, 2048:3072], b_score
        iota, b_iota = score[:, 3072:3584], b_score
    else:
        TBt, b_TBt = k.sb([128, 1024])
        iota, b_iota = k.sb([128, 512])

    pA, b_pA = k.ps([128, 512])
    pw = [k.ps([128, 512]) for _ in range(2)]
    pO = [k.ps([128, 512]) for _ in range(2)]
    pZ, b_pZ = k.ps([128, 512])

    selb, b_selb = k.sb([128, seq], BF16)
    cqs, b_cqs = k.sb([128, 784])
    widx, b_widx = k.sb([128, 16])
    cqT, b_cqT = k.sb([128, 6, 128], BF16)
    qiT, b_qiT = k.sb([64, 16, 128], BF16)
    qT, b_qT = k.sb([128, 8, 128], BF16)
    qlT, b_qlT = k.sb([128, 2, 8, 128], BF16)
    sq, b_sq = k.sb([128, 2, 1024], BF16)
    srow, b_srow = k.sb([1, 1024])
    nsr, b_nsr = k.sb([1, 1024], BF16)
    st, b_st = k.sb([128, 1])
    m8, b_m8 = k.sb([128, 8])
    kit = [k.sb([64, 512], BF16) for _ in range(2)]
    ckt = [k.sb([128, 2, 512], BF16) for _ in range(2)]
    Vt = [k.sb([128, 4, 256], BF16) for _ in range(2)]
    rl = [k.sb([128, 512]) for _ in range(2)]
    PT = [k.sb([128, 512], BF16) for _ in range(2)]
    rZ, b_rZ = k.sb([128, 512])
    olT, b_olT = k.sb([128, 2, 512], BF16)
    yat, b_yat = k.sb([128, 1024], BF16)

    for dst, src, bd in ((qg, qg_d, b_qg), (tpos, tpos_d, b_tpos), (sel, sel_d, b_sel),
                         (Jf, J_d, b_Jf), (rb, rb_d, b_rb), (ohv, ohv_d, b_ohv)):
        k.dma(dst[:], src, [], [bd], q="actq")
    k.dma(iota[:] if seq < 4096 else iota, iota_d, [], [b_iota], q="actq")
    k.dma(rb31[:], rb_d[31, :].partition_broadcast(32), [], [b_rb31], q="actq")
    k.cp(Jb[:], Jf[:], [b_Jf], [b_Jb])
    for c in range(4):
        k.cp(I4[:, c, :], hm.idf[:], [hm.b_idf], [b_I4])
    k.memset(onesb[:], 1.0, [b_onesb])
    for half in range(2):
        k.dma(stgf[:, 0:3072], wqidx_d[:, half * 3:(half + 1) * 3, :].rearrange("p a b -> p (a b)"), [], [b_stg])
        k.cp(wqidx[:, half * 3:(half + 1) * 3, :].rearrange("p a b -> p (a b)"), stgf[:, 0:3072], [b_stg], [b_wqidx], eng="pool")
    for half in range(2):
        k.dma(stgf[:, 0:3072], wuq_d[:, half * 3:(half + 1) * 3, :].rearrange("p a b -> p (a b)"), [], [b_stg])
        k.cp(wuq[:, half * 3:(half + 1) * 3, :].rearrange("p a b -> p (a b)"), stgf[:, 0:3072], [b_stg], [b_wuq], eng="pool")
    k.dma(stgf[:, 0:2048], wukT_d.rearrange("p a b -> p (a b)"), [], [b_stg])
    k.cp(wukT[:].rearrange("p a b -> p (a b)"), stgf[:, 0:2048], [b_stg], [b_wukT], eng="pool")
    k.dma(stgf[:, 0:2048], wuv_d.rearrange("p a b c -> p (a b c)"), [], [b_stg])
    k.cp(wuv[:].rearrange("p a b c -> p (a b c)"), stgf[:, 0:2048], [b_stg], [b_wuv], eng="pool")
    k.ts(cb[:], iota[:] if seq < 4096 else iota, tpos[:, 0:1], NEG_CAUSAL, ALU.is_gt, ALU.mult, [b_iota, b_tpos], [b_cb])
    k.ts(cbb[:], iota[:] if seq < 4096 else iota, tpos[:, 0:1], NEG_LOGIT, ALU.is_gt, ALU.mult, [b_iota, b_tpos], [b_cbb])
    k.dma(r1[:, 0:256], kvg_d[None, :], [], [b_r1], q="actq")
    k.P.op("dve", lambda e: e.tensor_reduce(out=fac[:, 0:1], in_=r1[:, 0:256], axis=AX.X, op=ALU.max,
                                            apply_absolute_value=True), reads=[b_r1], writes=[b_fac])
    k.dma(r1[:, 256:512], rb_d.rearrange("a b -> (a b)")[None, :], [], [b_r1], q="actq")
    k.P.op("dve", lambda e: e.tensor_reduce(out=fac[:, 1:2], in_=r1[:, 256:512], axis=AX.X, op=ALU.max,
                                            apply_absolute_value=True), reads=[b_r1], writes=[b_fac])
    k.ts(fac[:, 2:3], fac[:, 0:1], -16.0, None, ALU.mult, None, [b_fac], [b_fac])
    k.ts(fac[:, 3:4], fac[:, 1:2], -2.0, None, ALU.mult, None, [b_fac], [b_fac])
    k.tt(rb[:], rb[:], rb31[:], ALU.subtract, [b_rb, b_rb31], [b_rb])
    k.mm(pA[0:8, 0:384], rb[:], ohv[:], True, True, [b_rb, b_ohv], [b_pA])
    k.cp(vts[:], pA[0:8, 0:384], [b_pA], [b_vts])
    k.dma(vd, vts[:], [b_vts], [b_vd])
    for dl in range(2):
        t_, b_t = TBr[dl]
        src = bass.AP(tensor=vd.tensor, offset=128 * dl + 1, ap=[[1, 128], [384, 8], [1, 128]])
        k.dma(t_.rearrange("p (a b) -> p a b", a=8) if seq >= 4096 else t_[:].rearrange("p (a b) -> p a b", a=8), src, [b_vd], [b_t])
    for j in range(5):
        k.ts(TBt[:] if seq < 4096 else TBt, TBr[0][0][:] if seq < 4096 else TBr[0][0], sel[:, 2 * j:2 * j + 1], None, ALU.mult, None, [TBr[0][1], b_sel], [b_TBt])
        k.stt(TBsel[:, j, :], TBr[1][0][:] if seq < 4096 else TBr[1][0], sel[:, 2 * j + 1:2 * j + 2], TBt[:] if seq < 4096 else TBt, ALU.mult, ALU.add,
              [TBr[1][1], b_sel, b_TBt], [b_TBsel])

    inv_sqrt_d = float(128 ** -0.5)
    for qi in range(nq):
        Lk = 512 * (qi + 1)
        nkb = 4 * (qi + 1)
        hm.emit(xq[qi * 128:(qi + 1) * 128, :], lambda q: hqT[:, q, :], b_hqT)
        for j in range(7):
            n = 128 if j < 6 else 16
            wb, b_wb = wl.load(wq, j * 128, n)
            for kk in range(32):
                k.mm(pA[:, 0:n], hqT[:, kk, :], wb[:, kk, 0:n], kk == 0, kk == 31, [b_hqT, b_wb], [b_pA])
            k.act(cqs[:, j * 128:j * 128 + n], pA[:, 0:n], AF.Copy, [b_pA], [b_cqs])
        k.act(hm.junk[:, 0:768], cqs[:, 0:768], AF.Square, [b_cqs], [hm.b_junk, b_st], accum_out=st[:, 0:1])
        k.ts(st[:], st[:], 1.0 / 768, EPS, ALU.mult, ALU.add, [b_st], [b_st])
        k.act(st[:], st[:], AF.Sqrt, [b_st], [b_st])
        k.recip(st[:], st[:], [b_st], [b_st])
        k.ts(widx[:], cqs[:, 768:784], 1.0 / 32.0, None, ALU.mult, None, [b_cqs], [b_widx])
        k.ts(cqs[:, 0:768], cqs[:, 0:768], st[:, 0:1], None, ALU.mult, None, [b_cqs, b_st], [b_cqs])
        for c0, nchunk in ((0, 4), (4, 2)):
            for c in range(nchunk):
                k.tr(hm.ptf[:, c, :], cqs[:, (c0 + c) * 128:(c0 + c + 1) * 128], hm.idf[:], [b_cqs, hm.b_idf], [hm.b_ptf])
            for c in range(nchunk):
                k.act(cqT[:, c0 + c, :], hm.ptf[:, c, :], AF.Identity, [hm.b_ptf, b_qg], [b_cqT],
                      scale=qg[:, c0 + c:c0 + c + 1])
        for h4 in range(4):
            for hl in range(4):
                h = h4 * 4 + hl
                for rc in range(6):
                    k.mm(pA[0:64, hl * 128:(hl + 1) * 128], wqidx[:, rc, h * 64:(h + 1) * 64], cqT[:, rc, :],
                         rc == 0, rc == 5, [b_wqidx, b_cqT], [b_pA])
            k.act(qiT[:, h4 * 4:(h4 + 1) * 4, :].rearrange("p a b -> p (a b)"), pA[0:64, :], AF.Copy, [b_pA], [b_qiT])
        for h4 in range(2):
            for hl in range(4):
                h = h4 * 4 + hl
                for rc in range(6):
                    k.mm(pA[:, hl * 128:(hl + 1) * 128], wuq[:, rc, h * 128:(h + 1) * 128], cqT[:, rc, :],
                         rc == 0, rc == 5, [b_wuq, b_cqT], [b_pA])
            k.act(qT[:, h4 * 4:(h4 + 1) * 4, :].rearrange("p a b -> p (a b)"), pA[:], AF.Copy, [b_pA], [b_qT])
        for rc2 in range(2):
            for h4 in range(2):
                for hl in range(4):
                    h = h4 * 4 + hl
                    k.mm(pA[:, hl * 128:(hl + 1) * 128], wukT[:, h, rc2 * 128:(rc2 + 1) * 128], qT[:, h, :],
                         True, True, [b_wukT, b_qT], [b_pA])
                k.act(qlT[:, rc2, h4 * 4:(h4 + 1) * 4, :].rearrange("p a b -> p (a b)"), pA[:], AF.Copy,
                      [b_pA], [b_qlT], scale=inv_sqrt_d)
        qlf = qlT[:].rearrange("p a b c -> p a (b c)")
        k.tt(sq[:], qlf, qlf, ALU.mult, [b_qlT], [b_sq])
        for hp in range(2):
            for rc2 in range(2):
                k.mm(pA[:], onesb[:], sq[:, rc2, hp * 512:(hp + 1) * 512], rc2 == 0, rc2 == 1, [b_onesb, b_sq], [b_pA])
            k.act(srow[:, hp * 512:(hp + 1) * 512], pA[0:1, :], AF.Sqrt, [b_pA], [b_srow])
        k.ts(nsr[:], srow[:], fac[:, 2:3], fac[:, 3:4], ALU.mult, ALU.add, [b_srow, b_fac], [b_nsr])
        ih = 0
        for kt in range(qi + 1):
            ki_, b_ki = kit[kt % 2]
            k.dma(ki_[:], ki_d[:, kt * 512:(kt + 1) * 512], [], [b_ki])
            cs = slice(kt * 512, (kt + 1) * 512)
            for h in range(16):
                p_, b_p = pw[ih % 2]
                r_, b_r = rl[ih % 2]
                ih += 1
                k.mm(p_[:], qiT[:, h, :], ki_[:], True, True, [b_qiT, b_ki], [b_p])
                k.act(r_[:], p_[:], AF.Relu, [b_p], [b_r])
                if h == 0:
                    k.ts(score[:, cs], r_[:], widx[:, 0:1], None, ALU.mult, None, [b_r, b_widx], [b_score])
                else:
                    k.stt(score[:, cs], r_[:], widx[:, h:h + 1], score[:, cs], ALU.mult, ALU.add,
                          [b_r, b_widx, b_score], [b_score])
        last = slice(Lk - 512, Lk)
        k.tt(score[:, last], score[:, last], cb[:], ALU.add, [b_score, b_cb], [b_score])
        for r in range(32):
            k.P.op("dve", lambda e, Lk=Lk: e.max(out=m8[:], in_=score[:, 0:Lk]), reads=[b_score], writes=[b_m8])
            k.P.op("dve", lambda e, Lk=Lk: e.match_replace(out=score[:, 0:Lk], in_to_replace=m8[:],
                                                          in_values=score[:, 0:Lk], imm_value=NEG_SEL),
                   reads=[b_score, b_m8], writes=[b_score])
        k.ts(selb[:, 0:Lk], score[:, 0:Lk], -2.0e38, NEG_LOGIT, ALU.is_gt, ALU.mult, [b_score], [b_selb])
        k.tt(selb[:, last], selb[:, last], cbb[:], ALU.add, [b_selb, b_cbb], [b_selb])
        it = 0
        for hp in range(2):
            qlh = [qlT[:, rc2, hp * 4:(hp + 1) * 4, :].rearrange("p a b -> p (a b)") for rc2 in range(2)]
            for kt in range(qi + 1):
                c_, b_c = ckt[it % 2]
                v_, b_v = Vt[it % 2]
                it += 1
                k.dma(c_[:], ckT_d[:, :, kt * 512:(kt + 1) * 512], [], [b_c])
                k.dma(v_[:], V_d[kt * 512:(kt + 1) * 512, :].rearrange("(c p) r -> p c r", p=128), [], [b_v])
                for kb4 in range(4):
                    kb = kt * 4 + kb4
                    pl, b_pl = pw[kb % 2]
                    pt_, b_pt = PT[kb % 2]
                    ks = slice(kb4 * 128, (kb4 + 1) * 128)
                    k.mm(pl[:], c_[:, 0, ks], qlh[0], True, False, [b_c, b_qlT], [b_pl])
                    k.mm(pl[:], c_[:, 1, ks], qlh[1], False, False, [b_c, b_qlT], [b_pl])
                    k.mm(pl[:], selb[:, kb * 128:(kb + 1) * 128], I4[:].rearrange("p a b -> p (a b)"), False, False,
                         [b_selb, b_I4], [b_pl])
                    j = kb - (4 * qi - 1)
                    if 0 <= j <= 4:
                        k.mm(pl[:], Jb[:], TBsel[:, j, hp * 512:(hp + 1) * 512], False, False, [b_Jb, b_TBsel], [b_pl])
                    k.mm(pl[:], onesb[0:1, :], nsr[0:1, hp * 512:(hp + 1) * 512], False, True, [b_onesb, b_nsr], [b_pl])
                    k.act(pt_[:], pl[:], AF.Exp, [b_pl], [b_pt])
                    first, lastkb = kb == 0, kb == nkb - 1
                    k.mm(pO[0][0][:], v_[:, kb4, 0:128], pt_[:], first, lastkb, [b_v, b_pt], [pO[0][1]])
                    k.mm(pO[1][0][:], v_[:, kb4, 128:256], pt_[:], first, lastkb, [b_v, b_pt], [pO[1][1]])
                    k.mm(pZ[:], onesb[:], pt_[:], first, lastkb, [b_onesb, b_pt], [b_pZ])
            k.recip(rZ[:], pZ[:], [b_pZ], [b_rZ])
            for rc in range(2):
                k.tt(olT[:, rc, :], pO[rc][0][:], rZ[:], ALU.mult, [pO[rc][1], b_rZ], [b_olT])
            for hl in range(4):
                h = hp * 4 + hl
                for rc in range(2):
                    k.mm(pA[:, hl * 128:(hl + 1) * 128], olT[:, rc, hl * 128:(hl + 1) * 128], wuv[:, rc, h, :],
                         rc == 0, rc == 1, [b_olT, b_wuv], [b_pA])
            k.act(yat[:, hp * 512:(hp + 1) * 512], pA[:], AF.Copy, [b_pA], [b_yat])
        k.dma(ya[qi * 128:(qi + 1) * 128, :], yat[:], [b_yat], [], q="actq")
    return k.finish()


def _rel_bucket_np(n):
    n = np.asarray(n, np.int32)
    nf = np.maximum(n, 1).astype(np.float32)
    large = 16 + (np.log(nf / np.float32(16)) / np.float32(np.log(128 / 16)) * np.float32(16)).astype(np.int32)
    large = np.minimum(large, 31)
    return np.where(n < 16, n, large)


_OHV = np.zeros((32, 384), np.float32)
for _n in range(256):
    _OHV[int(_rel_bucket_np(_n)), _n + 128] = 1.0
_J = np.ascontiguousarray(np.eye(128, dtype=np.float32)[::-1])
_IOTA = np.ascontiguousarray(np.broadcast_to(np.arange(512, dtype=np.float32)[None], (128, 512)))


def run_1aq(inp, layer, x, modv_l, ckvnT, V, kidxT, nq=16, runner=None, cores=range(8), w_in_l=None):
    l = layer
    seq = nq * 512
    if w_in_l is None:
        w_in_l = run_wc(np.asarray(inp["w_in"][l]))
    nc = _get_nc(('1aq', nq, seq), lambda: build_1aq(nq, seq))
    wq = np.ascontiguousarray(np.concatenate([w_in_l[:, 0:768], w_in_l[:, 1088:1104]], axis=1))
    qg = np.ascontiguousarray(np.asarray(inp["mla_q_norm"][l], np.float32).reshape(6, 128).T)
    wqidx = np.ascontiguousarray(np.asarray(inp["w_qidx"][l]).reshape(6, 128, 1024).transpose(1, 0, 2))
    wuq = np.ascontiguousarray(np.asarray(inp["w_uq"][l]).reshape(6, 128, 1024).transpose(1, 0, 2))
    wukT = np.ascontiguousarray(np.asarray(inp["w_uk"][l]).transpose(2, 0, 1))
    wuv = np.ascontiguousarray(np.asarray(inp["w_uv"][l]).reshape(8, 2, 128, 128).transpose(2, 1, 0, 3))
    maps = []
    for ci in cores:
        b, g = ci // 4, ci % 4
        mr = np.ascontiguousarray(modv_l[b])
        mcm = np.ascontiguousarray(mr.reshape(6, 32, 128).transpose(0, 2, 1))
        xq = np.ascontiguousarray(x[b].reshape(64, 128, D)[g::4][:nq].reshape(nq * 128, D))
        sel = np.zeros((128, 10), np.float32)
        for j in range(5):
            dl = g + 1 - j
            if dl in (0, 1):
                sel[:, 2 * j + dl] = 1.0
        tpos = (g * 128 + np.arange(128, dtype=np.float32))[:, None]
        maps.append({"xq": xq, "modc": mcm, "identf": _IDENT, "wq": wq, "qg": qg, "wqidx": wqidx, "wuq": wuq,
                     "wukT": wukT, "wuv": wuv, "rb": np.ascontiguousarray(inp["rel_bias"]), "ohv": _OHV,
                     "kvg": np.ascontiguousarray(inp["mla_kv_norm"][l]),
                     "ckvnT": np.ascontiguousarray(ckvnT[b][:, :, :seq]), "V": np.ascontiguousarray(V[b][:seq]),
                     "kidxT": np.ascontiguousarray(kidxT[b][:, :seq]),
                     "tposrel": np.ascontiguousarray(tpos), "iota": _IOTA, "sel": sel, "J": _J})
    if runner is not None:
        return runner(nc, maps)
    res = run_bass_kernel_spmd(nc, maps, core_ids=list(range(8)))
    out = np.empty((2, 64, 128, 1024), ml_dtypes.bfloat16)
    for ci in range(8):
        b, g = ci // 4, ci % 4
        out[b, g::4] = res.results[ci]["ya"].reshape(16, 128, 1024)
    return out.reshape(2, 8192, 1024)


def build_wc(rows, cols):
    k = K()
    w = k.din("w", [rows, cols])
    o = k.dout("wb", [rows, cols], BF16)
    CW = max(c for c in range(1, 2049) if cols % c == 0)
    st = [k.sb([128, 2048]) for _ in range(3)]
    bf = [k.sb([128, 2048], BF16) for _ in range(3)]
    engs = ["pool", "dve", "act"]
    i = 0
    for r in range(rows // 128):
        for c in range(cols // CW):
            s_, b_s = st[i % 3]
            o_, b_o = bf[i % 3]
            k.dma(s_[:, 0:CW], w[r * 128:(r + 1) * 128, c * CW:(c + 1) * CW], [], [b_s])
            e = engs[i % 3]
            if e == "act":
                k.act(o_[:, 0:CW], s_[:, 0:CW], AF.Copy, [b_s], [b_o])
            else:
                k.cp(o_[:, 0:CW], s_[:, 0:CW], [b_s], [b_o], eng=e)
            k.dma(o[r * 128:(r + 1) * 128, c * CW:(c + 1) * CW], o_[:, 0:CW], [b_o], [], q="actq")
            i += 1
    return k.finish()


_NC_CACHE = {}


def _get_nc(key, fn):
    if key not in _NC_CACHE:
        _NC_CACHE[key] = fn()
    return _NC_CACHE[key]


def run_wc(w):
    R, C = w.shape
    rs = R // 8
    nc = _get_nc(("wc", rs, C), lambda: build_wc(rs, C))
    maps = [{"w": np.ascontiguousarray(w[ci * rs:(ci + 1) * rs])} for ci in range(8)]
    res = run_bass_kernel_spmd(nc, maps, core_ids=list(range(8)))
    return np.concatenate([r["wb"] for r in res.results], axis=0)


def kernel(**inp):
    x = np.ascontiguousarray(np.asarray(inp["x"], np.float32))
    modv = run_mod(inp)
    for l in range(2):
        w_in_l = run_wc(np.asarray(inp["w_in"][l]))
        ckvnT, V, kidxT = run_1ak(l, x, modv[l], w_in_l, inp["mla_kv_norm"][l])
        ya = run_1aq(inp, l, x, modv[l], ckvnT, V, kidxT, w_in_l=w_in_l)
        yb = run_1b(inp, l, x, modv[l], w_in_l=w_in_l)
        yc = run_1c(l, x, modv[l], w_in_l, inp["hgrn_lb"], inp["hgrn_onorm"][l])
        y = np.ascontiguousarray(np.concatenate([ya, yb, yc], axis=-1))
        wo = run_wc(np.asarray(inp["w_out"][l]))
        w1 = run_wc(np.asarray(inp["w_ff1"][l]))
        w2 = run_wc(np.asarray(inp["w_ff2"][l]))
        x = run_l2(x, y, modv[l], wo, w1, w2)
    return x
```

```python
import numpy as np
from contextlib import ExitStack
import ml_dtypes
import concourse.bass as bass
import concourse.mybir as mybir
from concourse.bass_utils import run_bass_kernel_spmd

F32 = mybir.dt.float32
BF16 = mybir.dt.bfloat16
AF = mybir.ActivationFunctionType
ALU = mybir.AluOpType
AX = mybir.AxisListType

D = 4096
SEQ = 8192
NB = 2
DFF = 16384
EPS = 1e-6


class Buf:
    __slots__ = ("name", "last_w", "readers")

    def __init__(self, name=""):
        self.name = name
        self.last_w = None
        self.readers = []


class Prog:
    COMPUTE = ("pe", "act", "dve", "pool")
    QUEUES = {"sp": 24, "actq": 8, "poolq": 8}
    Q2ENG = {"sp": "sp", "actq": "act", "poolq": "pool"}

    def __init__(self, nc):
        self.nc = nc
        self.streams = {e: [] for e in ("pe", "act", "dve", "pool", "sp")}
        self.nops = {e: 0 for e in self.COMPUTE}
        self.marked = {e: set() for e in self.COMPUTE}
        self.dma_n = {q: 0 for q in self.QUEUES}
        self.dma_val = {}

    def _deps(self, eng, reads, writes):
        deps = []
        for b in reads:
            if b.last_w is not None:
                deps.append((b.last_w, True))
        for b in writes:
            if b.last_w is not None:
                deps.append((b.last_w, True))
            for r in b.readers:
                deps.append((r, False))
        out = {}
        for d, strong in deps:
            if d[0] == "c" and d[1] == eng:
                if eng == "pe":
                    continue
            out[d] = True
        return list(out.keys())

    def _finish(self, tok, reads, writes):
        for b in writes:
            b.last_w = tok
            b.readers = []
        for b in reads:
            if b not in writes:
                b.readers.append(tok)

    def op(self, eng, fn, reads=(), writes=()):
        deps = self._deps(eng, reads, writes)
        idx = self.nops[eng]
        self.nops[eng] += 1
        tok = ("c", eng, idx)
        for d in deps:
            if d[0] == "c":
                self.marked[d[1]].add(d[2])
        self.streams[eng].append(("op", fn, deps, tok))
        self._finish(tok, reads, writes)
        return tok

    def dma(self, q, fn, reads=(), writes=()):
        eng = self.Q2ENG[q]
        deps = self._deps("dma", reads, writes)
        n = self.dma_n[q]
        self.dma_n[q] += 1
        slot = n % self.QUEUES[q]
        prev = self.dma_val.get((q, slot), 0)
        val = prev + 16
        self.dma_val[(q, slot)] = val
        tok = ("d", q, slot, val)
        if prev > 0:
            deps.append(("d", q, slot, prev))
        for d in deps:
            if d[0] == "c":
                self.marked[d[1]].add(d[2])
        self.streams[eng].append(("dma", fn, deps, tok))
        self._finish(tok, reads, writes)
        return tok

    def emit(self):
        nc = self.nc
        with ExitStack() as es:
            csem = {e: es.enter_context(nc.semaphore("s_" + e)) for e in self.COMPUTE}
            dsem = {}
            for q, n in self.QUEUES.items():
                for s in range(n):
                    dsem[(q, s)] = es.enter_context(nc.semaphore("d_%s_%d" % (q, s)))
            cnt = {}
            for e in self.COMPUTE:
                c = 0
                m = self.marked[e]
                arr = np.zeros(self.nops[e] + 1, dtype=np.int64)
                for i in range(self.nops[e]):
                    if i in m:
                        c += 1
                    arr[i] = c
                cnt[e] = arr
            final_dma = dict(self.dma_val)
            block = es.enter_context(nc.Block())

            def run_stream(engname, engobj):
                known_c = {e: 0 for e in self.COMPUTE}
                known_d = {}
                for kind, fn, deps, tok in self.streams[engname]:
                    need_c = {}
                    need_d = {}
                    for d in deps:
                        if d[0] == "c":
                            v = int(cnt[d[1]][d[2]])
                            if v > known_c[d[1]] and v > need_c.get(d[1], 0):
                                need_c[d[1]] = v
                        else:
                            key = (d[1], d[2])
                            if d[3] > known_d.get(key, 0) and d[3] > need_d.get(key, 0):
                                need_d[key] = d[3]
                    for e, v in need_c.items():
                        engobj.wait_ge(csem[e], v)
                        known_c[e] = v
                    for key, v in need_d.items():
                        engobj.wait_ge(dsem[key], v)
                        known_d[key] = v
                    ins = fn(engobj)
                    if kind == "op":
                        if tok[2] in self.marked[tok[1]]:
                            ins.then_inc(csem[tok[1]], 1)
                    else:
                        ins.then_inc(dsem[(tok[1], tok[2])], 16)
                if engname == "sp":
                    for key, v in final_dma.items():
                        if v > known_d.get(key, 0):
                            engobj.wait_ge(dsem[key], v)

            @block.tensor
            def _(e):
                run_stream("pe", e)

            @block.vector
            def _(e):
                run_stream("dve", e)

            @block.scalar
            def _(e):
                run_stream("act", e)

            @block.gpsimd
            def _(e):
                run_stream("pool", e)

            @block.sync
            def _(e):
                run_stream("sp", e)


class K:
    def __init__(self):
        self.nc = bass.Bass("TRN2", target_bir_lowering=False)
        self.P = Prog(self.nc)
        self.es = ExitStack()
        self.n = 0

    def din(self, name, shape, dt=F32):
        return self.nc.dram_tensor(name, list(shape), dt, kind="ExternalInput").ap()

    def dout(self, name, shape, dt=F32):
        return self.nc.dram_tensor(name, list(shape), dt, kind="ExternalOutput").ap()

    def dscr(self, name, shape, dt=F32):
        return self.nc.dram_tensor(name, list(shape), dt, kind="Internal").ap()

    def sb(self, shape, dt=F32, name=None):
        self.n += 1
        t = self.es.enter_context(self.nc.sbuf_tensor(name or ("sb%d" % self.n), list(shape), dt))
        return t, Buf(name or "")

    def ps(self, shape, dt=F32, name=None):
        self.n += 1
        t = self.es.enter_context(self.nc.psum_tensor(name or ("ps%d" % self.n), list(shape), dt))
        return t, Buf(name or "")

    def mm(self, out, lhsT, rhs, start, stop, reads, writes):
        self.P.op("pe", lambda e: e.matmul(out, lhsT=lhsT, rhs=rhs, start=start, stop=stop),
                  reads=reads, writes=writes)

    def tr(self, out, in_, ident, reads, writes):
        self.P.op("pe", lambda e: e.transpose(out=out, in_=in_, identity=ident),
                  reads=reads, writes=writes)

    def act(self, out, in_, func, reads, writes, bias=None, scale=None, accum_out=None, eng="act"):
        kw = {}
        if bias is not None:
            kw["bias"] = bias
        if scale is not None:
            kw["scale"] = scale
        if accum_out is not None:
            kw["accum_out"] = accum_out
        self.P.op("act", lambda e: e.activation(out=out, in_=in_, func=func, **kw),
                  reads=reads, writes=writes)

    def tt(self, out, in0, in1, op, reads, writes, eng="dve"):
        self.P.op(eng, lambda e: e.tensor_tensor(out=out, in0=in0, in1=in1, op=op),
                  reads=reads, writes=writes)

    def ts(self, out, in0, s1, s2, op0, op1, reads, writes, eng="dve", accum_out=None):
        if op1 is None:
            self.P.op(eng, lambda e: e.tensor_scalar(out=out, in0=in0, scalar1=s1, scalar2=None, op0=op0),
                      reads=reads, writes=writes)
        elif accum_out is not None:
            self.P.op(eng, lambda e: e.tensor_scalar(out=out, in0=in0, scalar1=s1, scalar2=s2, op0=op0,
                                                     op1=op1, accum_out=accum_out),
                      reads=reads, writes=writes)
        else:
            self.P.op(eng, lambda e: e.tensor_scalar(out=out, in0=in0, scalar1=s1, scalar2=s2, op0=op0, op1=op1),
                      reads=reads, writes=writes)

    def stt(self, out, in0, scalar, in1, op0, op1, reads, writes, accum_out=None):
        if accum_out is None:
            self.P.op("dve", lambda e: e.scalar_tensor_tensor(out=out, in0=in0, scalar=scalar, in1=in1,
                                                              op0=op0, op1=op1),
                      reads=reads, writes=writes)
        else:
            self.P.op("dve", lambda e: e.scalar_tensor_tensor(out=out, in0=in0, scalar=scalar, in1=in1,
                                                              op0=op0, op1=op1, accum_out=accum_out),
                      reads=reads, writes=writes)

    def cp(self, out, in_, reads, writes, eng="dve"):
        self.P.op(eng, lambda e: e.tensor_copy(out=out, in_=in_), reads=reads, writes=writes)

    def recip(self, out, in_, reads, writes):
        self.P.op("dve", lambda e: e.reciprocal(out=out, in_=in_), reads=reads, writes=writes)

    def memset(self, ap, v, writes, eng="pool"):
        self.P.op(eng, lambda e: e.memset(ap, v), writes=writes)

    def dma(self, out, in_, reads, writes, q="sp", **kw):
        self.P.dma(q, lambda e: e.dma_start(out=out, in_=in_, **kw), reads=reads, writes=writes)

    def rstd(self, ssq, n, eps, tmpb=None):
        t, b = ssq
        self.ts(t, t, 1.0 / n, eps, ALU.mult, ALU.add, [b], [b])
        self.act(t, t, AF.Sqrt, [b], [b])
        self.recip(t, t, [b], [b])

    def finish(self):
        self.P.emit()
        self.es.close()
        return self.nc


def build_mod():
    k = K()
    cT = k.din("cT", [128, 32, 2])
    aw = k.din("aw", [2, 6, 4096, 512])
    ab = k.din("ab", [2, 6, 512])
    ng = k.din("ng", [2, 4, 512])
    out = k.dout("modv", [2, 2, 6, 512])
    ct, b_ct = k.sb([128, 32, 2])
    ca, b_ca = k.sb([128, 32, 2])
    sg, b_sg = k.sb([128, 32, 2])
    wb = [k.sb([128, 16, 512]) for _ in range(3)]
    abt, b_ab = k.sb([2, 2, 6, 512])
    ngt, b_ng = k.sb([2, 2, 4, 512])
    modt, b_mod = k.sb([2, 6, 512])
    res, b_res = k.sb([2, 2, 6, 512])
    pm = [k.ps([2, 512]) for _ in range(2)]
    k.dma(ct[:], cT, [], [b_ct])
    for b in range(2):
        k.dma(abt[b:b + 1], ab[None], [], [b_ab], q="actq")
        k.dma(ngt[b:b + 1], ng[None], [], [b_ng], q="actq")
    k.act(sg[:], ct[:], AF.Sigmoid, [b_ct], [b_sg])
    k.tt(ca[:], ct[:], sg[:], ALU.mult, [b_ct, b_sg], [b_ca])
    it = 0
    for l in range(2):
        for j in range(6):
            pt, b_pt = pm[(l * 6 + j) % 2]
            for hf in range(2):
                wt, b_wt = wb[it % 3]
                it += 1
                src = aw[l, j, hf * 2048:(hf + 1) * 2048, :].rearrange("(c p) n -> p c n", p=128)
                k.dma(wt[:], src, [], [b_wt])
                for c in range(16):
                    kk = hf * 16 + c
                    k.mm(pt[:], ca[:, kk, :], wt[:, c, :], kk == 0, kk == 31, [b_ca, b_wt], [b_pt])
            k.tt(modt[:, j, :], pt[:], abt[:, l, j, :], ALU.add, [b_pt, b_ab], [b_mod])
        k.stt(res[:, l, 0, :], modt[:, 1, :], 1.0, ngt[:, l, 0, :], ALU.add, ALU.mult, [b_mod, b_ng], [b_res])
        k.cp(res[:, l, 1, :], modt[:, 0, :], [b_mod], [b_res])
        k.tt(res[:, l, 2, :], modt[:, 2, :], ngt[:, l, 1, :], ALU.mult, [b_mod, b_ng], [b_res])
        k.stt(res[:, l, 3, :], modt[:, 4, :], 1.0, ngt[:, l, 2, :], ALU.add, ALU.mult, [b_mod, b_ng], [b_res])
        k.cp(res[:, l, 4, :], modt[:, 3, :], [b_mod], [b_res])
        k.tt(res[:, l, 5, :], modt[:, 5, :], ngt[:, l, 3, :], ALU.mult, [b_mod, b_ng], [b_res])
    k.dma(out.rearrange("l b v n -> b l v n"), res[:], [b_res], [])
    return k.finish()


def run_mod(inp):
    c = np.asarray(inp["c"], np.float32)
    cT = np.ascontiguousarray(c.reshape(2, 32, 128).transpose(2, 1, 0))
    ada_w = inp["ada_w"]
    ada_b = np.asarray(inp["ada_b"], np.float32)
    norm_g = np.asarray(inp["norm_g"], np.float32)
    maps = []
    for ci in range(8):
        sl = slice(ci * 512, (ci + 1) * 512)
        aw = np.ascontiguousarray(ada_w.reshape(2, 4096, 6, 4096)[:, :, :, sl].transpose(0, 2, 1, 3))
        ab = np.ascontiguousarray(ada_b.reshape(2, 6, 4096)[:, :, sl])
        ng = np.ascontiguousarray(norm_g[:, :, sl])
        maps.append({"cT": cT, "aw": aw, "ab": ab, "ng": ng})
    nc = _get_nc('mod', build_mod)
    res = run_bass_kernel_spmd(nc, maps, core_ids=list(range(8)))
    modv = np.concatenate([r["modv"] for r in res.results], axis=-1)
    return modv


TG = 256
NT = TG // 128
KC = 8
NWB = 4


def build_l2(ntok=2048, stop=None):
    k = K()
    x = k.din("x", [ntok, D])
    y = k.din("y", [ntok, D], BF16)
    modr = k.din("modr", [6, D])
    modc = k.din("modc", [6, 128, 32])
    w_out = k.din("w_out", [D, D], BF16)
    w1 = k.din("w1", [D, DFF], BF16)
    w2 = k.din("w2", [DFF, D], BF16)
    idf_d = k.din("identf", [128, 128])
    xo = k.dout("xo", [ntok, D])

    idf, b_idf = k.sb([128, 128])
    idb, b_idb = k.sb([128, 128], BF16)
    mc, b_mc = k.sb([128, 6, 32])
    actT, b_actT = k.sb([128, 32, TG], BF16)
    hid, b_hid = k.sb([128, 128, TG], BF16)
    o1 = [k.sb([128, D]) for _ in range(NT)]
    tx, b_tx = k.sb([128, D])
    rowb, b_rowb = k.sb([128, D])
    yb, b_yb = k.sb([128, D], BF16)
    wbf = [k.sb([128, KC, 512], BF16) for _ in range(NWB)]
    rl, b_rl = k.sb([128, 4, TG])
    st, b_st = k.sb([128, 4])
    acc = [k.ps([128, 512]) for _ in range(NT)]
    accF, b_accF = k.ps([128, 4, 512])
    ptb, b_ptb = k.ps([128, 4, 128], BF16)
    ptf, b_ptf = k.ps([128, 4, 128])

    k.dma(idf[:], idf_d, [], [b_idf])
    k.dma(mc[:], modc.rearrange("v p c -> p v c"), [], [b_mc])
    k.cp(idb[:], idf[:], [b_idf], [b_idb])
    wi = [0]

    def load_w(src):
        i = wi[0] % NWB
        wi[0] += 1
        wb_, b_wb = wbf[i]
        k.dma(wb_[:], src, [], [b_wb])
        return wb_, b_wb

    def sumsq(src, b_src, col):
        k.act(yb[:], src, AF.Square, [b_src], [b_yb, b_st], accum_out=st[:, col:col + 1])

    def rstd_col(col):
        t = st[:, col:col + 1]
        k.ts(t, t, 1.0 / D, EPS, ALU.mult, ALU.add, [b_st], [b_st])
        k.act(t, t, AF.Sqrt, [b_st], [b_st])
        k.recip(t, t, [b_st], [b_st])

    for ps_ in range(ntok // TG):
        t0 = ps_ * TG
        b_xo = [Buf() for _ in range(NT)]
        for mt in range(NT):
            r0 = t0 + mt * 128
            k.dma(yb[:], y[r0:r0 + 128, :], [], [b_yb])
            for g in range(8):
                for c in range(4):
                    k.tr(ptb[:, c, :], yb[:, (g * 4 + c) * 128:(g * 4 + c + 1) * 128], idb[:], [b_yb, b_idb], [b_ptb])
                k.cp(actT[:, g * 4:(g + 1) * 4, mt * 128:(mt + 1) * 128], ptb[:], [b_ptb], [b_actT])
        for nb in range(8):
            for kt in range(32 // KC):
                src = w_out[kt * KC * 128:(kt + 1) * KC * 128, nb * 512:(nb + 1) * 512].rearrange("(c p) n -> p c n", p=128)
                wb_, b_wb = load_w(src)
                for mt in range(NT):
                    a, b_a = acc[mt]
                    for c in range(KC):
                        kk = kt * KC + c
                        k.mm(a[:], actT[:, kk, mt * 128:(mt + 1) * 128], wb_[:, c, :], kk == 0, kk == 31, [b_actT, b_wb], [b_a])
            for mt in range(NT):
                a, b_a = acc[mt]
                o, b_o = o1[mt]
                k.act(o[:, nb * 512:(nb + 1) * 512], a[:], AF.Copy, [b_a], [b_o])
        for mt in range(NT):
            r0 = t0 + mt * 128
            o, b_o = o1[mt]
            sumsq(o[:], b_o, 0)
            rstd_col(0)
            k.dma(tx[:], x[r0:r0 + 128, :], [], [b_tx])
            k.dma(rowb[:], modr[2, :].partition_broadcast(128), [], [b_rowb], q="actq")
            k.stt(o[:], o[:], st[:, 0:1], rowb[:], ALU.mult, ALU.mult, [b_o, b_st, b_rowb], [b_o])
            k.tt(tx[:], tx[:], o[:], ALU.add, [b_tx, b_o], [b_tx], eng="pool")
            k.dma(xo[r0:r0 + 128, :], tx[:], [b_tx], [b_xo[mt]])
            sumsq(tx[:], b_tx, 1)
            rstd_col(1)
            k.act(o[:], tx[:], AF.Identity, [b_tx, b_st], [b_o], scale=st[:, 1:2])
            for g in range(8):
                for c in range(4):
                    k.tr(ptf[:, c, :], o[:, (g * 4 + c) * 128:(g * 4 + c + 1) * 128], idf[:], [b_o, b_idf], [b_ptf])
                for c in range(4):
                    kk = g * 4 + c
                    k.act(actT[:, kk, mt * 128:(mt + 1) * 128], ptf[:, c, :], AF.Identity, [b_ptf, b_mc], [b_actT],
                          bias=mc[:, 4, kk:kk + 1], scale=mc[:, 3, kk:kk + 1])
        if stop == 'C':
            continue
        for g in range(DFF // 512):
            for kt in range(32 // KC):
                src = w1[kt * KC * 128:(kt + 1) * KC * 128, g * 512:(g + 1) * 512].rearrange("(c p) n -> p c n", p=128)
                wb_, b_wb = load_w(src)
                for fb in range(4):
                    for c in range(KC):
                        kk = kt * KC + c
                        k.mm(accF[:, fb, 0:TG], wb_[:, c, fb * 128:(fb + 1) * 128], actT[:, kk, :], kk == 0, kk == 31,
                             [b_actT, b_wb], [b_accF])
            k.act(rl[:], accF[:, :, 0:TG], AF.Relu, [b_accF], [b_rl])
            k.tt(hid[:, g * 4:(g + 1) * 4, :], rl[:], rl[:], ALU.mult, [b_rl], [b_hid])
        for db in range(8):
            for ft in range(128 // KC):
                src = w2[ft * KC * 128:(ft + 1) * KC * 128, db * 512:(db + 1) * 512].rearrange("(c p) n -> p c n", p=128)
                wb_, b_wb = load_w(src)
                for mt in range(NT):
                    a, b_a = acc[mt]
                    for c in range(KC):
                        kk = ft * KC + c
                        k.mm(a[:], hid[:, kk, mt * 128:(mt + 1) * 128], wb_[:, c, :], kk == 0, kk == 127, [b_hid, b_wb], [b_a])
            for mt in range(NT):
                a, b_a = acc[mt]
                o, b_o = o1[mt]
                k.act(o[:, db * 512:(db + 1) * 512], a[:], AF.Copy, [b_a], [b_o])
        for mt in range(NT):
            r0 = t0 + mt * 128
            o, b_o = o1[mt]
            sumsq(o[:], b_o, 2)
            rstd_col(2)
            k.dma(tx[:], xo[r0:r0 + 128, :], [b_xo[mt]], [b_tx])
            k.dma(rowb[:], modr[5, :].partition_broadcast(128), [], [b_rowb], q="actq")
            k.stt(o[:], o[:], st[:, 2:3], rowb[:], ALU.mult, ALU.mult, [b_o, b_st, b_rowb], [b_o])
            k.tt(tx[:], tx[:], o[:], ALU.add, [b_tx, b_o], [b_tx], eng="pool")
            k.dma(xo[r0:r0 + 128, :], tx[:], [b_tx], [b_xo[mt]])
    return k.finish()


_IDENT = np.eye(128, dtype=np.float32)


def run_l2(x, ybf, modv_l, w_out, w1, w2):
    nc = _get_nc('l2', build_l2)
    maps = []
    for ci in range(8):
        b, q = ci // 4, ci % 4
        sl = slice(q * 2048, (q + 1) * 2048)
        mr = np.ascontiguousarray(modv_l[b])
        mcm = np.ascontiguousarray(mr.reshape(6, 32, 128).transpose(0, 2, 1))
        maps.append({"x": np.ascontiguousarray(x[b, sl]), "y": np.ascontiguousarray(ybf[b, sl]),
                     "modr": mr, "modc": mcm, "w_out": w_out, "w1": w1, "w2": w2, "identf": _IDENT})
    res = run_bass_kernel_spmd(nc, maps, core_ids=list(range(8)))
    out = np.empty((2, 8192, 4096), np.float32)
    for ci in range(8):
        b, q = ci // 4, ci % 4
        out[b, q * 2048:(q + 1) * 2048] = res.results[ci]["xo"]
    return out


class HTMaker:
    def __init__(self, k, modc_d, idf_d, inplace=False, junk=None):
        self.k = k
        self.idf, self.b_idf = k.sb([128, 128])
        self.idb, self.b_idb = k.sb([128, 128], BF16)
        self.mc, self.b_mc = k.sb([128, 6, 32])
        self.tx, self.b_tx = k.sb([128, D])
        if inplace:
            self.xn, self.b_xn = self.tx, self.b_tx
        else:
            self.xn, self.b_xn = k.sb([128, D])
        self.junk, self.b_junk = junk if junk is not None else k.sb([128, D], BF16)
        self.st, self.b_st = k.sb([128, 2])
        self.ptf, self.b_ptf = k.ps([128, 4, 128])
        k.dma(self.idf[:], idf_d, [], [self.b_idf])
        k.dma(self.mc[:], modc_d.rearrange("v p c -> p v c"), [], [self.b_mc])
        k.cp(self.idb[:], self.idf[:], [self.b_idf], [self.b_idb])

    def emit(self, src, dst_fn, b_dst, ai=0, bi=1):
        k = self
        kk_ = self.k
        kk_.dma(self.tx[:], src, [], [self.b_tx])
        kk_.act(self.junk[:], self.tx[:], AF.Square, [self.b_tx], [self.b_junk, self.b_st],
                accum_out=self.st[:, 0:1])
        t = self.st[:, 0:1]
        kk_.ts(t, t, 1.0 / D, EPS, ALU.mult, ALU.add, [self.b_st], [self.b_st])
        kk_.act(t, t, AF.Sqrt, [self.b_st], [self.b_st])
        kk_.recip(t, t, [self.b_st], [self.b_st])
        kk_.act(self.xn[:], self.tx[:], AF.Identity, [self.b_tx, self.b_st], [self.b_xn], scale=self.st[:, 0:1])
        for g in range(8):
            for c in range(4):
                q = g * 4 + c
                kk_.tr(self.ptf[:, c, :], self.xn[:, q * 128:(q + 1) * 128], self.idf[:],
                       [self.b_xn, self.b_idf], [self.b_ptf])
            for c in range(4):
                q = g * 4 + c
                kk_.act(dst_fn(q), self.ptf[:, c, :], AF.Identity, [self.b_ptf, self.b_mc], [b_dst],
                        bias=self.mc[:, bi, q:q + 1], scale=self.mc[:, ai, q:q + 1])


class WLoader:
    def __init__(self, k, ncol=128, nbuf=2, nst=None):
        self.k = k
        self.ncol = ncol
        self.st = [k.sb([128, 32, ncol]) for _ in range(nst or nbuf)]
        self.bf = [k.sb([128, 32, ncol], BF16) for _ in range(nbuf)]
        self.i = 0

    def load(self, W, c0, n):
        k = self.k
        ws, b_ws = self.st[self.i % len(self.st)]
        wb, b_wb = self.bf[self.i % len(self.bf)]
        self.i += 1
        if W.dtype == BF16:
            k.dma(wb[:, :, 0:n], W[:, c0:c0 + n].rearrange("(c p) n -> p c n", p=128), [], [b_wb])
            return wb, b_wb
        k.dma(ws[:, :, 0:n], W[:, c0:c0 + n].rearrange("(c p) n -> p c n", p=128), [], [b_ws])
        k.cp(wb[:, :, 0:n], ws[:, :, 0:n], [b_ws], [b_wb], eng="pool")
        return wb, b_wb


TB = 512
CH = 64


def gemm_fm(k, wb, b_wb, ncols_off, hT, b_hT, out, b_out, M=128):
    for kk in range(32):
        k.mm(out, wb[:, kk, ncols_off:ncols_off + M], hT[:, kk, :], kk == 0, kk == 31, [b_wb, b_hT], [b_out])


def build_1c(layer):
    HC = 3
    k = K()
    x = k.din("x", [SEQ, D])
    modc = k.din("modc", [6, 128, 32])
    idf_d = k.din("identf", [128, 128])
    w = k.din("w", [D, 4 * HC * 128], BF16)
    lbraw = k.din("lbraw", [128, HC, 2])
    onorm = k.din("onorm", [HC * 128])
    cmask_d = k.din("cmask", [64, 64])
    smask_d = k.din("smask", [128, TB])
    yo = k.dout("yc", [SEQ, HC * 128], BF16)

    hm = HTMaker(k, modc, idf_d)
    wl = WLoader(k, 128, 2)
    hT, b_hT = k.sb([128, 32, TB], BF16)
    cmask, b_cm = k.sb([64, 64])
    smask, b_sm = k.sb([128, TB])
    lbt, b_lb = k.sb([128, HC, 2])
    lbw, b_lbw = k.sb([128, 6, HC])
    lb, b_lbv = k.sb([128, HC])
    oml, b_oml = k.sb([128, HC])
    onb, b_onb = k.sb([64, HC * 128])
    k.dma(cmask[:], cmask_d, [], [b_cm])
    k.dma(smask[:], smask_d, [], [b_sm])
    k.dma(lbt[:], lbraw, [], [b_lb])
    k.dma(onb[:], onorm.partition_broadcast(64), [], [b_onb])
    m_ = lbw[:, 0, :]
    k.tt(m_, lbt[:, :, 0], lbt[:, :, 1], ALU.max, [b_lb], [b_lbw])
    k.tt(lbw[:, 1, :], lbt[:, :, 0], m_, ALU.subtract, [b_lb, b_lbw], [b_lbw])
    k.tt(lbw[:, 2, :], lbt[:, :, 1], m_, ALU.subtract, [b_lb, b_lbw], [b_lbw])
    k.act(lbw[:, 1, :], lbw[:, 1, :], AF.Exp, [b_lbw], [b_lbw])
    k.act(lbw[:, 2, :], lbw[:, 2, :], AF.Exp, [b_lbw], [b_lbw])
    k.tt(lbw[:, 3, :], lbw[:, 1, :], lbw[:, 2, :], ALU.add, [b_lbw], [b_lbw])
    k.recip(lbw[:, 3, :], lbw[:, 3, :], [b_lbw], [b_lbw])
    k.tt(lbw[:, 4, :], lbw[:, 1, :], lbw[:, 3, :], ALU.mult, [b_lbw], [b_lbw])
    k.tt(lbw[:, 5, :], lbw[:, 2, :], lbw[:, 3, :], ALU.mult, [b_lbw], [b_lbw])
    if layer == 0:
        k.tt(lb[:], lbw[:, 4, :], lbw[:, 4, :], ALU.subtract, [b_lbw], [b_lbv])
    else:
        k.tt(lb[:], lbw[:, 4, :], lbw[:, 5, :], ALU.add, [b_lbw], [b_lbv])
        k.tt(lb[:], lb[:], lbw[:, 4, :], ALU.subtract, [b_lbw, b_lbv], [b_lbv])
    k.ts(oml[:], lb[:], -1.0, 1.0, ALU.mult, ALU.add, [b_lbv], [b_oml])

    pq = [k.ps([128, TB]) for _ in range(2)]
    pv, b_pv = k.ps([64, 4, 128])
    pAT, b_pAT = k.ps([64, HC, 64])
    po, b_po = k.ps([64, HC, 128])
    pS, b_pS = k.ps([128, HC, 128])
    pkt, b_pkt = k.ps([64, HC, 128], BF16)

    qs, b_qs = k.sb([128, TB])
    sg, b_sg = k.sb([128, TB])
    sgn, b_sgn = k.sb([128, TB])
    lf, b_lf = k.sb([128, TB])
    bb, b_bb = k.sb([128, TB])
    enb, b_enb = k.sb([128, TB])
    eb, b_eb = k.sb([128, HC, TB])
    qt, b_qt = k.sb([128, HC, TB], BF16)
    kt, b_kt = k.sb([128, HC, TB], BF16)
    V, b_V = k.sb([64, 8, HC * 128], BF16)
    gw, b_gw = k.sb([64, 8, HC * 128])
    gs, b_gs = k.sb([64, 4, 128])
    S, b_S = k.sb([128, HC, 128])
    Sb, b_Sb = k.sb([128, HC, 128], BF16)
    t1, b_t1 = k.sb([128, HC, 128])
    ATs, b_ATs = k.sb([64, HC, 64], BF16)
    kts, b_kts = k.sb([64, HC, 128], BF16)
    st, b_st = k.sb([64, HC])
    junk, b_junk = k.sb([64, 128])
    yt = [k.sb([64, HC * 128], BF16) for _ in range(2)]
    k.memset(S[:], 0.0, [b_S])
    k.memset(Sb[:], 0.0, [b_Sb])
    pqi = 0
    def HT(tb):
        for mt in range(TB // 128):
            hm.emit(x[tb * TB + mt * 128:tb * TB + (mt + 1) * 128, :],
                    lambda q, mt=mt: hT[:, q, mt * 128:(mt + 1) * 128], b_hT)

    HT(0)
    for tb in range(SEQ // TB):
        t0 = tb * TB
        for h in range(HC):
            wb, b_wb = wl.load(w, h * 128, 128)
            p_, b_p = pq[pqi % 2]; pqi += 1
            gemm_fm(k, wb, b_wb, 0, hT, b_hT, p_[:], b_p)
            k.act(qs[:], p_[:], AF.Silu, [b_p], [b_qs])
            wb, b_wb = wl.load(w, (HC + h) * 128, 128)
            p_, b_p = pq[pqi % 2]; pqi += 1
            gemm_fm(k, wb, b_wb, 0, hT, b_hT, p_[:], b_p)
            k.act(sg[:], p_[:], AF.Sigmoid, [b_p], [b_sg])
            k.act(sgn[:], p_[:], AF.Sigmoid, [b_p], [b_sgn], scale=-1.0)
            k.ts(sg[:], sg[:], oml[:, h:h + 1], lb[:, h:h + 1], ALU.mult, ALU.add, [b_sg, b_oml, b_lbv], [b_sg])
            k.act(lf[:], sg[:], AF.Ln, [b_sg], [b_lf])
            k.ts(sgn[:], sgn[:], oml[:, h:h + 1], None, ALU.mult, None, [b_sgn, b_oml], [b_sgn])
            k.P.op("dve", lambda e: e.tensor_tensor_scan(out=bb[:], data0=smask[:], data1=lf[:], initial=0.0,
                                                        op0=ALU.mult, op1=ALU.add),
                   reads=[b_sm, b_lf], writes=[b_bb])
            k.act(eb[:, h, :], bb[:], AF.Exp, [b_bb], [b_eb])
            k.act(enb[:], bb[:], AF.Exp, [b_bb], [b_enb], scale=-1.0)
            k.tt(qt[:, h, :], qs[:], eb[:, h, :], ALU.mult, [b_qs, b_eb], [b_qt])
            k.tt(kt[:, h, :], sgn[:], enb[:], ALU.mult, [b_sgn, b_enb], [b_kt])
        for j in range(2 * HC):
            wb, b_wb = wl.load(w, (2 * HC + j) * 128, 128)
            for c4 in range(2):
                for c in range(4):
                    cc = c4 * 4 + c
                    for kk in range(32):
                        k.mm(pv[:, c, :], hT[:, kk, cc * 64:(cc + 1) * 64], wb[:, kk, :], kk == 0, kk == 31,
                             [b_hT, b_wb], [b_pv])
                if j < HC:
                    k.act(V[:, c4 * 4:(c4 + 1) * 4, j * 128:(j + 1) * 128], pv[:], AF.Copy, [b_pv], [b_V])
                else:
                    hh = j - HC
                    k.act(gs[:], pv[:], AF.Silu, [b_pv], [b_gs])
                    for c in range(4):
                        k.tt(gw[:, c4 * 4 + c, hh * 128:(hh + 1) * 128], gs[:, c, :], onb[:, hh * 128:(hh + 1) * 128],
                             ALU.mult, [b_gs, b_onb], [b_gw])
        if tb + 1 < SEQ // TB:
            HT(tb + 1)
        for c in range(8):
            cs = slice(c * 64, (c + 1) * 64)
            y_, b_y = yt[c % 2]
            for h in range(HC):
                hs = slice(h * 128, (h + 1) * 128)
                k.mm(pAT[:, h, :], kt[:, h, cs], qt[:, h, cs], True, True, [b_kt, b_qt], [b_pAT])
                k.tt(ATs[:, h, :], pAT[:, h, :], cmask[:], ALU.mult, [b_pAT, b_cm], [b_ATs])
                k.mm(po[:, h, :], ATs[:, h, :], V[:, c, hs], True, False, [b_ATs, b_V], [b_po])
                k.mm(po[:, h, :], qt[:, h, cs], Sb[:, h, :], False, True, [b_qt, b_Sb], [b_po])
                k.tr(pkt[:, h, :], kt[:, h, cs], hm.idb[:], [b_kt, hm.b_idb], [b_pkt])
                k.act(kts[:, h, :], pkt[:, h, :], AF.Copy, [b_pkt], [b_kts])
                k.mm(pS[:, h, :], kts[:, h, :], V[:, c, hs], True, True, [b_kts, b_V], [b_pS])
                ec = eb[:, h, c * 64 + 63:c * 64 + 64]
                k.tt(t1[:, h, :], pS[:, h, :], S[:, h, :], ALU.add, [b_pS, b_S], [b_t1])
                k.ts(S[:, h, :], t1[:, h, :], ec, None, ALU.mult, None, [b_t1, b_eb], [b_S])
                k.act(Sb[:, h, :], t1[:, h, :], AF.Identity, [b_t1, b_eb], [b_Sb], scale=ec)
                k.act(junk[:], po[:, h, :], AF.Square, [b_po], [b_junk, b_st], accum_out=st[:, h:h + 1])
            k.ts(st[:], st[:], 1.0 / 128, EPS, ALU.mult, ALU.add, [b_st], [b_st])
            k.act(st[:], st[:], AF.Sqrt, [b_st], [b_st])
            k.recip(st[:], st[:], [b_st], [b_st])
            for h in range(HC):
                hs = slice(h * 128, (h + 1) * 128)
                k.stt(y_[:, hs], po[:, h, :], st[:, h:h + 1], gw[:, c, hs], ALU.mult, ALU.mult,
                      [b_po, b_st, b_gw], [b_y])
            k.dma(yo[t0 + c * 64:t0 + (c + 1) * 64, :], y_[:], [b_y], [], q="actq")
    return k.finish()


_CMASK = np.triu(np.ones((64, 64), np.float32))
_SMASK = np.ones((128, TB), np.float32)
_SMASK[:, ::CH] = 0.0


def run_1c(layer, x, modv_l, w_in_l, hgrn_lb, hgrn_onorm_l):
    A_COLS, B_COLS = 1104, 5344
    c0 = A_COLS + B_COLS
    if w_in_l.dtype == np.float32:
        w_in_l = run_wc(np.asarray(w_in_l))
    nc = _get_nc(('1c', layer), lambda: build_1c(layer))
    maps = []
    for ci in range(8):
        b, g = ci // 4, ci % 4
        hs = slice(g * 384, (g + 1) * 384)
        wc = w_in_l[:, c0:]
        wsl = np.ascontiguousarray(np.concatenate([wc[:, j * 1536:(j + 1) * 1536][:, hs] for j in range(4)], axis=1))
        mr = np.ascontiguousarray(modv_l[b])
        mcm = np.ascontiguousarray(mr.reshape(6, 32, 128).transpose(0, 2, 1))
        lbr = np.ascontiguousarray(hgrn_lb[:, hs].reshape(2, 3, 128).transpose(2, 1, 0))
        maps.append({"x": np.ascontiguousarray(x[b]), "modc": mcm, "identf": _IDENT, "w": wsl,
                     "lbraw": lbr, "onorm": np.ascontiguousarray(hgrn_onorm_l[hs]),
                     "cmask": _CMASK, "smask": _SMASK})
    res = run_bass_kernel_spmd(nc, maps, core_ids=list(range(8)))
    out = np.empty((2, 8192, 1536), ml_dtypes.bfloat16)
    for ci in range(8):
        b, g = ci // 4, ci % 4
        out[b, :, g * 384:(g + 1) * 384] = res.results[ci]["yc"]
    return out


TBB = 256
GN_EPS = 64e-5


def build_1b(seq=SEQ):
    HB = 6
    NCH = TBB // CH
    k = K()
    x = k.din("x", [seq, D])
    modc = k.din("modc", [6, 128, 32])
    idf_d = k.din("identf", [128, 128])
    wrkv = k.din("wrkv", [D, 3 * HB * 64], BF16)
    wlo = k.din("wlo", [D, 736], BF16)
    mu_rkv_d = k.din("mu_rkv", [64, 3 * HB])
    mu_wa_d = k.din("mu_wa", [128, 2])
    mu_g_d = k.din("mu_g", [120, 4])
    hp_d = k.din("hp", [64, 5, HB])
    wup_d = k.din("wup", [128, HB * 64])
    aup_d = k.din("aup", [128, HB * 64])
    gup_d = k.din("gup", [120, 4, HB * 64])
    lnw_d = k.din("lnw", [HB * 64])
    lnb_d = k.din("lnb", [HB * 64])
    smask_d = k.din("smask", [64, HB, TBB])
    m5_d = k.din("m5", [64, 5, 64])
    yo = k.dout("yb", [seq, HB * 64], BF16)

    wl = WLoader(k, 128, 2, nst=0)
    hm = HTMaker(k, modc, idf_d, inplace=True,
                 junk=(wl.bf[0][0][:].rearrange("p a b -> p (a b)"), wl.bf[0][1]))
    hT, b_hT = k.sb([128, 32, TBB], BF16)
    smask, b_sm = k.sb([64, HB, TBB])
    m5, b_m5 = k.sb([64, 5, 64])
    mu_rkv, b_mur = k.sb([64, 3 * HB])
    mu_wa, b_muw = k.sb([128, 2])
    mu_g, b_mug = k.sb([120, 4])
    hp, b_hp = k.sb([64, 5, HB])
    wup, b_wup = k.sb([128, HB * 64], BF16)
    aup, b_aup = k.sb([128, HB * 64], BF16)
    gup, b_gup = k.sb([120, 4, HB * 64], BF16)
    lnw, b_lnw = k.sb([64, HB * 64])
    lnb, b_lnb = k.sb([64, HB * 64])
    ones, b_ones = k.sb([64, 64])
    stgf, b_stg = hm.tx, hm.b_tx
    for dst, src, bd in ((smask, smask_d, b_sm), (m5, m5_d, b_m5), (mu_rkv, mu_rkv_d, b_mur), (mu_wa, mu_wa_d, b_muw),
                         (mu_g, mu_g_d, b_mug), (hp, hp_d, b_hp)):
        k.dma(dst[:], src, [], [bd], q="actq")
    k.dma(lnw[:], lnw_d.partition_broadcast(64), [], [b_lnw], q="actq")
    k.dma(lnb[:], lnb_d.partition_broadcast(64), [], [b_lnb], q="actq")
    k.dma(stgf[:, 0:384], wup_d, [], [b_stg])
    k.cp(wup[:], stgf[:, 0:384], [b_stg], [b_wup])
    k.dma(stgf[:, 0:384], aup_d, [], [b_stg])
    k.cp(aup[:], stgf[:, 0:384], [b_stg], [b_aup])
    k.dma(stgf[0:120, 0:1536], gup_d.rearrange("p a b -> p (a b)"), [], [b_stg])
    k.cp(gup[:].rearrange("p a b -> p (a b)"), stgf[0:120, 0:1536], [b_stg], [b_gup])
    k.memset(ones[:], 1.0, [b_ones])

    pq, b_pq = k.ps([128, 512])
    PA, b_PA = k.ps([64, 32, 64])
    PN, b_PN = k.ps([64, 16, 64])
    PAf = PA[:].rearrange("p a b -> p (a b)")
    PNf = PN[:].rearrange("p a b -> p (a b)")

    R_, b_R = k.sb([128, TBB + 1])
    carry, b_carry = k.sb([128, 3 * HB + 6])
    rTs = [k.sb([64, HB, TBB]) for _ in range(2)]
    kTs = [k.sb([64, HB, TBB]) for _ in range(2)]
    At, b_At = k.sb([64, HB, TBB])
    Bt, b_Bt = k.sb([64, HB, TBB])
    ebC, b_ebC = k.sb([64, HB, NCH])
    lsh, b_lsh = k.sb([128, TBB])
    tw, b_tw = k.sb([128, TBB], BF16)
    ta, b_ta = k.sb([128, TBB], BF16)
    tg, b_tg = k.sb([120, 4, TBB], BF16)
    tmp = [k.sb([64, HB, TBB]) for _ in range(6)]
    Vt, b_Vt = k.sb([64, NCH, HB, 64])
    gt, b_gt = k.sb([64, NCH, HB * 64])
    rk, b_rk = k.sb([64, HB, NCH])
    H, b_H = k.sb([64, HB, 64])
    mats, b_mats = k.sb([64, HB, 5, 64])
    _avf = tmp[1][0][:].rearrange("p a b -> p (a b)")
    _kkf = tmp[2][0][:].rearrange("p a b -> p (a b)")
    b_N = b_X = b_W = tmp[1][1]
    b_U = b_BKt = b_ysb = tmp[2][1]
    Nsb = _avf[:, 0:768].rearrange("p (a b c) -> p a b c", a=HB, b=2)
    X = _avf[:, 768:1152].rearrange("p (a c) -> p a c", a=HB)
    W = _avf[:, 1152:1536].rearrange("p (a c) -> p a c", a=HB)
    BKt = _kkf[:, 0:768].rearrange("p (a b c) -> p a b c", a=HB, b=2)
    U = _kkf[:, 768:1152].rearrange("p (a c) -> p a c", a=HB)
    ysb = _kkf[:, 1152:1536].rearrange("p (a c) -> p a c", a=HB)
    vT, b_vT = tmp[5]
    t2, b_t2 = k.sb([64, HB, 64])
    stt_, b_stt = k.sb([64, HB, 8])
    mv, b_mv = k.sb([64, HB, 2])
    yt = [k.sb([64, HB * 64], BF16) for _ in range(2)]
    k.memset(carry[:], 0.0, [b_carry])
    k.memset(H[:], 0.0, [b_H])
    idf, b_idf = hm.idf, hm.b_idf
    i64 = idf[0:64, 0:64]
    FL = "p a b -> p (a b)"

    def bc(ap3):
        return ap3.to_broadcast([64, HB, TBB])

    def shifted(M, col, mu_ap, b_mu, dst, b_dst):
        k.act(R_[0:M, 1:TBB + 1], pq[0:M, 0:TBB], AF.Copy, [b_pq], [b_R])
        k.cp(R_[0:M, 0:1], carry[0:M, col:col + 1], [b_carry], [b_R])
        k.cp(carry[0:M, col:col + 1], R_[0:M, TBB:TBB + 1], [b_R], [b_carry])
        k.tt(lsh[0:M, :], R_[0:M, 0:TBB], R_[0:M, 1:TBB + 1], ALU.subtract, [b_R], [b_lsh])
        k.stt(dst, lsh[0:M, :], mu_ap, R_[0:M, 1:TBB + 1], ALU.mult, ALU.add, [b_lsh, b_mu, b_R], [b_dst])

    def gemm(wb, b_wb, off, M):
        for kk in range(32):
            k.mm(pq[0:M, 0:TBB], wb[:, kk, off:off + M], hT[:, kk, :], kk == 0, kk == 31, [b_wb, b_hT], [b_pq])

    (sgz, b_sgz), (av, b_av), (kkv, b_kkv), (bv, b_bv), (e1, b_e1), (tq, b_tq) = tmp
    yi = [0]
    nblk = seq // TBB

    def HT(tb):
        t0 = tb * TBB
        for mt in range(TBB // 128):
            hm.emit(x[t0 + mt * 128:t0 + (mt + 1) * 128, :],
                    lambda q, mt=mt: hT[:, q, mt * 128:(mt + 1) * 128], b_hT)

    def gemm(wb, b_wb, off, M):
        for kk in range(32):
            k.mm(pq[0:M, 0:TBB], wb[:, kk, off:off + M], hT[:, kk, :], kk == 0, kk == 31, [b_wb, b_hT], [b_pq])
            if kk % 16 == 15:
                yield

    def gemm_gen(tb):
        rT, b_rT = rTs[tb % 2]
        kT, b_kT = kTs[tb % 2]
        for j in range(3 * HB // 2):
            wb, b_wb = wl.load(wrkv, j * 128, 128)
            for s_ in range(2):
                col = j * 2 + s_
                which, h = col // HB, col % HB
                dstt, b_d = ((rT, b_rT), (kT, b_kT), (vT, b_vT))[which]
                yield from gemm(wb, b_wb, s_ * 64, 64)
                shifted(64, col, mu_rkv[:, col:col + 1], b_mur, dstt[:, h, :], b_d)
        wb, b_wb = wl.load(wlo, 0, 128)
        yield from gemm(wb, b_wb, 0, 128)
        shifted(128, 3 * HB + 0, mu_wa[:, 0:1], b_muw, lsh[:, :], b_lsh)
        k.act(tw[:], lsh[:], AF.Tanh, [b_lsh], [b_tw])
        wb, b_wb = wl.load(wlo, 128, 128)
        yield from gemm(wb, b_wb, 0, 128)
        shifted(128, 3 * HB + 1, mu_wa[:, 1:2], b_muw, lsh[:, :], b_lsh)
        k.act(ta[:], lsh[:], AF.Copy, [b_lsh], [b_ta])
        for j in range(4):
            wb, b_wb = wl.load(wlo, 256 + j * 120, 120)
            yield from gemm(wb, b_wb, 0, 120)
            shifted(120, 3 * HB + 2 + j, mu_g[:, j:j + 1], b_mug, lsh[0:120, :], b_lsh)
            k.act(tg[:, j, :], lsh[0:120, :], AF.Sigmoid, [b_lsh], [b_tg])

    def prep(tb):
        rT, b_rT = rTs[tb % 2]
        kT, b_kT = kTs[tb % 2]
        for r3 in range(2):
            for hl in range(3):
                for c in range(NCH):
                    k.tr(PN[:, hl * NCH + c, :], vT[:, r3 * 3 + hl, c * CH:(c + 1) * CH], i64, [b_vT, b_idf], [b_PN])
            k.cp(Vt[:, :, r3 * 3:(r3 + 1) * 3, :].rearrange("p c h v -> p h c v"),
                 PN[:, 0:3 * NCH, :].rearrange("p (h c) v -> p h c v", h=3), [b_PN], [b_Vt])
        for h in range(HB):
            hs = slice(h * 64, (h + 1) * 64)
            k.mm(pq[0:64, 0:TBB], wup[:, hs], tw[:], True, True, [b_wup, b_tw], [b_pq])
            k.act(sgz[:, h, :], pq[0:64, 0:TBB], AF.Sigmoid, [b_pq, b_hp], [b_sgz], bias=hp[:, 0, h:h + 1])
        for h in range(HB):
            hs = slice(h * 64, (h + 1) * 64)
            k.mm(pq[0:64, 0:TBB], aup[:, hs], ta[:], True, True, [b_aup, b_ta], [b_pq])
            k.act(av[:, h, :], pq[0:64, 0:TBB], AF.Sigmoid, [b_pq, b_hp], [b_av], bias=hp[:, 1, h:h + 1])
        k.ts(sgz[:], sgz[:], -float(np.exp(-0.5)), None, ALU.mult, None, [b_sgz], [b_sgz])
        k.P.op("dve", lambda e: e.tensor_tensor_scan(out=bv[:].rearrange(FL), data0=smask[:].rearrange(FL),
                                                    data1=sgz[:].rearrange(FL), initial=0.0,
                                                    op0=ALU.mult, op1=ALU.add),
               reads=[b_sm, b_sgz], writes=[b_bv])
        k.act(e1[:], bv[:], AF.Exp, [b_bv], [b_e1])
        k.cp(ebC[:], e1[:].rearrange("p h (c s) -> p h c s", s=CH)[:, :, :, CH - 1], [b_e1], [b_ebC])
        k.tt(rT[:], rT[:], e1[:], ALU.mult, [b_rT, b_e1], [b_rT])
        k.tt(sgz[:], bv[:], sgz[:], ALU.subtract, [b_bv, b_sgz], [b_sgz])
        k.act(sgz[:], sgz[:], AF.Exp, [b_sgz], [b_sgz])
        k.act(bv[:], bv[:], AF.Exp, [b_bv], [b_bv], scale=-1.0)
        k.tt(kkv[:], kT[:], bc(hp[:, 2, :, None]), ALU.mult, [b_kT, b_hp], [b_kkv])
        k.tt(tq[:], kkv[:], kkv[:], ALU.mult, [b_kkv], [b_tq])
        for h2 in range(HB // 2):
            k.mm(pq[0:64, :], ones[:], tq[:, 2 * h2:2 * h2 + 2, :].rearrange(FL), True, True, [b_ones, b_tq], [b_pq])
            k.ts(e1[:, 2 * h2:2 * h2 + 2, :].rearrange(FL), pq[0:64, :], 1e-24, None, ALU.max, None, [b_pq], [b_e1])
        k.act(e1[:], e1[:], AF.Sqrt, [b_e1], [b_e1])
        k.recip(e1[:], e1[:], [b_e1], [b_e1])
        k.tt(kkv[:], kkv[:], e1[:], ALU.mult, [b_kkv, b_e1], [b_kkv])
        k.stt(At[:], kkv[:], -1.0, sgz[:], ALU.mult, ALU.mult, [b_kkv, b_sgz], [b_At])
        k.tt(tq[:], kkv[:], av[:], ALU.mult, [b_kkv, b_av], [b_tq])
        k.tt(Bt[:], tq[:], bv[:], ALU.mult, [b_tq, b_bv], [b_Bt])
        k.stt(tq[:], av[:], -1.0, bc(hp[:, 3, :, None]), ALU.add, ALU.mult, [b_av, b_hp], [b_tq])
        k.stt(kT[:], tq[:], 1.0, kT[:], ALU.add, ALU.mult, [b_tq, b_kT], [b_kT])
        k.tt(kT[:], kT[:], bv[:], ALU.mult, [b_kT, b_bv], [b_kT])
        k.tt(tq[:], rT[:], bc(hp[:, 4, :, None]), ALU.mult, [b_rT, b_hp], [b_tq])
        k.tt(tq[:], tq[:], kT[:], ALU.mult, [b_tq, b_kT], [b_tq])
        for h in range(HB):
            for c in range(NCH):
                idx = h * NCH + c
                k.mm(PNf[:, idx:idx + 1], tq[:, h, c * CH:(c + 1) * CH], ones[:, 0:1], True, True,
                     [b_tq, b_ones], [b_PN])
        k.cp(rk[:].rearrange(FL), PNf[:, 0:HB * NCH], [b_PN], [b_rk])
        for c in range(NCH):
            for j in range(4):
                k.mm(pq[0:64, 0:HB * 64], tg[:, j, c * CH:(c + 1) * CH], gup[:, j, :], j == 0, j == 3,
                     [b_tg, b_gup], [b_pq])
            k.act(gt[:, c, :], pq[0:64, 0:HB * 64], AF.Copy, [b_pq], [b_gt])

    def recurrence(tb, pump):
        t0 = tb * TBB
        rT, b_rT = rTs[tb % 2]
        kT, b_kT = kTs[tb % 2]
        for c in range(NCH):
            cs = slice(c * CH, (c + 1) * CH)
            for h in range(HB):
                k.mm(PA[:, h * 5 + 0, :], Bt[:, h, cs], At[:, h, cs], True, True, [b_Bt, b_At], [b_PA])
                k.mm(PA[:, h * 5 + 1, :], kT[:, h, cs], At[:, h, cs], True, True, [b_kT, b_At], [b_PA])
                k.mm(PA[:, h * 5 + 2, :], Bt[:, h, cs], rT[:, h, cs], True, True, [b_Bt, b_rT], [b_PA])
                k.mm(PA[:, h * 5 + 3, :], kT[:, h, cs], rT[:, h, cs], True, True, [b_kT, b_rT], [b_PA])
                k.mm(PA[:, h * 5 + 4, :], At[:, h, cs], Bt[:, h, cs], True, True, [b_Bt, b_At], [b_PA])
            k.tt(mats[:], PA[:, 0:HB * 5, :].rearrange("p (a b) c -> p a b c", a=HB),
                 m5[:, None, :, :].to_broadcast([64, HB, 5, 64]), ALU.mult, [b_PA, b_m5], [b_mats])
            pump()
            k.cp(Nsb[:, :, 0, :], mats[:, :, 4, :], [b_mats], [b_N])
            k.cp(Nsb[:, :, 1, :], mats[:, :, 0, :], [b_mats], [b_N])
            k.tt(X[:], mats[:, :, 0, :], i64[:, None, :].to_broadcast([64, HB, 64]), ALU.add, [b_mats, b_idf], [b_X])
            for step in range(5):
                for h in range(HB):
                    k.mm(PN[:, h * 2, :], Nsb[:, h, 1, :], Nsb[:, h, 0, :], True, True, [b_N], [b_PN])
                    if step < 4:
                        k.mm(PN[:, h * 2 + 1, :], Nsb[:, h, 0, :], Nsb[:, h, 1, :], True, True, [b_N], [b_PN])
                k.cp(Nsb[:].rearrange("p a b c -> p (a b) c"), PN[:, 0:2 * HB, :], [b_PN], [b_N])
                pump()
                for h in range(HB):
                    k.mm(PA[:, h, :], Nsb[:, h, 0, :], X[:, h, :], True, True, [b_N, b_X], [b_PA])
                k.tt(X[:], X[:], PA[:, 0:HB, :], ALU.add, [b_X, b_PA], [b_X])
                pump()
            for h in range(HB):
                k.mm(PA[:, 8 + h, :], At[:, h, cs], H[:, h, :], True, False, [b_At, b_H], [b_PA])
                k.mm(PA[:, 8 + h, :], mats[:, h, 1, :], Vt[:, c, h, :], False, True, [b_mats, b_Vt], [b_PA])
            k.cp(W[:], PA[:, 8:8 + HB, :], [b_PA], [b_W])
            pump()
            for h in range(HB):
                k.mm(PA[:, 8 + h, :], X[:, h, :], W[:, h, :], True, True, [b_X, b_W], [b_PA])
            k.cp(U[:], PA[:, 8:8 + HB, :], [b_PA], [b_U])
            pump()
            for h in range(HB):
                k.mm(PA[:, 16 + h, :], rT[:, h, cs], H[:, h, :], True, False, [b_rT, b_H], [b_PA])
                k.mm(PA[:, 16 + h, :], mats[:, h, 2, :], U[:, h, :], False, False, [b_mats, b_U], [b_PA])
                k.mm(PA[:, 16 + h, :], mats[:, h, 3, :], Vt[:, c, h, :], False, True, [b_mats, b_Vt], [b_PA])
            k.cp(ysb[:], PA[:, 16:16 + HB, :], [b_PA], [b_ysb])
            pump()
            for h in range(HB):
                k.tr(PN[:, h * 2, :], Bt[:, h, cs], i64, [b_Bt, b_idf], [b_PN])
                k.tr(PN[:, h * 2 + 1, :], kT[:, h, cs], i64, [b_kT, b_idf], [b_PN])
            k.cp(BKt[:].rearrange("p a b c -> p (a b) c"), PN[:, 0:2 * HB, :], [b_PN], [b_BKt])
            pump()
            for h in range(HB):
                k.mm(PA[:, 24 + h, :], BKt[:, h, 0, :], U[:, h, :], True, False, [b_BKt, b_U], [b_PA])
                k.mm(PA[:, 24 + h, :], BKt[:, h, 1, :], Vt[:, c, h, :], False, True, [b_BKt, b_Vt], [b_PA])
            k.tt(H[:], H[:], PA[:, 24:24 + HB, :], ALU.add, [b_H, b_PA], [b_H])
            k.tt(H[:], H[:], ebC[:, :, c:c + 1].to_broadcast([64, HB, 64]), ALU.mult, [b_H, b_ebC], [b_H])
            pump()
            for h in range(HB):
                k.P.op("dve", lambda e, h=h: e.bn_stats(out=stt_[:, h, 0:6], in_=ysb[:, h, :]),
                       reads=[b_ysb], writes=[b_stt])
                k.P.op("dve", lambda e, h=h: e.bn_aggr(out=mv[:, h, :], in_=stt_[:, h, 0:6]),
                       reads=[b_stt], writes=[b_mv])
            k.ts(mv[:, :, 1], mv[:, :, 1], 1.0, GN_EPS, ALU.mult, ALU.add, [b_mv], [b_mv])
            k.act(mv[:, :, 1], mv[:, :, 1], AF.Sqrt, [b_mv], [b_mv])
            k.recip(mv[:, :, 1], mv[:, :, 1], [b_mv], [b_mv])
            k.tt(t2[:], ysb[:], mv[:, :, 0:1].to_broadcast([64, HB, 64]), ALU.subtract, [b_ysb, b_mv], [b_t2])
            k.tt(t2[:], t2[:], mv[:, :, 1:2].to_broadcast([64, HB, 64]), ALU.mult, [b_t2, b_mv], [b_t2])
            t2f = t2[:].rearrange(FL)
            k.tt(t2f, t2f, lnw[:], ALU.mult, [b_t2, b_lnw], [b_t2])
            k.tt(t2f, t2f, lnb[:], ALU.add, [b_t2, b_lnb], [b_t2])
            pump()
            k.tt(W[:], Vt[:, c, :, :], rk[:, :, c:c + 1].to_broadcast([64, HB, 64]), ALU.mult, [b_Vt, b_rk], [b_W])
            k.tt(t2[:], t2[:], W[:], ALU.add, [b_t2, b_W], [b_t2])
            y_, b_y = yt[yi[0] % 2]
            yi[0] += 1
            k.tt(y_[:], t2f, gt[:, c, :], ALU.mult, [b_t2, b_gt], [b_y])
            k.dma(yo[t0 + c * CH:t0 + (c + 1) * CH, :], y_[:], [b_y], [], q="actq")

    HT(0)
    for _ in gemm_gen(0):
        pass
    for tb in range(nblk):
        g = None
        if tb + 1 < nblk:
            HT(tb + 1)
            g = gemm_gen(tb + 1)
        prep(tb)
        recurrence(tb, (lambda g=g: next(g, None)) if g is not None else (lambda: None))
        if g is not None:
            for _ in g:
                pass
    return k.finish()


_SMASKB = np.ones((64, 6, TBB), np.float32)
_SMASKB[:, :, ::CH] = 0.0
_su = np.triu(np.ones((64, 64), np.float32), 1)
_iu = np.triu(np.ones((64, 64), np.float32), 0)
_M5 = np.ascontiguousarray(np.stack([_su, _su, _iu, _iu, _su.T], axis=1))


def run_1b(inp, layer, x, modv_l, seq=SEQ, runner=None, cores=range(8), w_in_l=None):
    A_COLS = 1104
    W_B = 1536
    if w_in_l is None:
        w_in_l = run_wc(np.asarray(inp["w_in"][layer]))
    mu = np.asarray(inp["rwkv_mu"][layer], np.float32)
    nc = _get_nc(('1b', seq), lambda: build_1b(seq))
    maps = []
    fm = lambda v: np.ascontiguousarray(np.asarray(v, np.float32).reshape(6, 64).T)
    for ci in cores:
        b, g = ci // 4, ci % 4
        hs = slice(g * 384, (g + 1) * 384)
        wB = w_in_l[:, A_COLS:A_COLS + 5344]
        wrkv = np.ascontiguousarray(np.concatenate([wB[:, j * W_B:(j + 1) * W_B][:, hs] for j in range(3)], axis=1))
        wlo = np.ascontiguousarray(wB[:, 3 * W_B:])
        mu_rkv = np.concatenate([mu[j * W_B:(j + 1) * W_B][hs] for j in range(3)]).reshape(18, 64).T
        mu_l = mu[3 * W_B:]
        mu_wa = np.stack([mu_l[0:128], mu_l[128:256]], axis=1)
        mu_g = mu_l[256:].reshape(4, 120).T
        hp = np.stack([fm(inp["rwkv_w0"][layer][hs]), fm(inp["rwkv_a0"][layer][hs]), fm(inp["rwkv_k_k"][layer][hs]),
                       fm(inp["rwkv_k_a"][layer][hs]), fm(inp["rwkv_r_k"][layer].reshape(-1)[hs])], axis=1)
        mr = np.ascontiguousarray(modv_l[b])
        mcm = np.ascontiguousarray(mr.reshape(6, 32, 128).transpose(0, 2, 1))
        maps.append({"x": np.ascontiguousarray(x[b, :seq]), "modc": mcm, "identf": _IDENT, "wrkv": wrkv, "wlo": wlo,
                     "mu_rkv": np.ascontiguousarray(mu_rkv), "mu_wa": np.ascontiguousarray(mu_wa),
                     "mu_g": np.ascontiguousarray(mu_g), "hp": np.ascontiguousarray(hp),
                     "wup": np.ascontiguousarray(inp["rwkv_w_up"][layer][:, hs]),
                     "aup": np.ascontiguousarray(inp["rwkv_a_up"][layer][:, hs]),
                     "gup": np.ascontiguousarray(inp["rwkv_g_up"][layer][:, hs].reshape(4, 120, 384).transpose(1, 0, 2)),
                     "lnw": np.ascontiguousarray(inp["rwkv_lnx_w"][layer][hs]),
                     "lnb": np.ascontiguousarray(inp["rwkv_lnx_b"][layer][hs]),
                     "smask": _SMASKB, "m5": _M5})
    if runner is not None:
        return runner(nc, maps)
    res = run_bass_kernel_spmd(nc, maps, core_ids=list(range(8)))
    out = np.empty((2, 8192, 1536), ml_dtypes.bfloat16)
    for ci in range(8):
        b, g = ci // 4, ci % 4
        out[b, :, g * 384:(g + 1) * 384] = res.results[ci]["yb"]
    return out


def build_1ak(ntok=2048):
    k = K()
    x = k.din("x", [ntok, D])
    modc = k.din("modc", [6, 128, 32])
    idf_d = k.din("identf", [128, 128])
    wkv = k.din("wkv", [D, 320], BF16)
    kvg = k.din("kvg", [256])
    o_ct = k.dout("ckvnT", [128, 2, ntok], BF16)
    o_v = k.dout("V", [ntok, 256], BF16)
    o_ki = k.dout("kidxT", [64, ntok], BF16)
    hm = HTMaker(k, modc, idf_d)
    wl = WLoader(k, 128, 2)
    hT, b_hT = k.sb([128, 32, TB], BF16)
    wres, b_wres = k.sb([128, 32, 320], BF16)
    gb, b_gb = k.sb([128, 256])
    k.dma(gb[:], kvg.partition_broadcast(128), [], [b_gb], q="actq")
    for j, (c0, n) in enumerate(((0, 128), (128, 128), (256, 64))):
        wb, b_wb = wl.load(wkv, c0, n)
        k.cp(wres[:, :, c0:c0 + n], wb[:, :, 0:n], [b_wb], [b_wres], eng="pool")
    pkv = [k.ps([128, 512]) for _ in range(2)]
    pki, b_pki = k.ps([64, 512])
    ptb, b_ptb = k.ps([128, 2, 128], BF16)
    st, b_st = k.sb([128, 1])
    junk, b_junk = k.sb([128, 256])
    vt = [k.sb([128, 256], BF16) for _ in range(2)]
    ct = [k.sb([128, 2, 128], BF16) for _ in range(2)]
    kit = [k.sb([64, TB], BF16) for _ in range(2)]
    n_ = 0
    for tb in range(ntok // TB):
        t0 = tb * TB
        for mt in range(TB // 128):
            hm.emit(x[t0 + mt * 128:t0 + (mt + 1) * 128, :],
                    lambda q, mt=mt: hT[:, q, mt * 128:(mt + 1) * 128], b_hT)
        for mt in range(TB // 128):
            p_, b_p = pkv[n_ % 2]
            v_, b_v = vt[n_ % 2]
            c_, b_c = ct[n_ % 2]
            n_ += 1
            for kk in range(32):
                k.mm(p_[:, 0:256], hT[:, kk, mt * 128:(mt + 1) * 128], wres[:, kk, 0:256], kk == 0, kk == 31,
                     [b_hT, b_wres], [b_p])
            k.act(junk[:], p_[:, 0:256], AF.Square, [b_p], [b_junk, b_st], accum_out=st[:, 0:1])
            k.ts(st[:], st[:], 1.0 / 256, EPS, ALU.mult, ALU.add, [b_st], [b_st])
            k.act(st[:], st[:], AF.Sqrt, [b_st], [b_st])
            k.recip(st[:], st[:], [b_st], [b_st])
            k.stt(v_[:], p_[:, 0:256], st[:, 0:1], gb[:], ALU.mult, ALU.mult, [b_p, b_st, b_gb], [b_v])
            r0 = t0 + mt * 128
            k.dma(o_v[r0:r0 + 128, :], v_[:], [b_v], [], q="actq")
            for rc in range(2):
                k.tr(ptb[:, rc, :], v_[:, rc * 128:(rc + 1) * 128], hm.idb[:], [b_v, hm.b_idb], [b_ptb])
            k.cp(c_[:], ptb[:], [b_ptb], [b_c])
            k.dma(o_ct[:, :, r0:r0 + 128], c_[:], [b_c], [], q="actq")
        ki_, b_ki = kit[tb % 2]
        for kk in range(32):
            k.mm(pki[:], wres[:, kk, 256:320], hT[:, kk, :], kk == 0, kk == 31, [b_hT, b_wres], [b_pki])
        k.act(ki_[:], pki[:], AF.Copy, [b_pki], [b_ki])
        k.dma(o_ki[:, t0:t0 + TB], ki_[:], [b_ki], [], q="actq")
    return k.finish()


def run_1ak(layer, x, modv_l, w_in_l, kvg):
    if w_in_l.dtype == np.float32:
        w_in_l = run_wc(np.asarray(w_in_l))
    nc = _get_nc('1ak', build_1ak)
    maps = []
    wkv = np.ascontiguousarray(w_in_l[:, 768:768 + 320])
    for ci in range(8):
        b, q = ci // 4, ci % 4
        mr = np.ascontiguousarray(modv_l[b])
        mcm = np.ascontiguousarray(mr.reshape(6, 32, 128).transpose(0, 2, 1))
        maps.append({"x": np.ascontiguousarray(x[b, q * 2048:(q + 1) * 2048]), "modc": mcm, "identf": _IDENT,
                     "wkv": wkv, "kvg": np.ascontiguousarray(kvg)})
    res = run_bass_kernel_spmd(nc, maps, core_ids=list(range(8)))
    ckvnT = np.empty((2, 128, 2, 8192), ml_dtypes.bfloat16)
    V = np.empty((2, 8192, 256), ml_dtypes.bfloat16)
    kidxT = np.empty((2, 64, 8192), ml_dtypes.bfloat16)
    for ci in range(8):
        b, q = ci // 4, ci % 4
        sl = slice(q * 2048, (q + 1) * 2048)
        ckvnT[b, :, :, sl] = res.results[ci]["ckvnT"]
        V[b, sl] = res.results[ci]["V"]
        kidxT[b, :, sl] = res.results[ci]["kidxT"]
    return ckvnT, V, kidxT


NEG_SEL = -3.0e38
NEG_CAUSAL = -1.0e30
NEG_LOGIT = -30000.0


def build_1aq(nq=16, seq=SEQ):
    k = K()
    xq = k.din("xq", [nq * 128, D])
    modc = k.din("modc", [6, 128, 32])
    idf_d = k.din("identf", [128, 128])
    wq = k.din("wq", [D, 784], BF16)
    qg_d = k.din("qg", [128, 6])
    wqidx_d = k.din("wqidx", [128, 6, 1024])
    wuq_d = k.din("wuq", [128, 6, 1024])
    wukT_d = k.din("wukT", [128, 8, 256])
    wuv_d = k.din("wuv", [128, 2, 8, 128])
    rb_d = k.din("rb", [32, 8])
    ohv_d = k.din("ohv", [32, 384])
    kvg_d = k.din("kvg", [256])
    ckT_d = k.din("ckvnT", [128, 2, seq], BF16)
    V_d = k.din("V", [seq, 256], BF16)
    ki_d = k.din("kidxT", [64, seq], BF16)
    tpos_d = k.din("tposrel", [128, 1])
    iota_d = k.din("iota", [128, 512])
    sel_d = k.din("sel", [128, 10])
    J_d = k.din("J", [128, 128])
    ya = k.dout("ya", [nq * 128, 1024], BF16)
    vd = k.dscr("vd", [8, 384])
    b_vd = Buf()

    hm = HTMaker(k, modc, idf_d, inplace=True)
    wl = WLoader(k, 128, 1, nst=1)
    stg, b_stg = wl.st[0]
    stgf = stg[:].rearrange("p a b -> p (a b)")
    hqT, b_hqT = k.sb([128, 32, 128], BF16)
    qg, b_qg = k.sb([128, 6])
    wqidx, b_wqidx = k.sb([128, 6, 1024], BF16)
    wuq, b_wuq = k.sb([128, 6, 1024], BF16)
    wukT, b_wukT = k.sb([128, 8, 256], BF16)
    wuv, b_wuv = k.sb([128, 2, 8, 128], BF16)
    tpos, b_tpos = k.sb([128, 1])
    sel, b_sel = k.sb([128, 10])
    Jf, b_Jf = k.sb([128, 128])
    Jb, b_Jb = k.sb([128, 128], BF16)
    I4, b_I4 = k.sb([128, 4, 128], BF16)
    onesb, b_onesb = k.sb([128, 128], BF16)
    cb, b_cb = k.sb([128, 512])
    cbb, b_cbb = k.sb([128, 512], BF16)
    rb, b_rb = k.sb([32, 8])
    rb31, b_rb31 = k.sb([32, 8])
    ohv, b_ohv = k.sb([32, 384])
    vts, b_vts = k.sb([8, 384])
    r1, b_r1 = k.sb([1, 512])
    fac, b_fac = k.sb([1, 4])
    TBsel, b_TBsel = k.sb([128, 5, 1024], BF16)
    score, b_score = k.sb([128, seq])
    assert seq >= 3584 or True
    TBr = [(score[:, 0:1024], b_score), (score[:, 1024:2048], b_score)] if seq >= 4096 else [k.sb([128, 1024]) for _ in range(2)]
    if seq >= 4096:
        TBt, b_TBt = score[:, 2048:3072], b_score
        iota, b_iota = score[:, 3072:3584], b_score
    else:
        TBt, b_TBt = k.sb([128, 1024])
        iota, b_iota = k.sb([128, 512])

    pA, b_pA = k.ps([128, 512])
    pw = [k.ps([128, 512]) for _ in range(2)]
    pO = [k.ps([128, 512]) for _ in range(2)]
    pZ, b_pZ = k.ps([128, 512])

    selb, b_selb = k.sb([128, seq], BF16)
    cqs, b_cqs = k.sb([128, 784])
    widx, b_widx = k.sb([128, 16])
    cqT, b_cqT = k.sb([128, 6, 128], BF16)
    qiT, b_qiT = k.sb([64, 16, 128], BF16)
    qT, b_qT = k.sb([128, 8, 128], BF16)
    qlT, b_qlT = k.sb([128, 2, 8, 128], BF16)
    sq, b_sq = k.sb([128, 2, 1024], BF16)
    srow, b_srow = k.sb([1, 1024])
    nsr, b_nsr = k.sb([1, 1024], BF16)
    st, b_st = k.sb([128, 1])
    m8, b_m8 = k.sb([128, 8])
    kit = [k.sb([64, 512], BF16) for _ in range(2)]
    ckt = [k.sb([128, 2, 512], BF16) for _ in range(2)]
    Vt = [k.sb([128, 4, 256], BF16) for _ in range(2)]
    rl = [k.sb([128, 512]) for _ in range(2)]
    PT = [k.sb([128, 512], BF16) for _ in range(2)]
    rZ, b_rZ = k.sb([128, 512])
    olT, b_olT = k.sb([128, 2, 512], BF16)
    yat, b_yat = k.sb([128, 1024], BF16)

    for dst, src, bd in ((qg, qg_d, b_qg), (tpos, tpos_d, b_tpos), (sel, sel_d, b_sel),
                         (Jf, J_d, b_Jf), (rb, rb_d, b_rb), (ohv, ohv_d, b_ohv)):
        k.dma(dst[:], src, [], [bd], q="actq")
    k.dma(iota[:] if seq < 4096 else iota, iota_d, [], [b_iota], q="actq")
    k.dma(rb31[:], rb_d[31, :].partition_broadcast(32), [], [b_rb31], q="actq")
    k.cp(Jb[:], Jf[:], [b_Jf], [b_Jb])
    for c in range(4):
        k.cp(I4[:, c, :], hm.idf[:], [hm.b_idf], [b_I4])
    k.memset(onesb[:], 1.0, [b_onesb])
    for half in range(2):
        k.dma(stgf[:, 0:3072], wqidx_d[:, half * 3:(half + 1) * 3, :].rearrange("p a b -> p (a b)"), [], [b_stg])
        k.cp(wqidx[:, half * 3:(half + 1) * 3, :].rearrange("p a b -> p (a b)"), stgf[:, 0:3072], [b_stg], [b_wqidx], eng="pool")
    for half in range(2):
        k.dma(stgf[:, 0:3072], wuq_d[:, half * 3:(half + 1) * 3, :].rearrange("p a b -> p (a b)"), [], [b_stg])
        k.cp(wuq[:, half * 3:(half + 1) * 3, :].rearrange("p a b -> p (a b)"), stgf[:, 0:3072], [b_stg], [b_wuq], eng="pool")
    k.dma(stgf[:, 0:2048], wukT_d.rearrange("p a b -> p (a b)"), [], [b_stg])
    k.cp(wukT[:].rearrange("p a b -> p (a b)"), stgf[:, 0:2048], [b_stg], [b_wukT], eng="pool")
    k.dma(stgf[:, 0:2048], wuv_d.rearrange("p a b c -> p (a b c)"), [], [b_stg])
    k.cp(wuv[:].rearrange("p a b c -> p (a b c)"), stgf[:, 0:2048], [b_stg], [b_wuv], eng="pool")
    k.ts(cb[:], iota[:] if seq < 4096 else iota, tpos[:, 0:1], NEG_CAUSAL, ALU.is_gt, ALU.mult, [b_iota, b_tpos], [b_cb])
    k.ts(cbb[:], iota[:] if seq < 4096 else iota, tpos[:, 0:1], NEG_LOGIT, ALU.is_gt, ALU.mult, [b_iota, b_tpos], [b_cbb])
    k.dma(r1[:, 0:256], kvg_d[None, :], [], [b_r1], q="actq")
    k.P.op("dve", lambda e: e.tensor_reduce(out=fac[:, 0:1], in_=r1[:, 0:256], axis=AX.X, op=ALU.max,
                                            apply_absolute_value=True), reads=[b_r1], writes=[b_fac])
    k.dma(r1[:, 256:512], rb_d.rearrange("a b -> (a b)")[None, :], [], [b_r1], q="actq")
    k.P.op("dve", lambda e: e.tensor_reduce(out=fac[:, 1:2], in_=r1[:, 256:512], axis=AX.X, op=ALU.max,
                                            apply_absolute_value=True), reads=[b_r1], writes=[b_fac])
    k.ts(fac[:, 2:3], fac[:, 0:1], -16.0, None, ALU.mult, None, [b_fac], [b_fac])
    k.ts(fac[:, 3:4], fac[:, 1:2], -2.0, None, ALU.mult, None, [b_fac], [b_fac])
    k.tt(rb[:], rb[:], rb31[:], ALU.subtract, [b_rb, b_rb31], [b_rb])
    k.mm(pA[0:8, 0:384], rb[:], ohv[:], True, True, [b_rb, b_ohv], [b_pA])
    k.cp(vts[:], pA[0:8, 0:384], [b_pA], [b_vts])
    k.dma(vd, vts[:], [b_vts], [b_vd])
    for dl in range(2):
        t_, b_t = TBr[dl]
        src = bass.AP(tensor=vd.tensor, offset=128 * dl + 1, ap=[[1, 128], [384, 8], [1, 128]])
        k.dma(t_.rearrange("p (a b) -> p a b", a=8) if seq >= 4096 else t_[:].rearrange("p (a b) -> p a b", a=8), src, [b_vd], [b_t])
    for j in range(5):
        k.ts(TBt[:] if seq < 4096 else TBt, TBr[0][0][:] if seq < 4096 else TBr[0][0], sel[:, 2 * j:2 * j + 1], None, ALU.mult, None, [TBr[0][1], b_sel], [b_TBt])
        k.stt(TBsel[:, j, :], TBr[1][0][:] if seq < 4096 else TBr[1][0], sel[:, 2 * j + 1:2 * j + 2], TBt[:] if seq < 4096 else TBt, ALU.mult, ALU.add,
              [TBr[1][1], b_sel, b_TBt], [b_TBsel])

    inv_sqrt_d = float(128 ** -0.5)
    for qi in range(nq):
        Lk = 512 * (qi + 1)
        nkb = 4 * (qi + 1)
        hm.emit(xq[qi * 128:(qi + 1) * 128, :], lambda q: hqT[:, q, :], b_hqT)
        for j in range(7):
            n = 128 if j < 6 else 16
            wb, b_wb = wl.load(wq, j * 128, n)
            for kk in range(32):
                k.mm(pA[:, 0:n], hqT[:, kk, :], wb[:, kk, 0:n], kk == 0, kk == 31, [b_hqT, b_wb], [b_pA])
            k.act(cqs[:, j * 128:j * 128 + n], pA[:, 0:n], AF.Copy, [b_pA], [b_cqs])
        k.act(hm.junk[:, 0:768], cqs[:, 0:768], AF.Square, [b_cqs], [hm.b_junk, b_st], accum_out=st[:, 0:1])
        k.ts(st[:], st[:], 1.0 / 768, EPS, ALU.mult, ALU.add, [b_st], [b_st])
        k.act(st[:], st[:], AF.Sqrt, [b_st], [b_st])
        k.recip(st[:], st[:], [b_st], [b_st])
        k.ts(widx[:], cqs[:, 768:784], 1.0 / 32.0, None, ALU.mult, None, [b_cqs], [b_widx])
        k.ts(cqs[:, 0:768], cqs[:, 0:768], st[:, 0:1], None, ALU.mult, None, [b_cqs, b_st], [b_cqs])
        for c0, nchunk in ((0, 4), (4, 2)):
            for c in range(nchunk):
                k.tr(hm.ptf[:, c, :], cqs[:, (c0 + c) * 128:(c0 + c + 1) * 128], hm.idf[:], [b_cqs, hm.b_idf], [hm.b_ptf])
            for c in range(nchunk):
                k.act(cqT[:, c0 + c, :], hm.ptf[:, c, :], AF.Identity, [hm.b_ptf, b_qg], [b_cqT],
                      scale=qg[:, c0 + c:c0 + c + 1])
        for h4 in range(4):
            for hl in range(4):
                h = h4 * 4 + hl
                for rc in range(6):
                    k.mm(pA[0:64, hl * 128:(hl + 1) * 128], wqidx[:, rc, h * 64:(h + 1) * 64], cqT[:, rc, :],
                         rc == 0, rc == 5, [b_wqidx, b_cqT], [b_pA])
            k.act(qiT[:, h4 * 4:(h4 + 1) * 4, :].rearrange("p a b -> p (a b)"), pA[0:64, :], AF.Copy, [b_pA], [b_qiT])
        for h4 in range(2):
            for hl in range(4):
                h = h4 * 4 + hl
                for rc in range(6):
                    k.mm(pA[:, hl * 128:(hl + 1) * 128], wuq[:, rc, h * 128:(h + 1) * 128], cqT[:, rc, :],
                         rc == 0, rc == 5, [b_wuq, b_cqT], [b_pA])
            k.act(qT[:, h4 * 4:(h4 + 1) * 4, :].rearrange("p a b -> p (a b)"), pA[:], AF.Copy, [b_pA], [b_qT])
        for rc2 in range(2):
            for h4 in range(2):
                for hl in range(4):
                    h = h4 * 4 + hl
                    k.mm(pA[:, hl * 128:(hl + 1) * 128], wukT[:, h, rc2 * 128:(rc2 + 1) * 128], qT[:, h, :],
                         True, True, [b_wukT, b_qT], [b_pA])
                k.act(qlT[:, rc2, h4 * 4:(h4 + 1) * 4, :].rearrange("p a b -> p (a b)"), pA[:], AF.Copy,
                      [b_pA], [b_qlT], scale=inv_sqrt_d)
        qlf = qlT[:].rearrange("p a b c -> p a (b c)")
        k.tt(sq[:], qlf, qlf, ALU.mult, [b_qlT], [b_sq])
        for hp in range(2):
            for rc2 in range(2):
                k.mm(pA[:], onesb[:], sq[:, rc2, hp * 512:(hp + 1) * 512], rc2 == 0, rc2 == 1, [b_onesb, b_sq], [b_pA])
            k.act(srow[:, hp * 512:(hp + 1) * 512], pA[0:1, :], AF.Sqrt, [b_pA], [b_srow])
        k.ts(nsr[:], srow[:], fac[:, 2:3], fac[:, 3:4], ALU.mult, ALU.add, [b_srow, b_fac], [b_nsr])
        ih = 0
        for kt in range(qi + 1):
            ki_, b_ki = kit[kt % 2]
            k.dma(ki_[:], ki_d[:, kt * 512:(kt + 1) * 512], [], [b_ki])
            cs = slice(kt * 512, (kt + 1) * 512)
            for h in range(16):
                p_, b_p = pw[ih % 2]
                r_, b_r = rl[ih % 2]
                ih += 1
                k.mm(p_[:], qiT[:, h, :], ki_[:], True, True, [b_qiT, b_ki], [b_p])
                k.act(r_[:], p_[:], AF.Relu, [b_p], [b_r])
                if h == 0:
                    k.ts(score[:, cs], r_[:], widx[:, 0:1], None, ALU.mult, None, [b_r, b_widx], [b_score])
                else:
                    k.stt(score[:, cs], r_[:], widx[:, h:h + 1], score[:, cs], ALU.mult, ALU.add,
                          [b_r, b_widx, b_score], [b_score])
        last = slice(Lk - 512, Lk)
        k.tt(score[:, last], score[:, last], cb[:], ALU.add, [b_score, b_cb], [b_score])
        for r in range(32):
            k.P.op("dve", lambda e, Lk=Lk: e.max(out=m8[:], in_=score[:, 0:Lk]), reads=[b_score], writes=[b_m8])
            k.P.op("dve", lambda e, Lk=Lk: e.match_replace(out=score[:, 0:Lk], in_to_replace=m8[:],
                                                          in_values=score[:, 0:Lk], imm_value=NEG_SEL),
                   reads=[b_score, b_m8], writes=[b_score])
        k.ts(selb[:, 0:Lk], score[:, 0:Lk], -2.0e38, NEG_LOGIT, ALU.is_gt, ALU.mult, [b_score], [b_selb])
        k.tt(selb[:, last], selb[:, last], cbb[:], ALU.add, [b_selb, b_cbb], [b_selb])
        it = 0
        for hp in range(2):
            qlh = [qlT[:, rc2, hp * 4:(hp + 1) * 4, :].rearrange("p a b -> p (a b)") for rc2 in range(2)]
            for kt in range(qi + 1):
                c_, b_c = ckt[it % 2]
                v_, b_v = Vt[it % 2]
                it += 1
                k.dma(c_[:], ckT_d[:, :, kt * 512:(kt + 1) * 512], [], [b_c])
                k.dma(v_[:], V_d[kt * 512:(kt + 1) * 512, :].rearrange("(c p) r -> p c r", p=128), [], [b_v])
                for kb4 in range(4):
                    kb = kt * 4 + kb4
                    pl, b_pl = pw[kb % 2]
                    pt_, b_pt = PT[kb % 2]
                    ks = slice(kb4 * 128, (kb4 + 1) * 128)
                    k.mm(pl[:], c_[:, 0, ks], qlh[0], True, False, [b_c, b_qlT], [b_pl])
                    k.mm(pl[:], c_[:, 1, ks], qlh[1], False, False, [b_c, b_qlT], [b_pl])
                    k.mm(pl[:], selb[:, kb * 128:(kb + 1) * 128], I4[:].rearrange("p a b -> p (a b)"), False, False,
                         [b_selb, b_I4], [b_pl])
                    j = kb - (4 * qi - 1)
                    if 0 <= j <= 4:
                        k.mm(pl[:], Jb[:], TBsel[:, j, hp * 512:(hp + 1) * 512], False, False, [b_Jb, b_TBsel], [b_pl])
                    k.mm(pl[:], onesb[0:1, :], nsr[0:1, hp * 512:(hp + 1) * 512], False, True, [b_onesb, b_nsr], [b_pl])
                    k.act(pt_[:], pl[:], AF.Exp, [b_pl], [b_pt])
                    first, lastkb = kb == 0, kb == nkb - 1
                    k.mm(pO[0][0][:], v_[:, kb4, 0:128], pt_[:], first, lastkb, [b_v, b_pt], [pO[0][1]])
                    k.mm(pO[1][0][:], v_[:, kb4, 128:256], pt_[:], first, lastkb, [b_v, b_pt], [pO[1][1]])
                    k.mm(pZ[:], onesb[:], pt_[:], first, lastkb, [b_onesb, b_pt], [b_pZ])
            k.recip(rZ[:], pZ[:], [b_pZ], [b_rZ])
            for rc in range(2):
                k.tt(olT[:, rc, :], pO[rc][0][:], rZ[:], ALU.mult, [pO[rc][1], b_rZ], [b_olT])
            for hl in range(4):
                h = hp * 4 + hl
                for rc in range(2):
                    k.mm(pA[:, hl * 128:(hl + 1) * 128], olT[:, rc, hl * 128:(hl + 1) * 128], wuv[:, rc, h, :],
                         rc == 0, rc == 1, [b_olT, b_wuv], [b_pA])
            k.act(yat[:, hp * 512:(hp + 1) * 512], pA[:], AF.Copy, [b_pA], [b_yat])
        k.dma(ya[qi * 128:(qi + 1) * 128, :], yat[:], [b_yat], [], q="actq")
    return k.finish()


def _rel_bucket_np(n):
    n = np.asarray(n, np.int32)
    nf = np.maximum(n, 1).astype(np.float32)
    large = 16 + (np.log(nf / np.float32(16)) / np.float32(np.log(128 / 16)) * np.float32(16)).astype(np.int32)
    large = np.minimum(large, 31)
    return np.where(n < 16, n, large)


_OHV = np.zeros((32, 384), np.float32)
for _n in range(256):
    _OHV[int(_rel_bucket_np(_n)), _n + 128] = 1.0
_J = np.ascontiguousarray(np.eye(128, dtype=np.float32)[::-1])
_IOTA = np.ascontiguousarray(np.broadcast_to(np.arange(512, dtype=np.float32)[None], (128, 512)))


def run_1aq(inp, layer, x, modv_l, ckvnT, V, kidxT, nq=16, runner=None, cores=range(8), w_in_l=None):
    l = layer
    seq = nq * 512
    if w_in_l is None:
        w_in_l = run_wc(np.asarray(inp["w_in"][l]))
    nc = _get_nc(('1aq', nq, seq), lambda: build_1aq(nq, seq))
    wq = np.ascontiguousarray(np.concatenate([w_in_l[:, 0:768], w_in_l[:, 1088:1104]], axis=1))
    qg = np.ascontiguousarray(np.asarray(inp["mla_q_norm"][l], np.float32).reshape(6, 128).T)
    wqidx = np.ascontiguousarray(np.asarray(inp["w_qidx"][l]).reshape(6, 128, 1024).transpose(1, 0, 2))
    wuq = np.ascontiguousarray(np.asarray(inp["w_uq"][l]).reshape(6, 128, 1024).transpose(1, 0, 2))
    wukT = np.ascontiguousarray(np.asarray(inp["w_uk"][l]).transpose(2, 0, 1))
    wuv = np.ascontiguousarray(np.asarray(inp["w_uv"][l]).reshape(8, 2, 128, 128).transpose(2, 1, 0, 3))
    maps = []
    for ci in cores:
        b, g = ci // 4, ci % 4
        mr = np.ascontiguousarray(modv_l[b])
        mcm = np.ascontiguousarray(mr.reshape(6, 32, 128).transpose(0, 2, 1))
        xq = np.ascontiguousarray(x[b].reshape(64, 128, D)[g::4][:nq].reshape(nq * 128, D))
        sel = np.zeros((128, 10), np.float32)
        for j in range(5):
            dl = g + 1 - j
            if dl in (0, 1):
                sel[:, 2 * j + dl] = 1.0
        tpos = (g * 128 + np.arange(128, dtype=np.float32))[:, None]
        maps.append({"xq": xq, "modc": mcm, "identf": _IDENT, "wq": wq, "qg": qg, "wqidx": wqidx, "wuq": wuq,
                     "wukT": wukT, "wuv": wuv, "rb": np.ascontiguousarray(inp["rel_bias"]), "ohv": _OHV,
                     "kvg": np.ascontiguousarray(inp["mla_kv_norm"][l]),
                     "ckvnT": np.ascontiguousarray(ckvnT[b][:, :, :seq]), "V": np.ascontiguousarray(V[b][:seq]),
                     "kidxT": np.ascontiguousarray(kidxT[b][:, :seq]),
                     "tposrel": np.ascontiguousarray(tpos), "iota": _IOTA, "sel": sel, "J": _J})
    if runner is not None:
        return runner(nc, maps)
    res = run_bass_kernel_spmd(nc, maps, core_ids=list(range(8)))
    out = np.empty((2, 64, 128, 1024), ml_dtypes.bfloat16)
    for ci in range(8):
        b, g = ci // 4, ci % 4
        out[b, g::4] = res.results[ci]["ya"].reshape(16, 128, 1024)
    return out.reshape(2, 8192, 1024)


def build_wc(rows, cols):
    k = K()
    w = k.din("w", [rows, cols])
    o = k.dout("wb", [rows, cols], BF16)
    CW = max(c for c in range(1, 2049) if cols % c == 0)
    st = [k.sb([128, 2048]) for _ in range(3)]
    bf = [k.sb([128, 2048], BF16) for _ in range(3)]
    engs = ["pool", "dve", "act"]
    i = 0
    for r in range(rows // 128):
        for c in range(cols // CW):
            s_, b_s = st[i % 3]
            o_, b_o = bf[i % 3]
            k.dma(s_[:, 0:CW], w[r * 128:(r + 1) * 128, c * CW:(c + 1) * CW], [], [b_s])
            e = engs[i % 3]
            if e == "act":
                k.act(o_[:, 0:CW], s_[:, 0:CW], AF.Copy, [b_s], [b_o])
            else:
                k.cp(o_[:, 0:CW], s_[:, 0:CW], [b_s], [b_o], eng=e)
            k.dma(o[r * 128:(r + 1) * 128, c * CW:(c + 1) * CW], o_[:, 0:CW], [b_o], [], q="actq")
            i += 1
    return k.finish()


_NC_CACHE = {}


def _get_nc(key, fn):
    if key not in _NC_CACHE:
        _NC_CACHE[key] = fn()
    return _NC_CACHE[key]


def run_wc(w):
    R, C = w.shape
    rs = R // 8
    nc = _get_nc(("wc", rs, C), lambda: build_wc(rs, C))
    maps = [{"w": np.ascontiguousarray(w[ci * rs:(ci + 1) * rs])} for ci in range(8)]
    res = run_bass_kernel_spmd(nc, maps, core_ids=list(range(8)))
    return np.concatenate([r["wb"] for r in res.results], axis=0)


def kernel(**inp):
    x = np.ascontiguousarray(np.asarray(inp["x"], np.float32))
    modv = run_mod(inp)
    for l in range(2):
        w_in_l = run_wc(np.asarray(inp["w_in"][l]))
        ckvnT, V, kidxT = run_1ak(l, x, modv[l], w_in_l, inp["mla_kv_norm"][l])
        ya = run_1aq(inp, l, x, modv[l], ckvnT, V, kidxT, w_in_l=w_in_l)
        yb = run_1b(inp, l, x, modv[l], w_in_l=w_in_l)
        yc = run_1c(l, x, modv[l], w_in_l, inp["hgrn_lb"], inp["hgrn_onorm"][l])
        y = np.ascontiguousarray(np.concatenate([ya, yb, yc], axis=-1))
        wo = run_wc(np.asarray(inp["w_out"][l]))
        w1 = run_wc(np.asarray(inp["w_ff1"][l]))
        w2 = run_wc(np.asarray(inp["w_ff2"][l]))
        x = run_l2(x, y, modv[l], wo, w1, w2)
    return x
```

```python
import numpy as np
from contextlib import ExitStack
import ml_dtypes
import concourse.bass as bass
import concourse.mybir as mybir
from concourse.bass_utils import run_bass_kernel_spmd

F32 = mybir.dt.float32
BF16 = mybir.dt.bfloat16
AF = mybir.ActivationFunctionType
ALU = mybir.AluOpType
AX = mybir.AxisListType

D = 4096
SEQ = 8192
NB = 2
DFF = 16384
EPS = 1e-6


class Buf:
    __slots__ = ("name", "last_w", "readers")

    def __init__(self, name=""):
        self.name = name
        self.last_w = None
        self.readers = []


class Prog:
    COMPUTE = ("pe", "act", "dve", "pool")
    QUEUES = {"sp": 24, "actq": 8, "poolq": 8}
    Q2ENG = {"sp": "sp", "actq": "act", "poolq": "pool"}

    def __init__(self, nc):
        self.nc = nc
        self.streams = {e: [] for e in ("pe", "act", "dve", "pool", "sp")}
        self.nops = {e: 0 for e in self.COMPUTE}
        self.marked = {e: set() for e in self.COMPUTE}
        self.dma_n = {q: 0 for q in self.QUEUES}
        self.dma_val = {}

    def _deps(self, eng, reads, writes):
        deps = []
        for b in reads:
            if b.last_w is not None:
                deps.append((b.last_w, True))
        for b in writes:
            if b.last_w is not None:
                deps.append((b.last_w, True))
            for r in b.readers:
                deps.append((r, False))
        out = {}
        for d, strong in deps:
            if d[0] == "c" and d[1] == eng:
                if eng == "pe":
                    continue
            out[d] = True
        return list(out.keys())

    def _finish(self, tok, reads, writes):
        for b in writes:
            b.last_w = tok
            b.readers = []
        for b in reads:
            if b not in writes:
                b.readers.append(tok)

    def op(self, eng, fn, reads=(), writes=()):
        deps = self._deps(eng, reads, writes)
        idx = self.nops[eng]
        self.nops[eng] += 1
        tok = ("c", eng, idx)
        for d in deps:
            if d[0] == "c":
                self.marked[d[1]].add(d[2])
        self.streams[eng].append(("op", fn, deps, tok))
        self._finish(tok, reads, writes)
        return tok

    def dma(self, q, fn, reads=(), writes=()):
        eng = self.Q2ENG[q]
        deps = self._deps("dma", reads, writes)
        n = self.dma_n[q]
        self.dma_n[q] += 1
        slot = n % self.QUEUES[q]
        prev = self.dma_val.get((q, slot), 0)
        val = prev + 16
        self.dma_val[(q, slot)] = val
        tok = ("d", q, slot, val)
        if prev > 0:
            deps.append(("d", q, slot, prev))
        for d in deps:
            if d[0] == "c":
                self.marked[d[1]].add(d[2])
        self.streams[eng].append(("dma", fn, deps, tok))
        self._finish(tok, reads, writes)
        return tok

    def emit(self):
        nc = self.nc
        with ExitStack() as es:
            csem = {e: es.enter_context(nc.semaphore("s_" + e)) for e in self.COMPUTE}
            dsem = {}
            for q, n in self.QUEUES.items():
                for s in range(n):
                    dsem[(q, s)] = es.enter_context(nc.semaphore("d_%s_%d" % (q, s)))
            cnt = {}
            for e in self.COMPUTE:
                c = 0
                m = self.marked[e]
                arr = np.zeros(self.nops[e] + 1, dtype=np.int64)
                for i in range(self.nops[e]):
                    if i in m:
                        c += 1
                    arr[i] = c
                cnt[e] = arr
            final_dma = dict(self.dma_val)
            block = es.enter_context(nc.Block())

            def run_stream(engname, engobj):
                known_c = {e: 0 for e in self.COMPUTE}
                known_d = {}
                for kind, fn, deps, tok in self.streams[engname]:
                    need_c = {}
                    need_d = {}
                    for d in deps:
                        if d[0] == "c":
                            v = int(cnt[d[1]][d[2]])
                            if v > known_c[d[1]] and v > need_c.get(d[1], 0):
                                need_c[d[1]] = v
                        else:
                            key = (d[1], d[2])
                            if d[3] > known_d.get(key, 0) and d[3] > need_d.get(key, 0):
                                need_d[key] = d[3]
                    for e, v in need_c.items():
                        engobj.wait_ge(csem[e], v)
                        known_c[e] = v
                    for key, v in need_d.items():
                        engobj.wait_ge(dsem[key], v)
                        known_d[key] = v
                    ins = fn(engobj)
                    if kind == "op":
                        if tok[2] in self.marked[tok[1]]:
                            ins.then_inc(csem[tok[1]], 1)
                    else:
                        ins.then_inc(dsem[(tok[1], tok[2])], 16)
                if engname == "sp":
                    for key, v in final_dma.items():
                        if v > known_d.get(key, 0):
                            engobj.wait_ge(dsem[key], v)

            @block.tensor
            def _(e):
                run_stream("pe", e)

            @block.vector
            def _(e):
                run_stream("dve", e)

            @block.scalar
            def _(e):
                run_stream("act", e)

            @block.gpsimd
            def _(e):
                run_stream("pool", e)

            @block.sync
            def _(e):
                run_stream("sp", e)


class K:
    def __init__(self):
        self.nc = bass.Bass("TRN2", target_bir_lowering=False)
        self.P = Prog(self.nc)
        self.es = ExitStack()
        self.n = 0

    def din(self, name, shape, dt=F32):
        return self.nc.dram_tensor(name, list(shape), dt, kind="ExternalInput").ap()

    def dout(self, name, shape, dt=F32):
        return self.nc.dram_tensor(name, list(shape), dt, kind="ExternalOutput").ap()

    def dscr(self, name, shape, dt=F32):
        return self.nc.dram_tensor(name, list(shape), dt, kind="Internal").ap()

    def sb(self, shape, dt=F32, name=None):
        self.n += 1
        t = self.es.enter_context(self.nc.sbuf_tensor(name or ("sb%d" % self.n), list(shape), dt))
        return t, Buf(name or "")

    def ps(self, shape, dt=F32, name=None):
        self.n += 1
        t = self.es.enter_context(self.nc.psum_tensor(name or ("ps%d" % self.n), list(shape), dt))
        return t, Buf(name or "")

    def mm(self, out, lhsT, rhs, start, stop, reads, writes):
        self.P.op("pe", lambda e: e.matmul(out, lhsT=lhsT, rhs=rhs, start=start, stop=stop),
                  reads=reads, writes=writes)

    def tr(self, out, in_, ident, reads, writes):
        self.P.op("pe", lambda e: e.transpose(out=out, in_=in_, identity=ident),
                  reads=reads, writes=writes)

    def act(self, out, in_, func, reads, writes, bias=None, scale=None, accum_out=None, eng="act"):
        kw = {}
        if bias is not None:
            kw["bias"] = bias
        if scale is not None:
            kw["scale"] = scale
        if accum_out is not None:
            kw["accum_out"] = accum_out
        self.P.op("act", lambda e: e.activation(out=out, in_=in_, func=func, **kw),
                  reads=reads, writes=writes)

    def tt(self, out, in0, in1, op, reads, writes, eng="dve"):
        self.P.op(eng, lambda e: e.tensor_tensor(out=out, in0=in0, in1=in1, op=op),
                  reads=reads, writes=writes)

    def ts(self, out, in0, s1, s2, op0, op1, reads, writes, eng="dve", accum_out=None):
        if op1 is None:
            self.P.op(eng, lambda e: e.tensor_scalar(out=out, in0=in0, scalar1=s1, scalar2=None, op0=op0),
                      reads=reads, writes=writes)
        elif accum_out is not None:
            self.P.op(eng, lambda e: e.tensor_scalar(out=out, in0=in0, scalar1=s1, scalar2=s2, op0=op0,
                                                     op1=op1, accum_out=accum_out),
                      reads=reads, writes=writes)
        else:
            self.P.op(eng, lambda e: e.tensor_scalar(out=out, in0=in0, scalar1=s1, scalar2=s2, op0=op0, op1=op1),
                      reads=reads, writes=writes)

    def stt(self, out, in0, scalar, in1, op0, op1, reads, writes, accum_out=None):
        if accum_out is None:
            self.P.op("dve", lambda e: e.scalar_tensor_tensor(out=out, in0=in0, scalar=scalar, in1=in1,
                                                              op0=op0, op1=op1),
                      reads=reads, writes=writes)
        else:
            self.P.op("dve", lambda e: e.scalar_tensor_tensor(out=out, in0=in0, scalar=scalar, in1=in1,
                                                              op0=op0, op1=op1, accum_out=accum_out),
                      reads=reads, writes=writes)

    def cp(self, out, in_, reads, writes, eng="dve"):
        self.P.op(eng, lambda e: e.tensor_copy(out=out, in_=in_), reads=reads, writes=writes)

    def recip(self, out, in_, reads, writes):
        self.P.op("dve", lambda e: e.reciprocal(out=out, in_=in_), reads=reads, writes=writes)

    def memset(self, ap, v, writes, eng="pool"):
        self.P.op(eng, lambda e: e.memset(ap, v), writes=writes)

    def dma(self, out, in_, reads, writes, q="sp", **kw):
        self.P.dma(q, lambda e: e.dma_start(out=out, in_=in_, **kw), reads=reads, writes=writes)

    def rstd(self, ssq, n, eps, tmpb=None):
        t, b = ssq
        self.ts(t, t, 1.0 / n, eps, ALU.mult, ALU.add, [b], [b])
        self.act(t, t, AF.Sqrt, [b], [b])
        self.recip(t, t, [b], [b])

    def finish(self):
        self.P.emit()
        self.es.close()
        return self.nc


def build_mod():
    k = K()
    cT = k.din("cT", [128, 32, 2])
    aw = k.din("aw", [2, 6, 4096, 512])
    ab = k.din("ab", [2, 6, 512])
    ng = k.din("ng", [2, 4, 512])
    out = k.dout("modv", [2, 2, 6, 512])
    ct, b_ct = k.sb([128, 32, 2])
    ca, b_ca = k.sb([128, 32, 2])
    sg, b_sg = k.sb([128, 32, 2])
    wb = [k.sb([128, 16, 512]) for _ in range(3)]
    abt, b_ab = k.sb([2, 2, 6, 512])
    ngt, b_ng = k.sb([2, 2, 4, 512])
    modt, b_mod = k.sb([2, 6, 512])
    res, b_res = k.sb([2, 2, 6, 512])
    pm = [k.ps([2, 512]) for _ in range(2)]
    k.dma(ct[:], cT, [], [b_ct])
    for b in range(2):
        k.dma(abt[b:b + 1], ab[None], [], [b_ab], q="actq")
        k.dma(ngt[b:b + 1], ng[None], [], [b_ng], q="actq")
    k.act(sg[:], ct[:], AF.Sigmoid, [b_ct], [b_sg])
    k.tt(ca[:], ct[:], sg[:], ALU.mult, [b_ct, b_sg], [b_ca])
    it = 0
    for l in range(2):
        for j in range(6):
            pt, b_pt = pm[(l * 6 + j) % 2]
            for hf in range(2):
                wt, b_wt = wb[it % 3]
                it += 1
                src = aw[l, j, hf * 2048:(hf + 1) * 2048, :].rearrange("(c p) n -> p c n", p=128)
                k.dma(wt[:], src, [], [b_wt])
                for c in range(16):
                    kk = hf * 16 + c
                    k.mm(pt[:], ca[:, kk, :], wt[:, c, :], kk == 0, kk == 31, [b_ca, b_wt], [b_pt])
            k.tt(modt[:, j, :], pt[:], abt[:, l, j, :], ALU.add, [b_pt, b_ab], [b_mod])
        k.stt(res[:, l, 0, :], modt[:, 1, :], 1.0, ngt[:, l, 0, :], ALU.add, ALU.mult, [b_mod, b_ng], [b_res])
        k.cp(res[:, l, 1, :], modt[:, 0, :], [b_mod], [b_res])
        k.tt(res[:, l, 2, :], modt[:, 2, :], ngt[:, l, 1, :], ALU.mult, [b_mod, b_ng], [b_res])
        k.stt(res[:, l, 3, :], modt[:, 4, :], 1.0, ngt[:, l, 2, :], ALU.add, ALU.mult, [b_mod, b_ng], [b_res])
        k.cp(res[:, l, 4, :], modt[:, 3, :], [b_mod], [b_res])
        k.tt(res[:, l, 5, :], modt[:, 5, :], ngt[:, l, 3, :], ALU.mult, [b_mod, b_ng], [b_res])
    k.dma(out.rearrange("l b v n -> b l v n"), res[:], [b_res], [])
    return k.finish()


def run_mod(inp):
    c = np.asarray(inp["c"], np.float32)
    cT = np.ascontiguousarray(c.reshape(2, 32, 128).transpose(2, 1, 0))
    ada_w = inp["ada_w"]
    ada_b = np.asarray(inp["ada_b"], np.float32)
    norm_g = np.asarray(inp["norm_g"], np.float32)
    maps = []
    for ci in range(8):
        sl = slice(ci * 512, (ci + 1) * 512)
        aw = np.ascontiguousarray(ada_w.reshape(2, 4096, 6, 4096)[:, :, :, sl].transpose(0, 2, 1, 3))
        ab = np.ascontiguousarray(ada_b.reshape(2, 6, 4096)[:, :, sl])
        ng = np.ascontiguousarray(norm_g[:, :, sl])
        maps.append({"cT": cT, "aw": aw, "ab": ab, "ng": ng})
    nc = _get_nc('mod', build_mod)
    res = run_bass_kernel_spmd(nc, maps, core_ids=list(range(8)))
    modv = np.concatenate([r["modv"] for r in res.results], axis=-1)
    return modv


TG = 256
NT = TG // 128
KC = 8
NWB = 4


def build_l2(ntok=2048, stop=None):
    k = K()
    x = k.din("x", [ntok, D])
    y = k.din("y", [ntok, D], BF16)
    modr = k.din("modr", [6, D])
    modc = k.din("modc", [6, 128, 32])
    w_out = k.din("w_out", [D, D], BF16)
    w1 = k.din("w1", [D, DFF], BF16)
    w2 = k.din("w2", [DFF, D], BF16)
    idf_d = k.din("identf", [128, 128])
    xo = k.dout("xo", [ntok, D])

    idf, b_idf = k.sb([128, 128])
    idb, b_idb = k.sb([128, 128], BF16)
    mc, b_mc = k.sb([128, 6, 32])
    actT, b_actT = k.sb([128, 32, TG], BF16)
    hid, b_hid = k.sb([128, 128, TG], BF16)
    o1 = [k.sb([128, D]) for _ in range(NT)]
    tx, b_tx = k.sb([128, D])
    rowb, b_rowb = k.sb([128, D])
    yb, b_yb = k.sb([128, D], BF16)
    wbf = [k.sb([128, KC, 512], BF16) for _ in range(NWB)]
    rl, b_rl = k.sb([128, 4, TG])
    st, b_st = k.sb([128, 4])
    acc = [k.ps([128, 512]) for _ in range(NT)]
    accF, b_accF = k.ps([128, 4, 512])
    ptb, b_ptb = k.ps([128, 4, 128], BF16)
    ptf, b_ptf = k.ps([128, 4, 128])

    k.dma(idf[:], idf_d, [], [b_idf])
    k.dma(mc[:], modc.rearrange("v p c -> p v c"), [], [b_mc])
    k.cp(idb[:], idf[:], [b_idf], [b_idb])
    wi = [0]

    def load_w(src):
        i = wi[0] % NWB
        wi[0] += 1
        wb_, b_wb = wbf[i]
        k.dma(wb_[:], src, [], [b_wb])
        return wb_, b_wb

    def sumsq(src, b_src, col):
        k.act(yb[:], src, AF.Square, [b_src], [b_yb, b_st], accum_out=st[:, col:col + 1])

    def rstd_col(col):
        t = st[:, col:col + 1]
        k.ts(t, t, 1.0 / D, EPS, ALU.mult, ALU.add, [b_st], [b_st])
        k.act(t, t, AF.Sqrt, [b_st], [b_st])
        k.recip(t, t, [b_st], [b_st])

    for ps_ in range(ntok // TG):
        t0 = ps_ * TG
        b_xo = [Buf() for _ in range(NT)]
        for mt in range(NT):
            r0 = t0 + mt * 128
            k.dma(yb[:], y[r0:r0 + 128, :], [], [b_yb])
            for g in range(8):
                for c in range(4):
                    k.tr(ptb[:, c, :], yb[:, (g * 4 + c) * 128:(g * 4 + c + 1) * 128], idb[:], [b_yb, b_idb], [b_ptb])
                k.cp(actT[:, g * 4:(g + 1) * 4, mt * 128:(mt + 1) * 128], ptb[:], [b_ptb], [b_actT])
        for nb in range(8):
            for kt in range(32 // KC):
                src = w_out[kt * KC * 128:(kt + 1) * KC * 128, nb * 512:(nb + 1) * 512].rearrange("(c p) n -> p c n", p=128)
                wb_, b_wb = load_w(src)
                for mt in range(NT):
                    a, b_a = acc[mt]
                    for c in range(KC):
                        kk = kt * KC + c
                        k.mm(a[:], actT[:, kk, mt * 128:(mt + 1) * 128], wb_[:, c, :], kk == 0, kk == 31, [b_actT, b_wb], [b_a])
            for mt in range(NT):
                a, b_a = acc[mt]
                o, b_o = o1[mt]
                k.act(o[:, nb * 512:(nb + 1) * 512], a[:], AF.Copy, [b_a], [b_o])
        for mt in range(NT):
            r0 = t0 + mt * 128
            o, b_o = o1[mt]
            sumsq(o[:], b_o, 0)
            rstd_col(0)
            k.dma(tx[:], x[r0:r0 + 128, :], [], [b_tx])
            k.dma(rowb[:], modr[2, :].partition_broadcast(128), [], [b_rowb], q="actq")
            k.stt(o[:], o[:], st[:, 0:1], rowb[:], ALU.mult, ALU.mult, [b_o, b_st, b_rowb], [b_o])
            k.tt(tx[:], tx[:], o[:], ALU.add, [b_tx, b_o], [b_tx], eng="pool")
            k.dma(xo[r0:r0 + 128, :], tx[:], [b_tx], [b_xo[mt]])
            sumsq(tx[:], b_tx, 1)
            rstd_col(1)
            k.act(o[:], tx[:], AF.Identity, [b_tx, b_st], [b_o], scale=st[:, 1:2])
            for g in range(8):
                for c in range(4):
                    k.tr(ptf[:, c, :], o[:, (g * 4 + c) * 128:(g * 4 + c + 1) * 128], idf[:], [b_o, b_idf], [b_ptf])
                for c in range(4):
                    kk = g * 4 + c
                    k.act(actT[:, kk, mt * 128:(mt + 1) * 128], ptf[:, c, :], AF.Identity, [b_ptf, b_mc], [b_actT],
                          bias=mc[:, 4, kk:kk + 1], scale=mc[:, 3, kk:kk + 1])
        if stop == 'C':
            continue
        for g in range(DFF // 512):
            for kt in range(32 // KC):
                src = w1[kt * KC * 128:(kt + 1) * KC * 128, g * 512:(g + 1) * 512].rearrange("(c p) n -> p c n", p=128)
                wb_, b_wb = load_w(src)
                for fb in range(4):
                    for c in range(KC):
                        kk = kt * KC + c
                        k.mm(accF[:, fb, 0:TG], wb_[:, c, fb * 128:(fb + 1) * 128], actT[:, kk, :], kk == 0, kk == 31,
                             [b_actT, b_wb], [b_accF])
            k.act(rl[:], accF[:, :, 0:TG], AF.Relu, [b_accF], [b_rl])
            k.tt(hid[:, g * 4:(g + 1) * 4, :], rl[:], rl[:], ALU.mult, [b_rl], [b_hid])
        for db in range(8):
            for ft in range(128 // KC):
                src = w2[ft * KC * 128:(ft + 1) * KC * 128, db * 512:(db + 1) * 512].rearrange("(c p) n -> p c n", p=128)
                wb_, b_wb = load_w(src)
                for mt in range(NT):
                    a, b_a = acc[mt]
                    for c in range(KC):
                        kk = ft * KC + c
                        k.mm(a[:], hid[:, kk, mt * 128:(mt + 1) * 128], wb_[:, c, :], kk == 0, kk == 127, [b_hid, b_wb], [b_a])
            for mt in range(NT):
                a, b_a = acc[mt]
                o, b_o = o1[mt]
                k.act(o[:, db * 512:(db + 1) * 512], a[:], AF.Copy, [b_a], [b_o])
        for mt in range(NT):
            r0 = t0 + mt * 128
            o, b_o = o1[mt]
            sumsq(o[:], b_o, 2)
            rstd_col(2)
            k.dma(tx[:], xo[r0:r0 + 128, :], [b_xo[mt]], [b_tx])
            k.dma(rowb[:], modr[5, :].partition_broadcast(128), [], [b_rowb], q="actq")
            k.stt(o[:], o[:], st[:, 2:3], rowb[:], ALU.mult, ALU.mult, [b_o, b_st, b_rowb], [b_o])
            k.tt(tx[:], tx[:], o[:], ALU.add, [b_tx, b_o], [b_tx], eng="pool")
            k.dma(xo[r0:r0 + 128, :], tx[:], [b_tx], [b_xo[mt]])
    return k.finish()


_IDENT = np.eye(128, dtype=np.float32)


def run_l2(x, ybf, modv_l, w_out, w1, w2):
    nc = _get_nc('l2', build_l2)
    maps = []
    for ci in range(8):
        b, q = ci // 4, ci % 4
        sl = slice(q * 2048, (q + 1) * 2048)
        mr = np.ascontiguousarray(modv_l[b])
        mcm = np.ascontiguousarray(mr.reshape(6, 32, 128).transpose(0, 2, 1))
        maps.append({"x": np.ascontiguousarray(x[b, sl]), "y": np.ascontiguousarray(ybf[b, sl]),
                     "modr": mr, "modc": mcm, "w_out": w_out, "w1": w1, "w2": w2, "identf": _IDENT})
    res = run_bass_kernel_spmd(nc, maps, core_ids=list(range(8)))
    out = np.empty((2, 8192, 4096), np.float32)
    for ci in range(8):
        b, q = ci // 4, ci % 4
        out[b, q * 2048:(q + 1) * 2048] = res.results[ci]["xo"]
    return out


class HTMaker:
    def __init__(self, k, modc_d, idf_d, inplace=False, junk=None):
        self.k = k
        self.idf, self.b_idf = k.sb([128, 128])
        self.idb, self.b_idb = k.sb([128, 128], BF16)
        self.mc, self.b_mc = k.sb([128, 6, 32])
        self.tx, self.b_tx = k.sb([128, D])
        if inplace:
            self.xn, self.b_xn = self.tx, self.b_tx
        else:
            self.xn, self.b_xn = k.sb([128, D])
        self.junk, self.b_junk = junk if junk is not None else k.sb([128, D], BF16)
        self.st, self.b_st = k.sb([128, 2])
        self.ptf, self.b_ptf = k.ps([128, 4, 128])
        k.dma(self.idf[:], idf_d, [], [self.b_idf])
        k.dma(self.mc[:], modc_d.rearrange("v p c -> p v c"), [], [self.b_mc])
        k.cp(self.idb[:], self.idf[:], [self.b_idf], [self.b_idb])

    def emit(self, src, dst_fn, b_dst, ai=0, bi=1):
        k = self
        kk_ = self.k
        kk_.dma(self.tx[:], src, [], [self.b_tx])
        kk_.act(self.junk[:], self.tx[:], AF.Square, [self.b_tx], [self.b_junk, self.b_st],
                accum_out=self.st[:, 0:1])
        t = self.st[:, 0:1]
        kk_.ts(t, t, 1.0 / D, EPS, ALU.mult, ALU.add, [self.b_st], [self.b_st])
        kk_.act(t, t, AF.Sqrt, [self.b_st], [self.b_st])
        kk_.recip(t, t, [self.b_st], [self.b_st])
        kk_.act(self.xn[:], self.tx[:], AF.Identity, [self.b_tx, self.b_st], [self.b_xn], scale=self.st[:, 0:1])
        for g in range(8):
            for c in range(4):
                q = g * 4 + c
                kk_.tr(self.ptf[:, c, :], self.xn[:, q * 128:(q + 1) * 128], self.idf[:],
                       [self.b_xn, self.b_idf], [self.b_ptf])
            for c in range(4):
                q = g * 4 + c
                kk_.act(dst_fn(q), self.ptf[:, c, :], AF.Identity, [self.b_ptf, self.b_mc], [b_dst],
                        bias=self.mc[:, bi, q:q + 1], scale=self.mc[:, ai, q:q + 1])


class WLoader:
    def __init__(self, k, ncol=128, nbuf=2, nst=None):
        self.k = k
        self.ncol = ncol
        self.st = [k.sb([128, 32, ncol]) for _ in range(nst or nbuf)]
        self.bf = [k.sb([128, 32, ncol], BF16) for _ in range(nbuf)]
        self.i = 0

    def load(self, W, c0, n):
        k = self.k
        ws, b_ws = self.st[self.i % len(self.st)]
        wb, b_wb = self.bf[self.i % len(self.bf)]
        self.i += 1
        if W.dtype == BF16:
            k.dma(wb[:, :, 0:n], W[:, c0:c0 + n].rearrange("(c p) n -> p c n", p=128), [], [b_wb])
            return wb, b_wb
        k.dma(ws[:, :, 0:n], W[:, c0:c0 + n].rearrange("(c p) n -> p c n", p=128), [], [b_ws])
        k.cp(wb[:, :, 0:n], ws[:, :, 0:n], [b_ws], [b_wb], eng="pool")
        return wb, b_wb


TB = 512
CH = 64


def gemm_fm(k, wb, b_wb, ncols_off, hT, b_hT, out, b_out, M=128):
    for kk in range(32):
        k.mm(out, wb[:, kk, ncols_off:ncols_off + M], hT[:, kk, :], kk == 0, kk == 31, [b_wb, b_hT], [b_out])


def build_1c(layer):
    HC = 3
    k = K()
    x = k.din("x", [SEQ, D])
    modc = k.din("modc", [6, 128, 32])
    idf_d = k.din("identf", [128, 128])
    w = k.din("w", [D, 4 * HC * 128], BF16)
    lbraw = k.din("lbraw", [128, HC, 2])
    onorm = k.din("onorm", [HC * 128])
    cmask_d = k.din("cmask", [64, 64])
    smask_d = k.din("smask", [128, TB])
    yo = k.dout("yc", [SEQ, HC * 128], BF16)

    hm = HTMaker(k, modc, idf_d)
    wl = WLoader(k, 128, 2)
    hT, b_hT = k.sb([128, 32, TB], BF16)
    cmask, b_cm = k.sb([64, 64])
    smask, b_sm = k.sb([128, TB])
    lbt, b_lb = k.sb([128, HC, 2])
    lbw, b_lbw = k.sb([128, 6, HC])
    lb, b_lbv = k.sb([128, HC])
    oml, b_oml = k.sb([128, HC])
    onb, b_onb = k.sb([64, HC * 128])
    k.dma(cmask[:], cmask_d, [], [b_cm])
    k.dma(smask[:], smask_d, [], [b_sm])
    k.dma(lbt[:], lbraw, [], [b_lb])
    k.dma(onb[:], onorm.partition_broadcast(64), [], [b_onb])
    m_ = lbw[:, 0, :]
    k.tt(m_, lbt[:, :, 0], lbt[:, :, 1], ALU.max, [b_lb], [b_lbw])
    k.tt(lbw[:, 1, :], lbt[:, :, 0], m_, ALU.subtract, [b_lb, b_lbw], [b_lbw])
    k.tt(lbw[:, 2, :], lbt[:, :, 1], m_, ALU.subtract, [b_lb, b_lbw], [b_lbw])
    k.act(lbw[:, 1, :], lbw[:, 1, :], AF.Exp, [b_lbw], [b_lbw])
    k.act(lbw[:, 2, :], lbw[:, 2, :], AF.Exp, [b_lbw], [b_lbw])
    k.tt(lbw[:, 3, :], lbw[:, 1, :], lbw[:, 2, :], ALU.add, [b_lbw], [b_lbw])
    k.recip(lbw[:, 3, :], lbw[:, 3, :], [b_lbw], [b_lbw])
    k.tt(lbw[:, 4, :], lbw[:, 1, :], lbw[:, 3, :], ALU.mult, [b_lbw], [b_lbw])
    k.tt(lbw[:, 5, :], lbw[:, 2, :], lbw[:, 3, :], ALU.mult, [b_lbw], [b_lbw])
    if layer == 0:
        k.tt(lb[:], lbw[:, 4, :], lbw[:, 4, :], ALU.subtract, [b_lbw], [b_lbv])
    else:
        k.tt(lb[:], lbw[:, 4, :], lbw[:, 5, :], ALU.add, [b_lbw], [b_lbv])
        k.tt(lb[:], lb[:], lbw[:, 4, :], ALU.subtract, [b_lbw, b_lbv], [b_lbv])
    k.ts(oml[:], lb[:], -1.0, 1.0, ALU.mult, ALU.add, [b_lbv], [b_oml])

    pq = [k.ps([128, TB]) for _ in range(2)]
    pv, b_pv = k.ps([64, 4, 128])
    pAT, b_pAT = k.ps([64, HC, 64])
    po, b_po = k.ps([64, HC, 128])
    pS, b_pS = k.ps([128, HC, 128])
    pkt, b_pkt = k.ps([64, HC, 128], BF16)

    qs, b_qs = k.sb([128, TB])
    sg, b_sg = k.sb([128, TB])
    sgn, b_sgn = k.sb([128, TB])
    lf, b_lf = k.sb([128, TB])
    bb, b_bb = k.sb([128, TB])
    enb, b_enb = k.sb([128, TB])
    eb, b_eb = k.sb([128, HC, TB])
    qt, b_qt = k.sb([128, HC, TB], BF16)
    kt, b_kt = k.sb([128, HC, TB], BF16)
    V, b_V = k.sb([64, 8, HC * 128], BF16)
    gw, b_gw = k.sb([64, 8, HC * 128])
    gs, b_gs = k.sb([64, 4, 128])
    S, b_S = k.sb([128, HC, 128])
    Sb, b_Sb = k.sb([128, HC, 128], BF16)
    t1, b_t1 = k.sb([128, HC, 128])
    ATs, b_ATs = k.sb([64, HC, 64], BF16)
    kts, b_kts = k.sb([64, HC, 128], BF16)
    st, b_st = k.sb([64, HC])
    junk, b_junk = k.sb([64, 128])
    yt = [k.sb([64, HC * 128], BF16) for _ in range(2)]
    k.memset(S[:], 0.0, [b_S])
    k.memset(Sb[:], 0.0, [b_Sb])
    pqi = 0
    def HT(tb):
        for mt in range(TB // 128):
            hm.emit(x[tb * TB + mt * 128:tb * TB + (mt + 1) * 128, :],
                    lambda q, mt=mt: hT[:, q, mt * 128:(mt + 1) * 128], b_hT)

    HT(0)
    for tb in range(SEQ // TB):
        t0 = tb * TB
        for h in range(HC):
            wb, b_wb = wl.load(w, h * 128, 128)
            p_, b_p = pq[pqi % 2]; pqi += 1
            gemm_fm(k, wb, b_wb, 0, hT, b_hT, p_[:], b_p)
            k.act(qs[:], p_[:], AF.Silu, [b_p], [b_qs])
            wb, b_wb = wl.load(w, (HC + h) * 128, 128)
            p_, b_p = pq[pqi % 2]; pqi += 1
            gemm_fm(k, wb, b_wb, 0, hT, b_hT, p_[:], b_p)
            k.act(sg[:], p_[:], AF.Sigmoid, [b_p], [b_sg])
            k.act(sgn[:], p_[:], AF.Sigmoid, [b_p], [b_sgn], scale=-1.0)
            k.ts(sg[:], sg[:], oml[:, h:h + 1], lb[:, h:h + 1], ALU.mult, ALU.add, [b_sg, b_oml, b_lbv], [b_sg])
            k.act(lf[:], sg[:], AF.Ln, [b_sg], [b_lf])
            k.ts(sgn[:], sgn[:], oml[:, h:h + 1], None, ALU.mult, None, [b_sgn, b_oml], [b_sgn])
            k.P.op("dve", lambda e: e.tensor_tensor_scan(out=bb[:], data0=smask[:], data1=lf[:], initial=0.0,
                                                        op0=ALU.mult, op1=ALU.add),
                   reads=[b_sm, b_lf], writes=[b_bb])
            k.act(eb[:, h, :], bb[:], AF.Exp, [b_bb], [b_eb])
            k.act(enb[:], bb[:], AF.Exp, [b_bb], [b_enb], scale=-1.0)
            k.tt(qt[:, h, :], qs[:], eb[:, h, :], ALU.mult, [b_qs, b_eb], [b_qt])
            k.tt(kt[:, h, :], sgn[:], enb[:], ALU.mult, [b_sgn, b_enb], [b_kt])
        for j in range(2 * HC):
            wb, b_wb = wl.load(w, (2 * HC + j) * 128, 128)
            for c4 in range(2):
                for c in range(4):
                    cc = c4 * 4 + c
                    for kk in range(32):
                        k.mm(pv[:, c, :], hT[:, kk, cc * 64:(cc + 1) * 64], wb[:, kk, :], kk == 0, kk == 31,
                             [b_hT, b_wb], [b_pv])
                if j < HC:
                    k.act(V[:, c4 * 4:(c4 + 1) * 4, j * 128:(j + 1) * 128], pv[:], AF.Copy, [b_pv], [b_V])
                else:
                    hh = j - HC
                    k.act(gs[:], pv[:], AF.Silu, [b_pv], [b_gs])
                    for c in range(4):
                        k.tt(gw[:, c4 * 4 + c, hh * 128:(hh + 1) * 128], gs[:, c, :], onb[:, hh * 128:(hh + 1) * 128],
                             ALU.mult, [b_gs, b_onb], [b_gw])
        if tb + 1 < SEQ // TB:
            HT(tb + 1)
        for c in range(8):
            cs = slice(c * 64, (c + 1) * 64)
            y_, b_y = yt[c % 2]
            for h in range(HC):
                hs = slice(h * 128, (h + 1) * 128)
                k.mm(pAT[:, h, :], kt[:, h, cs], qt[:, h, cs], True, True, [b_kt, b_qt], [b_pAT])
                k.tt(ATs[:, h, :], pAT[:, h, :], cmask[:], ALU.mult, [b_pAT, b_cm], [b_ATs])
                k.mm(po[:, h, :], ATs[:, h, :], V[:, c, hs], True, False, [b_ATs, b_V], [b_po])
                k.mm(po[:, h, :], qt[:, h, cs], Sb[:, h, :], False, True, [b_qt, b_Sb], [b_po])
                k.tr(pkt[:, h, :], kt[:, h, cs], hm.idb[:], [b_kt, hm.b_idb], [b_pkt])
                k.act(kts[:, h, :], pkt[:, h, :], AF.Copy, [b_pkt], [b_kts])
                k.mm(pS[:, h, :], kts[:, h, :], V[:, c, hs], True, True, [b_kts, b_V], [b_pS])
                ec = eb[:, h, c * 64 + 63:c * 64 + 64]
                k.tt(t1[:, h, :], pS[:, h, :], S[:, h, :], ALU.add, [b_pS, b_S], [b_t1])
                k.ts(S[:, h, :], t1[:, h, :], ec, None, ALU.mult, None, [b_t1, b_eb], [b_S])
                k.act(Sb[:, h, :], t1[:, h, :], AF.Identity, [b_t1, b_eb], [b_Sb], scale=ec)
                k.act(junk[:], po[:, h, :], AF.Square, [b_po], [b_junk, b_st], accum_out=st[:, h:h + 1])
            k.ts(st[:], st[:], 1.0 / 128, EPS, ALU.mult, ALU.add, [b_st], [b_st])
            k.act(st[:], st[:], AF.Sqrt, [b_st], [b_st])
            k.recip(st[:], st[:], [b_st], [b_st])
            for h in range(HC):
                hs = slice(h * 128, (h + 1) * 128)
                k.stt(y_[:, hs], po[:, h, :], st[:, h:h + 1], gw[:, c, hs], ALU.mult, ALU.mult,
                      [b_po, b_st, b_gw], [b_y])
            k.dma(yo[t0 + c * 64:t0 + (c + 1) * 64, :], y_[:], [b_y], [], q="actq")
    return k.finish()


_CMASK = np.triu(np.ones((64, 64), np.float32))
_SMASK = np.ones((128, TB), np.float32)
_SMASK[:, ::CH] = 0.0


def run_1c(layer, x, modv_l, w_in_l, hgrn_lb, hgrn_onorm_l):
    A_COLS, B_COLS = 1104, 5344
    c0 = A_COLS + B_COLS
    if w_in_l.dtype == np.float32:
        w_in_l = run_wc(np.asarray(w_in_l))
    nc = _get_nc(('1c', layer), lambda: build_1c(layer))
    maps = []
    for ci in range(8):
        b, g = ci // 4, ci % 4
        hs = slice(g * 384, (g + 1) * 384)
        wc = w_in_l[:, c0:]
        wsl = np.ascontiguousarray(np.concatenate([wc[:, j * 1536:(j + 1) * 1536][:, hs] for j in range(4)], axis=1))
        mr = np.ascontiguousarray(modv_l[b])
        mcm = np.ascontiguousarray(mr.reshape(6, 32, 128).transpose(0, 2, 1))
        lbr = np.ascontiguousarray(hgrn_lb[:, hs].reshape(2, 3, 128).transpose(2, 1, 0))
        maps.append({"x": np.ascontiguousarray(x[b]), "modc": mcm, "identf": _IDENT, "w": wsl,
                     "lbraw": lbr, "onorm": np.ascontiguousarray(hgrn_onorm_l[hs]),
                     "cmask": _CMASK, "smask": _SMASK})
    res = run_bass_kernel_spmd(nc, maps, core_ids=list(range(8)))
    out = np.empty((2, 8192, 1536), ml_dtypes.bfloat16)
    for ci in range(8):
        b, g = ci // 4, ci % 4
        out[b, :, g * 384:(g + 1) * 384] = res.results[ci]["yc"]
    return out


TBB = 256
GN_EPS = 64e-5


def build_1b(seq=SEQ):
    HB = 6
    NCH = TBB // CH
    k = K()
    x = k.din("x", [seq, D])
    modc = k.din("modc", [6, 128, 32])
    idf_d = k.din("identf", [128, 128])
    wrkv = k.din("wrkv", [D, 3 * HB * 64], BF16)
    wlo = k.din("wlo", [D, 736], BF16)
    mu_rkv_d = k.din("mu_rkv", [64, 3 * HB])
    mu_wa_d = k.din("mu_wa", [128, 2])
    mu_g_d = k.din("mu_g", [120, 4])
    hp_d = k.din("hp", [64, 5, HB])
    wup_d = k.din("wup", [128, HB * 64])
    aup_d = k.din("aup", [128, HB * 64])
    gup_d = k.din("gup", [120, 4, HB * 64])
    lnw_d = k.din("lnw", [HB * 64])
    lnb_d = k.din("lnb", [HB * 64])
    smask_d = k.din("smask", [64, HB, TBB])
    m5_d = k.din("m5", [64, 5, 64])
    yo = k.dout("yb", [seq, HB * 64], BF16)

    wl = WLoader(k, 128, 2, nst=0)
    hm = HTMaker(k, modc, idf_d, inplace=True,
                 junk=(wl.bf[0][0][:].rearrange("p a b -> p (a b)"), wl.bf[0][1]))
    hT, b_hT = k.sb([128, 32, TBB], BF16)
    smask, b_sm = k.sb([64, HB, TBB])
    m5, b_m5 = k.sb([64, 5, 64])
    mu_rkv, b_mur = k.sb([64, 3 * HB])
    mu_wa, b_muw = k.sb([128, 2])
    mu_g, b_mug = k.sb([120, 4])
    hp, b_hp = k.sb([64, 5, HB])
    wup, b_wup = k.sb([128, HB * 64], BF16)
    aup, b_aup = k.sb([128, HB * 64], BF16)
    gup, b_gup = k.sb([120, 4, HB * 64], BF16)
    lnw, b_lnw = k.sb([64, HB * 64])
    lnb, b_lnb = k.sb([64, HB * 64])
    ones, b_ones = k.sb([64, 64])
    stgf, b_stg = hm.tx, hm.b_tx
    for dst, src, bd in ((smask, smask_d, b_sm), (m5, m5_d, b_m5), (mu_rkv, mu_rkv_d, b_mur), (mu_wa, mu_wa_d, b_muw),
                         (mu_g, mu_g_d, b_mug), (hp, hp_d, b_hp)):
        k.dma(dst[:], src, [], [bd], q="actq")
    k.dma(lnw[:], lnw_d.partition_broadcast(64), [], [b_lnw], q="actq")
    k.dma(lnb[:], lnb_d.partition_broadcast(64), [], [b_lnb], q="actq")
    k.dma(stgf[:, 0:384], wup_d, [], [b_stg])
    k.cp(wup[:], stgf[:, 0:384], [b_stg], [b_wup])
    k.dma(stgf[:, 0:384], aup_d, [], [b_stg])
    k.cp(aup[:], stgf[:, 0:384], [b_stg], [b_aup])
    k.dma(stgf[0:120, 0:1536], gup_d.rearrange("p a b -> p (a b)"), [], [b_stg])
    k.cp(gup[:].rearrange("p a b -> p (a b)"), stgf[0:120, 0:1536], [b_stg], [b_gup])
    k.memset(ones[:], 1.0, [b_ones])

    pq, b_pq = k.ps([128, 512])
    PA, b_PA = k.ps([64, 32, 64])
    PN, b_PN = k.ps([64, 16, 64])
    PAf = PA[:].rearrange("p a b -> p (a b)")
    PNf = PN[:].rearrange("p a b -> p (a b)")

    R_, b_R = k.sb([128, TBB + 1])
    carry, b_carry = k.sb([128, 3 * HB + 6])
    rTs = [k.sb([64, HB, TBB]) for _ in range(2)]
    kTs = [k.sb([64, HB, TBB]) for _ in range(2)]
    At, b_At = k.sb([64, HB, TBB])
    Bt, b_Bt = k.sb([64, HB, TBB])
    ebC, b_ebC = k.sb([64, HB, NCH])
    lsh, b_lsh = k.sb([128, TBB])
    tw, b_tw = k.sb([128, TBB], BF16)
    ta, b_ta = k.sb([128, TBB], BF16)
    tg, b_tg = k.sb([120, 4, TBB], BF16)
    tmp = [k.sb([64, HB, TBB]) for _ in range(6)]
    Vt, b_Vt = k.sb([64, NCH, HB, 64])
    gt, b_gt = k.sb([64, NCH, HB * 64])
    rk, b_rk = k.sb([64, HB, NCH])
    H, b_H = k.sb([64, HB, 64])
    mats, b_mats = k.sb([64, HB, 5, 64])
    _avf = tmp[1][0][:].rearrange("p a b -> p (a b)")
    _kkf = tmp[2][0][:].rearrange("p a b -> p (a b)")
    b_N = b_X = b_W = tmp[1][1]
    b_U = b_BKt = b_ysb = tmp[2][1]
    Nsb = _avf[:, 0:768].rearrange("p (a b c) -> p a b c", a=HB, b=2)
    X = _avf[:, 768:1152].rearrange("p (a c) -> p a c", a=HB)
    W = _avf[:, 1152:1536].rearrange("p (a c) -> p a c", a=HB)
    BKt = _kkf[:, 0:768].rearrange("p (a b c) -> p a b c", a=HB, b=2)
    U = _kkf[:, 768:1152].rearrange("p (a c) -> p a c", a=HB)
    ysb = _kkf[:, 1152:1536].rearrange("p (a c) -> p a c", a=HB)
    vT, b_vT = tmp[5]
    t2, b_t2 = k.sb([64, HB, 64])
    stt_, b_stt = k.sb([64, HB, 8])
    mv, b_mv = k.sb([64, HB, 2])
    yt = [k.sb([64, HB * 64], BF16) for _ in range(2)]
    k.memset(carry[:], 0.0, [b_carry])
    k.memset(H[:], 0.0, [b_H])
    idf, b_idf = hm.idf, hm.b_idf
    i64 = idf[0:64, 0:64]
    FL = "p a b -> p (a b)"

    def bc(ap3):
        return ap3.to_broadcast([64, HB, TBB])

    def shifted(M, col, mu_ap, b_mu, dst, b_dst):
        k.act(R_[0:M, 1:TBB + 1], pq[0:M, 0:TBB], AF.Copy, [b_pq], [b_R])
        k.cp(R_[0:M, 0:1], carry[0:M, col:col + 1], [b_carry], [b_R])
        k.cp(carry[0:M, col:col + 1], R_[0:M, TBB:TBB + 1], [b_R], [b_carry])
        k.tt(lsh[0:M, :], R_[0:M, 0:TBB], R_[0:M, 1:TBB + 1], ALU.subtract, [b_R], [b_lsh])
        k.stt(dst, lsh[0:M, :], mu_ap, R_[0:M, 1:TBB + 1], ALU.mult, ALU.add, [b_lsh, b_mu, b_R], [b_dst])

    def gemm(wb, b_wb, off, M):
        for kk in range(32):
            k.mm(pq[0:M, 0:TBB], wb[:, kk, off:off + M], hT[:, kk, :], kk == 0, kk == 31, [b_wb, b_hT], [b_pq])

    (sgz, b_sgz), (av, b_av), (kkv, b_kkv), (bv, b_bv), (e1, b_e1), (tq, b_tq) = tmp
    yi = [0]
    nblk = seq // TBB

    def HT(tb):
        t0 = tb * TBB
        for mt in range(TBB // 128):
            hm.emit(x[t0 + mt * 128:t0 + (mt + 1) * 128, :],
                    lambda q, mt=mt: hT[:, q, mt * 128:(mt + 1) * 128], b_hT)

    def gemm(wb, b_wb, off, M):
        for kk in range(32):
            k.mm(pq[0:M, 0:TBB], wb[:, kk, off:off + M], hT[:, kk, :], kk == 0, kk == 31, [b_wb, b_hT], [b_pq])
            if kk % 16 == 15:
                yield

    def gemm_gen(tb):
        rT, b_rT = rTs[tb % 2]
        kT, b_kT = kTs[tb % 2]
        for j in range(3 * HB // 2):
            wb, b_wb = wl.load(wrkv, j * 128, 128)
            for s_ in range(2):
                col = j * 2 + s_
                which, h = col // HB, col % HB
                dstt, b_d = ((rT, b_rT), (kT, b_kT), (vT, b_vT))[which]
                yield from gemm(wb, b_wb, s_ * 64, 64)
                shifted(64, col, mu_rkv[:, col:col + 1], b_mur, dstt[:, h, :], b_d)
        wb, b_wb = wl.load(wlo, 0, 128)
        yield from gemm(wb, b_wb, 0, 128)
        shifted(128, 3 * HB + 0, mu_wa[:, 0:1], b_muw, lsh[:, :], b_lsh)
        k.act(tw[:], lsh[:], AF.Tanh, [b_lsh], [b_tw])
        wb, b_wb = wl.load(wlo, 128, 128)
        yield from gemm(wb, b_wb, 0, 128)
        shifted(128, 3 * HB + 1, mu_wa[:, 1:2], b_muw, lsh[:, :], b_lsh)
        k.act(ta[:], lsh[:], AF.Copy, [b_lsh], [b_ta])
        for j in range(4):
            wb, b_wb = wl.load(wlo, 256 + j * 120, 120)
            yield from gemm(wb, b_wb, 0, 120)
            shifted(120, 3 * HB + 2 + j, mu_g[:, j:j + 1], b_mug, lsh[0:120, :], b_lsh)
            k.act(tg[:, j, :], lsh[0:120, :], AF.Sigmoid, [b_lsh], [b_tg])

    def prep(tb):
        rT, b_rT = rTs[tb % 2]
        kT, b_kT = kTs[tb % 2]
        for r3 in range(2):
            for hl in range(3):
                for c in range(NCH):
                    k.tr(PN[:, hl * NCH + c, :], vT[:, r3 * 3 + hl, c * CH:(c + 1) * CH], i64, [b_vT, b_idf], [b_PN])
            k.cp(Vt[:, :, r3 * 3:(r3 + 1) * 3, :].rearrange("p c h v -> p h c v"),
                 PN[:, 0:3 * NCH, :].rearrange("p (h c) v -> p h c v", h=3), [b_PN], [b_Vt])
        for h in range(HB):
            hs = slice(h * 64, (h + 1) * 64)
            k.mm(pq[0:64, 0:TBB], wup[:, hs], tw[:], True, True, [b_wup, b_tw], [b_pq])
            k.act(sgz[:, h, :], pq[0:64, 0:TBB], AF.Sigmoid, [b_pq, b_hp], [b_sgz], bias=hp[:, 0, h:h + 1])
        for h in range(HB):
            hs = slice(h * 64, (h + 1) * 64)
            k.mm(pq[0:64, 0:TBB], aup[:, hs], ta[:], True, True, [b_aup, b_ta], [b_pq])
            k.act(av[:, h, :], pq[0:64, 0:TBB], AF.Sigmoid, [b_pq, b_hp], [b_av], bias=hp[:, 1, h:h + 1])
        k.ts(sgz[:], sgz[:], -float(np.exp(-0.5)), None, ALU.mult, None, [b_sgz], [b_sgz])
        k.P.op("dve", lambda e: e.tensor_tensor_scan(out=bv[:].rearrange(FL), data0=smask[:].rearrange(FL),
                                                    data1=sgz[:].rearrange(FL), initial=0.0,
                                                    op0=ALU.mult, op1=ALU.add),
               reads=[b_sm, b_sgz], writes=[b_bv])
        k.act(e1[:], bv[:], AF.Exp, [b_bv], [b_e1])
        k.cp(ebC[:], e1[:].rearrange("p h (c s) -> p h c s", s=CH)[:, :, :, CH - 1], [b_e1], [b_ebC])
        k.tt(rT[:], rT[:], e1[:], ALU.mult, [b_rT, b_e1], [b_rT])
        k.tt(sgz[:], bv[:], sgz[:], ALU.subtract, [b_bv, b_sgz], [b_sgz])
        k.act(sgz[:], sgz[:], AF.Exp, [b_sgz], [b_sgz])
        k.act(bv[:], bv[:], AF.Exp, [b_bv], [b_bv], scale=-1.0)
        k.tt(kkv[:], kT[:], bc(hp[:, 2, :, None]), ALU.mult, [b_kT, b_hp], [b_kkv])
        k.tt(tq[:], kkv[:], kkv[:], ALU.mult, [b_kkv], [b_tq])
        for h2 in range(HB // 2):
            k.mm(pq[0:64, :], ones[:], tq[:, 2 * h2:2 * h2 + 2, :].rearrange(FL), True, True, [b_ones, b_tq], [b_pq])
            k.ts(e1[:, 2 * h2:2 * h2 + 2, :].rearrange(FL), pq[0:64, :], 1e-24, None, ALU.max, None, [b_pq], [b_e1])
        k.act(e1[:], e1[:], AF.Sqrt, [b_e1], [b_e1])
        k.recip(e1[:], e1[:], [b_e1], [b_e1])
        k.tt(kkv[:], kkv[:], e1[:], ALU.mult, [b_kkv, b_e1], [b_kkv])
        k.stt(At[:], kkv[:], -1.0, sgz[:], ALU.mult, ALU.mult, [b_kkv, b_sgz], [b_At])
        k.tt(tq[:], kkv[:], av[:], ALU.mult, [b_kkv, b_av], [b_tq])
        k.tt(Bt[:], tq[:], bv[:], ALU.mult, [b_tq, b_bv], [b_Bt])
        k.stt(tq[:], av[:], -1.0, bc(hp[:, 3, :, None]), ALU.add, ALU.mult, [b_av, b_hp], [b_tq])
        k.stt(kT[:], tq[:], 1.0, kT[:], ALU.add, ALU.mult, [b_tq, b_kT], [b_kT])
        k.tt(kT[:], kT[:], bv[:], ALU.mult, [b_kT, b_bv], [b_kT])
        k.tt(tq[:], rT[:], bc(hp[:, 4, :, None]), ALU.mult, [b_rT, b_hp], [b_tq])
        k.tt(tq[:], tq[:], kT[:], ALU.mult, [b_tq, b_kT], [b_tq])
        for h in range(HB):
            for c in range(NCH):
                idx = h * NCH + c
                k.mm(PNf[:, idx:idx + 1], tq[:, h, c * CH:(c + 1) * CH], ones[:, 0:1], True, True,
                     [b_tq, b_ones], [b_PN])
        k.cp(rk[:].rearrange(FL), PNf[:, 0:HB * NCH], [b_PN], [b_rk])
        for c in range(NCH):
            for j in range(4):
                k.mm(pq[0:64, 0:HB * 64], tg[:, j, c * CH:(c + 1) * CH], gup[:, j, :], j == 0, j == 3,
                     [b_tg, b_gup], [b_pq])
            k.act(gt[:, c, :], pq[0:64, 0:HB * 64], AF.Copy, [b_pq], [b_gt])

    def recurrence(tb, pump):
        t0 = tb * TBB
        rT, b_rT = rTs[tb % 2]
        kT, b_kT = kTs[tb % 2]
        for c in range(NCH):
            cs = slice(c * CH, (c + 1) * CH)
            for h in range(HB):
                k.mm(PA[:, h * 5 + 0, :], Bt[:, h, cs], At[:, h, cs], True, True, [b_Bt, b_At], [b_PA])
                k.mm(PA[:, h * 5 + 1, :], kT[:, h, cs], At[:, h, cs], True, True, [b_kT, b_At], [b_PA])
                k.mm(PA[:, h * 5 + 2, :], Bt[:, h, cs], rT[:, h, cs], True, True, [b_Bt, b_rT], [b_PA])
                k.mm(PA[:, h * 5 + 3, :], kT[:, h, cs], rT[:, h, cs], True, True, [b_kT, b_rT], [b_PA])
                k.mm(PA[:, h * 5 + 4, :], At[:, h, cs], Bt[:, h, cs], True, True, [b_Bt, b_At], [b_PA])
            k.tt(mats[:], PA[:, 0:HB * 5, :].rearrange("p (a b) c -> p a b c", a=HB),
                 m5[:, None, :, :].to_broadcast([64, HB, 5, 64]), ALU.mult, [b_PA, b_m5], [b_mats])
            pump()
            k.cp(Nsb[:, :, 0, :], mats[:, :, 4, :], [b_mats], [b_N])
            k.cp(Nsb[:, :, 1, :], mats[:, :, 0, :], [b_mats], [b_N])
            k.tt(X[:], mats[:, :, 0, :], i64[:, None, :].to_broadcast([64, HB, 64]), ALU.add, [b_mats, b_idf], [b_X])
            for step in range(5):
                for h in range(HB):
                    k.mm(PN[:, h * 2, :], Nsb[:, h, 1, :], Nsb[:, h, 0, :], True, True, [b_N], [b_PN])
                    if step < 4:
                        k.mm(PN[:, h * 2 + 1, :], Nsb[:, h, 0, :], Nsb[:, h, 1, :], True, True, [b_N], [b_PN])
                k.cp(Nsb[:].rearrange("p a b c -> p (a b) c"), PN[:, 0:2 * HB, :], [b_PN], [b_N])
                pump()
                for h in range(HB):
                    k.mm(PA[:, h, :], Nsb[:, h, 0, :], X[:, h, :], True, True, [b_N, b_X], [b_PA])
                k.tt(X[:], X[:], PA[:, 0:HB, :], ALU.add, [b_X, b_PA], [b_X])
                pump()
            for h in range(HB):
                k.mm(PA[:, 8 + h, :], At[:, h, cs], H[:, h, :], True, False, [b_At, b_H], [b_PA])
                k.mm(PA[:, 8 + h, :], mats[:, h, 1, :], Vt[:, c, h, :], False, True, [b_mats, b_Vt], [b_PA])
            k.cp(W[:], PA[:, 8:8 + HB, :], [b_PA], [b_W])
            pump()
            for h in range(HB):
                k.mm(PA[:, 8 + h, :], X[:, h, :], W[:, h, :], True, True, [b_X, b_W], [b_PA])
            k.cp(U[:], PA[:, 8:8 + HB, :], [b_PA], [b_U])
            pump()
            for h in range(HB):
                k.mm(PA[:, 16 + h, :], rT[:, h, cs], H[:, h, :], True, False, [b_rT, b_H], [b_PA])
                k.mm(PA[:, 16 + h, :], mats[:, h, 2, :], U[:, h, :], False, False, [b_mats, b_U], [b_PA])
                k.mm(PA[:, 16 + h, :], mats[:, h, 3, :], Vt[:, c, h, :], False, True, [b_mats, b_Vt], [b_PA])
            k.cp(ysb[:], PA[:, 16:16 + HB, :], [b_PA], [b_ysb])
            pump()
            for h in range(HB):
                k.tr(PN[:, h * 2, :], Bt[:, h, cs], i64, [b_Bt, b_idf], [b_PN])
                k.tr(PN[:, h * 2 + 1, :], kT[:, h, cs], i64, [b_kT, b_idf], [b_PN])
            k.cp(BKt[:].rearrange("p a b c -> p (a b) c"), PN[:, 0:2 * HB, :], [b_PN], [b_BKt])
            pump()
            for h in range(HB):
                k.mm(PA[:, 24 + h, :], BKt[:, h, 0, :], U[:, h, :], True, False, [b_BKt, b_U], [b_PA])
                k.mm(PA[:, 24 + h, :], BKt[:, h, 1, :], Vt[:, c, h, :], False, True, [b_BKt, b_Vt], [b_PA])
            k.tt(H[:], H[:], PA[:, 24:24 + HB, :], ALU.add, [b_H, b_PA], [b_H])
            k.tt(H[:], H[:], ebC[:, :, c:c + 1].to_broadcast([64, HB, 64]), ALU.mult, [b_H, b_ebC], [b_H])
            pump()
            for h in range(HB):
                k.P.op("dve", lambda e, h=h: e.bn_stats(out=stt_[:, h, 0:6], in_=ysb[:, h, :]),
                       reads=[b_ysb], writes=[b_stt])
                k.P.op("dve", lambda e, h=h: e.bn_aggr(out=mv[:, h, :], in_=stt_[:, h, 0:6]),
                       reads=[b_stt], writes=[b_mv])
            k.ts(mv[:, :, 1], mv[:, :, 1], 1.0, GN_EPS, ALU.mult, ALU.add, [b_mv], [b_mv])
            k.act(mv[:, :, 1], mv[:, :, 1], AF.Sqrt, [b_mv], [b_mv])
            k.recip(mv[:, :, 1], mv[:, :, 1], [b_mv], [b_mv])
            k.tt(t2[:], ysb[:], mv[:, :, 0:1].to_broadcast([64, HB, 64]), ALU.subtract, [b_ysb, b_mv], [b_t2])
            k.tt(t2[:], t2[:], mv[:, :, 1:2].to_broadcast([64, HB, 64]), ALU.mult, [b_t2, b_mv], [b_t2])
            t2f = t2[:].rearrange(FL)
            k.tt(t2f, t2f, lnw[:], ALU.mult, [b_t2, b_lnw], [b_t2])
            k.tt(t2f, t2f, lnb[:], ALU.add, [b_t2, b_lnb], [b_t2])
            pump()
            k.tt(W[:], Vt[:, c, :, :], rk[:, :, c:c + 1].to_broadcast([64, HB, 64]), ALU.mult, [b_Vt, b_rk], [b_W])
            k.tt(t2[:], t2[:], W[:], ALU.add, [b_t2, b_W], [b_t2])
            y_, b_y = yt[yi[0] % 2]
            yi[0] += 1
            k.tt(y_[:], t2f, gt[:, c, :], ALU.mult, [b_t2, b_gt], [b_y])
            k.dma(yo[t0 + c * CH:t0 + (c + 1) * CH, :], y_[:], [b_y], [], q="actq")

    HT(0)
    for _ in gemm_gen(0):
        pass
    for tb in range(nblk):
        g = None
        if tb + 1 < nblk:
            HT(tb + 1)
            g = gemm_gen(tb + 1)
        prep(tb)
        recurrence(tb, (lambda g=g: next(g, None)) if g is not None else (lambda: None))
        if g is not None:
            for _ in g:
                pass
    return k.finish()


_SMASKB = np.ones((64, 6, TBB), np.float32)
_SMASKB[:, :, ::CH] = 0.0
_su = np.triu(np.ones((64, 64), np.float32), 1)
_iu = np.triu(np.ones((64, 64), np.float32), 0)
_M5 = np.ascontiguousarray(np.stack([_su, _su, _iu, _iu, _su.T], axis=1))


def run_1b(inp, layer, x, modv_l, seq=SEQ, runner=None, cores=range(8), w_in_l=None):
    A_COLS = 1104
    W_B = 1536
    if w_in_l is None:
        w_in_l = run_wc(np.asarray(inp["w_in"][layer]))
    mu = np.asarray(inp["rwkv_mu"][layer], np.float32)
    nc = _get_nc(('1b', seq), lambda: build_1b(seq))
    maps = []
    fm = lambda v: np.ascontiguousarray(np.asarray(v, np.float32).reshape(6, 64).T)
    for ci in cores:
        b, g = ci // 4, ci % 4
        hs = slice(g * 384, (g + 1) * 384)
        wB = w_in_l[:, A_COLS:A_COLS + 5344]
        wrkv = np.ascontiguousarray(np.concatenate([wB[:, j * W_B:(j + 1) * W_B][:, hs] for j in range(3)], axis=1))
        wlo = np.ascontiguousarray(wB[:, 3 * W_B:])
        mu_rkv = np.concatenate([mu[j * W_B:(j + 1) * W_B][hs] for j in range(3)]).reshape(18, 64).T
        mu_l = mu[3 * W_B:]
        mu_wa = np.stack([mu_l[0:128], mu_l[128:256]], axis=1)
        mu_g = mu_l[256:].reshape(4, 120).T
        hp = np.stack([fm(inp["rwkv_w0"][layer][hs]), fm(inp["rwkv_a0"][layer][hs]), fm(inp["rwkv_k_k"][layer][hs]),
                       fm(inp["rwkv_k_a"][layer][hs]), fm(inp["rwkv_r_k"][layer].reshape(-1)[hs])], axis=1)
        mr = np.ascontiguousarray(modv_l[b])
        mcm = np.ascontiguousarray(mr.reshape(6, 32, 128).transpose(0, 2, 1))
        maps.append({"x": np.ascontiguousarray(x[b, :seq]), "modc": mcm, "identf": _IDENT, "wrkv": wrkv, "wlo": wlo,
                     "mu_rkv": np.ascontiguousarray(mu_rkv), "mu_wa": np.ascontiguousarray(mu_wa),
                     "mu_g": np.ascontiguousarray(mu_g), "hp": np.ascontiguousarray(hp),
                     "wup": np.ascontiguousarray(inp["rwkv_w_up"][layer][:, hs]),
                     "aup": np.ascontiguousarray(inp["rwkv_a_up"][layer][:, hs]),
                     "gup": np.ascontiguousarray(inp["rwkv_g_up"][layer][:, hs].reshape(4, 120, 384).transpose(1, 0, 2)),
                     "lnw": np.ascontiguousarray(inp["rwkv_lnx_w"][layer][hs]),
                     "lnb": np.ascontiguousarray(inp["rwkv_lnx_b"][layer][hs]),
                     "smask": _SMASKB, "m5": _M5})
    if runner is not None:
        return runner(nc, maps)
    res = run_bass_kernel_spmd(nc, maps, core_ids=list(range(8)))
    out = np.empty((2, 8192, 1536), ml_dtypes.bfloat16)
    for ci in range(8):
        b, g = ci // 4, ci % 4
        out[b, :, g * 384:(g + 1) * 384] = res.results[ci]["yb"]
    return out


def build_1ak(ntok=2048):
    k = K()
    x = k.din("x", [ntok, D])
    modc = k.din("modc", [6, 128, 32])
    idf_d = k.din("identf", [128, 128])
    wkv = k.din("wkv", [D, 320], BF16)
    kvg = k.din("kvg", [256])
    o_ct = k.dout("ckvnT", [128, 2, ntok], BF16)
    o_v = k.dout("V", [ntok, 256], BF16)
    o_ki = k.dout("kidxT", [64, ntok], BF16)
    hm = HTMaker(k, modc, idf_d)
    wl = WLoader(k, 128, 2)
    hT, b_hT = k.sb([128, 32, TB], BF16)
    wres, b_wres = k.sb([128, 32, 320], BF16)
    gb, b_gb = k.sb([128, 256])
    k.dma(gb[:], kvg.partition_broadcast(128), [], [b_gb], q="actq")
    for j, (c0, n) in enumerate(((0, 128), (128, 128), (256, 64))):
        wb, b_wb = wl.load(wkv, c0, n)
        k.cp(wres[:, :, c0:c0 + n], wb[:, :, 0:n], [b_wb], [b_wres], eng="pool")
    pkv = [k.ps([128, 512]) for _ in range(2)]
    pki, b_pki = k.ps([64, 512])
    ptb, b_ptb = k.ps([128, 2, 128], BF16)
    st, b_st = k.sb([128, 1])
    junk, b_junk = k.sb([128, 256])
    vt = [k.sb([128, 256], BF16) for _ in range(2)]
    ct = [k.sb([128, 2, 128], BF16) for _ in range(2)]
    kit = [k.sb([64, TB], BF16) for _ in range(2)]
    n_ = 0
    for tb in range(ntok // TB):
        t0 = tb * TB
        for mt in range(TB // 128):
            hm.emit(x[t0 + mt * 128:t0 + (mt + 1) * 128, :],
                    lambda q, mt=mt: hT[:, q, mt * 128:(mt + 1) * 128], b_hT)
        for mt in range(TB // 128):
            p_, b_p = pkv[n_ % 2]
            v_, b_v = vt[n_ % 2]
            c_, b_c = ct[n_ % 2]
            n_ += 1
            for kk in range(32):
                k.mm(p_[:, 0:256], hT[:, kk, mt * 128:(mt + 1) * 128], wres[:, kk, 0:256], kk == 0, kk == 31,
                     [b_hT, b_wres], [b_p])
            k.act(junk[:], p_[:, 0:256], AF.Square, [b_p], [b_junk, b_st], accum_out=st[:, 0:1])
            k.ts(st[:], st[:], 1.0 / 256, EPS, ALU.mult, ALU.add, [b_st], [b_st])
            k.act(st[:], st[:], AF.Sqrt, [b_st], [b_st])
            k.recip(st[:], st[:], [b_st], [b_st])
            k.stt(v_[:], p_[:, 0:256], st[:, 0:1], gb[:], ALU.mult, ALU.mult, [b_p, b_st, b_gb], [b_v])
            r0 = t0 + mt * 128
            k.dma(o_v[r0:r0 + 128, :], v_[:], [b_v], [], q="actq")
            for rc in range(2):
                k.tr(ptb[:, rc, :], v_[:, rc * 128:(rc + 1) * 128], hm.idb[:], [b_v, hm.b_idb], [b_ptb])
            k.cp(c_[:], ptb[:], [b_ptb], [b_c])
            k.dma(o_ct[:, :, r0:r0 + 128], c_[:], [b_c], [], q="actq")
        ki_, b_ki = kit[tb % 2]
        for kk in range(32):
            k.mm(pki[:], wres[:, kk, 256:320], hT[:, kk, :], kk == 0, kk == 31, [b_hT, b_wres], [b_pki])
        k.act(ki_[:], pki[:], AF.Copy, [b_pki], [b_ki])
        k.dma(o_ki[:, t0:t0 + TB], ki_[:], [b_ki], [], q="actq")
    return k.finish()


def run_1ak(layer, x, modv_l, w_in_l, kvg):
    if w_in_l.dtype == np.float32:
        w_in_l = run_wc(np.asarray(w_in_l))
    nc = _get_nc('1ak', build_1ak)
    maps = []
    wkv = np.ascontiguousarray(w_in_l[:, 768:768 + 320])
    for ci in range(8):
        b, q = ci // 4, ci % 4
        mr = np.ascontiguousarray(modv_l[b])
        mcm = np.ascontiguousarray(mr.reshape(6, 32, 128).transpose(0, 2, 1))
        maps.append({"x": np.ascontiguousarray(x[b, q * 2048:(q + 1) * 2048]), "modc": mcm, "identf": _IDENT,
                     "wkv": wkv, "kvg": np.ascontiguousarray(kvg)})
    res = run_bass_kernel_spmd(nc, maps, core_ids=list(range(8)))
    ckvnT = np.empty((2, 128, 2, 8192), ml_dtypes.bfloat16)
    V = np.empty((2, 8192, 256), ml_dtypes.bfloat16)
    kidxT = np.empty((2, 64, 8192), ml_dtypes.bfloat16)
    for ci in range(8):
        b, q = ci // 4, ci % 4
        sl = slice(q * 2048, (q + 1) * 2048)
        ckvnT[b, :, :, sl] = res.results[ci]["ckvnT"]
        V[b, sl] = res.results[ci]["V"]
        kidxT[b, :, sl] = res.results[ci]["kidxT"]
    return ckvnT, V, kidxT


NEG_SEL = -3.0e38
NEG_CAUSAL = -1.0e30
NEG_LOGIT = -30000.0


def build_1aq(nq=16, seq=SEQ):
    k = K()
    xq = k.din("xq", [nq * 128, D])
    modc = k.din("modc", [6, 128, 32])
    idf_d = k.din("identf", [128, 128])
    wq = k.din("wq", [D, 784], BF16)
    qg_d = k.din("qg", [128, 6])
    wqidx_d = k.din("wqidx", [128, 6, 1024])
    wuq_d = k.din("wuq", [128, 6, 1024])
    wukT_d = k.din("wukT", [128, 8, 256])
    wuv_d = k.din("wuv", [128, 2, 8, 128])
    rb_d = k.din("rb", [32, 8])
    ohv_d = k.din("ohv", [32, 384])
    kvg_d = k.din("kvg", [256])
    ckT_d = k.din("ckvnT", [128, 2, seq], BF16)
    V_d = k.din("V", [seq, 256], BF16)
    ki_d = k.din("kidxT", [64, seq], BF16)
    tpos_d = k.din("tposrel", [128, 1])
    iota_d = k.din("iota", [128, 512])
    sel_d = k.din("sel", [128, 10])
    J_d = k.din("J", [128, 128])
    ya = k.dout("ya", [nq * 128, 1024], BF16)
    vd = k.dscr("vd", [8, 384])
    b_vd = Buf()

    wl = WLoader(k, 128, 1, nst=1)
    stg, b_stg = wl.st[0]
    stgf = stg[:].rearrange("p a b -> p (a b)")
    hqT, b_hqT = k.sb([128, 32, 128], BF16)
    qg, b_qg = k.sb([128, 6])
    wqidx, b_wqidx = k.sb([128, 6, 1024], BF16)
    wuq, b_wuq = k.sb([128, 6, 1024], BF16)
    wukT, b_wukT = k.sb([128, 8, 256], BF16)
    wuv, b_wuv = k.sb([128, 2, 8, 128], BF16)
    tpos, b_tpos = k.sb([128, 1])
    sel, b_sel = k.sb([128, 10])
    Jf, b_Jf = k.sb([128, 128])
    Jb, b_Jb = k.sb([128, 128], BF16)
    I4, b_I4 = k.sb([128, 4, 128], BF16)
    onesb, b_onesb = k.sb([128, 128], BF16)
    cb, b_cb = k.sb([128, 512])
    cbb, b_cbb = k.sb([128, 512], BF16)
    rb, b_rb = k.sb([32, 8])
    rb31, b_rb31 = k.sb([32, 8])
    ohv, b_ohv = k.sb([32, 384])
    vts, b_vts = k.sb([8, 384])
    r1, b_r1 = k.sb([1, 512])
    fac, b_fac = k.sb([1, 4])
    TBsel, b_TBsel = k.sb([128, 5, 1024], BF16)
    score, b_score = k.sb([128, seq])
    hm = HTMaker(k, modc, idf_d, inplace=True,
                 junk=((score[:, 0:2048].bitcast(BF16), b_score) if seq >= 4096 else None))
    assert seq >= 3584 or True
    TBr = [(score[:, 0:1024], b_score), (score[:, 1024:2048], b_score)] if seq >= 4096 else [k.sb([128, 1024]) for _ in range(2)]
    if seq >= 4096:
        TBt, b_TBt = score[:, 2048:3072], b_score
        iota, b_iota = score[:, 3072:3584], b_score
    else:
        TBt, b_TBt = k.sb([128, 1024])
        iota, b_iota = k.sb([128, 512])

    pA, b_pA = k.ps([128, 512])
    pw = [k.ps([128, 512]) for _ in range(2)]
    pO = [k.ps([128, 512]) for _ in range(2)]
    pZ, b_pZ = k.ps([128, 512])

    selb, b_selb = k.sb([128, seq], BF16)
    cqs, b_cqs = k.sb([128, 784])
    widx, b_widx = k.sb([128, 16])
    cqT, b_cqT = k.sb([128, 6, 128], BF16)
    qiT, b_qiT = k.sb([64, 16, 128], BF16)
    qT, b_qT = k.sb([128, 8, 128], BF16)
    qlTs = [k.sb([128, 2, 8, 128], BF16) for _ in range(2)]
    sq, b_sq = k.sb([128, 2, 1024], BF16)
    srow, b_srow = k.sb([1, 1024])
    nsrs = [k.sb([1, 1024], BF16) for _ in range(2)]
    zrow, b_zrow = k.sb([1, 512])
    zc, b_zc = k.sb([128, 4])
    rz, b_rz = k.sb([128, 4])
    ones4, b_ones4 = k.sb([128, 4])
    negs4, b_negs4 = k.sb([128, 4])
    st, b_st = k.sb([128, 1])
    m8, b_m8 = k.sb([128, 8])
    kit = [k.sb([64, 512], BF16) for _ in range(2)]
    ckt = [k.sb([128, 2, 512], BF16) for _ in range(2)]
    Vt = [k.sb([128, 4, 256], BF16) for _ in range(2)]
    rl = [k.sb([128, 512]) for _ in range(2)]
    PT = [k.sb([128, 512], BF16) for _ in range(2)]
    olT, b_olT = k.sb([128, 2, 512], BF16)
    yat, b_yat = k.sb([128, 1024], BF16)

    for dst, src, bd in ((qg, qg_d, b_qg), (tpos, tpos_d, b_tpos), (sel, sel_d, b_sel),
                         (Jf, J_d, b_Jf), (rb, rb_d, b_rb), (ohv, ohv_d, b_ohv)):
        k.dma(dst[:], src, [], [bd], q="actq")
    k.dma(iota[:] if seq < 4096 else iota, iota_d, [], [b_iota], q="actq")
    k.dma(rb31[:], rb_d[31, :].partition_broadcast(32), [], [b_rb31], q="actq")
    k.cp(Jb[:], Jf[:], [b_Jf], [b_Jb])
    for c in range(4):
        k.cp(I4[:, c, :], hm.idf[:], [hm.b_idf], [b_I4])
    k.memset(onesb[:], 1.0, [b_onesb])
    k.memset(ones4[:], 1.0, [b_ones4])
    k.memset(negs4[:], -1.0, [b_negs4])
    for half in range(2):
        k.dma(stgf[:, 0:3072], wqidx_d[:, half * 3:(half + 1) * 3, :].rearrange("p a b -> p (a b)"), [], [b_stg])
        k.cp(wqidx[:, half * 3:(half + 1) * 3, :].rearrange("p a b -> p (a b)"), stgf[:, 0:3072], [b_stg], [b_wqidx], eng="pool")
    for half in range(2):
        k.dma(stgf[:, 0:3072], wuq_d[:, half * 3:(half + 1) * 3, :].rearrange("p a b -> p (a b)"), [], [b_stg])
        k.cp(wuq[:, half * 3:(half + 1) * 3, :].rearrange("p a b -> p (a b)"), stgf[:, 0:3072], [b_stg], [b_wuq], eng="pool")
    k.dma(stgf[:, 0:2048], wukT_d.rearrange("p a b -> p (a b)"), [], [b_stg])
    k.cp(wukT[:].rearrange("p a b -> p (a b)"), stgf[:, 0:2048], [b_stg], [b_wukT], eng="pool")
    k.dma(stgf[:, 0:2048], wuv_d.rearrange("p a b c -> p (a b c)"), [], [b_stg])
    k.cp(wuv[:].rearrange("p a b c -> p (a b c)"), stgf[:, 0:2048], [b_stg], [b_wuv], eng="pool")
    k.ts(cb[:], iota[:] if seq < 4096 else iota, tpos[:, 0:1], NEG_CAUSAL, ALU.is_gt, ALU.mult, [b_iota, b_tpos], [b_cb])
    k.ts(cbb[:], iota[:] if seq < 4096 else iota, tpos[:, 0:1], NEG_LOGIT, ALU.is_gt, ALU.mult, [b_iota, b_tpos], [b_cbb])
    k.dma(r1[:, 0:256], kvg_d[None, :], [], [b_r1], q="actq")
    k.P.op("dve", lambda e: e.tensor_reduce(out=fac[:, 0:1], in_=r1[:, 0:256], axis=AX.X, op=ALU.max,
                                            apply_absolute_value=True), reads=[b_r1], writes=[b_fac])
    k.dma(r1[:, 256:512], rb_d.rearrange("a b -> (a b)")[None, :], [], [b_r1], q="actq")
    k.P.op("dve", lambda e: e.tensor_reduce(out=fac[:, 1:2], in_=r1[:, 256:512], axis=AX.X, op=ALU.max,
                                            apply_absolute_value=True), reads=[b_r1], writes=[b_fac])
    k.ts(fac[:, 2:3], fac[:, 0:1], -16.0, None, ALU.mult, None, [b_fac], [b_fac])
    k.ts(fac[:, 3:4], fac[:, 1:2], -2.0, None, ALU.mult, None, [b_fac], [b_fac])
    k.tt(rb[:], rb[:], rb31[:], ALU.subtract, [b_rb, b_rb31], [b_rb])
    k.mm(pA[0:8, 0:384], rb[:], ohv[:], True, True, [b_rb, b_ohv], [b_pA])
    k.cp(vts[:], pA[0:8, 0:384], [b_pA], [b_vts])
    k.dma(vd, vts[:], [b_vts], [b_vd])
    for dl in range(2):
        t_, b_t = TBr[dl]
        src = bass.AP(tensor=vd.tensor, offset=128 * dl + 1, ap=[[1, 128], [384, 8], [1, 128]])
        k.dma(t_.rearrange("p (a b) -> p a b", a=8) if seq >= 4096 else t_[:].rearrange("p (a b) -> p a b", a=8), src, [b_vd], [b_t])
    for j in range(5):
        k.ts(TBt[:] if seq < 4096 else TBt, TBr[0][0][:] if seq < 4096 else TBr[0][0], sel[:, 2 * j:2 * j + 1], None, ALU.mult, None, [TBr[0][1], b_sel], [b_TBt])
        k.stt(TBsel[:, j, :], TBr[1][0][:] if seq < 4096 else TBr[1][0], sel[:, 2 * j + 1:2 * j + 2], TBt[:] if seq < 4096 else TBt, ALU.mult, ALU.add,
              [TBr[1][1], b_sel, b_TBt], [b_TBsel])

    inv_sqrt_d = float(128 ** -0.5)

    def stAF(qi):
        Lk = 512 * (qi + 1)
        nkb = 4 * (qi + 1)
        last = slice(Lk - 512, Lk)
        qlT, b_qlT = qlTs[qi % 2]
        nsr, b_nsr = nsrs[qi % 2]
        hm.emit(xq[qi * 128:(qi + 1) * 128, :], lambda q: hqT[:, q, :], b_hqT)
        for j in range(7):
            n = 128 if j < 6 else 16
            wb, b_wb = wl.load(wq, j * 128, n)
            for kk in range(32):
                k.mm(pA[:, 0:n], hqT[:, kk, :], wb[:, kk, 0:n], kk == 0, kk == 31, [b_hqT, b_wb], [b_pA])
            k.act(cqs[:, j * 128:j * 128 + n], pA[:, 0:n], AF.Copy, [b_pA], [b_cqs])
        k.act(hm.junk[:, 0:768], cqs[:, 0:768], AF.Square, [b_cqs], [hm.b_junk, b_st], accum_out=st[:, 0:1])
        k.ts(st[:], st[:], 1.0 / 768, EPS, ALU.mult, ALU.add, [b_st], [b_st])
        k.act(st[:], st[:], AF.Sqrt, [b_st], [b_st])
        k.recip(st[:], st[:], [b_st], [b_st])
        k.ts(widx[:], cqs[:, 768:784], 1.0 / 32.0, None, ALU.mult, None, [b_cqs], [b_widx])
        k.ts(cqs[:, 0:768], cqs[:, 0:768], st[:, 0:1], None, ALU.mult, None, [b_cqs, b_st], [b_cqs])
        for c0, nchunk in ((0, 4), (4, 2)):
            for c in range(nchunk):
                k.tr(hm.ptf[:, c, :], cqs[:, (c0 + c) * 128:(c0 + c + 1) * 128], hm.idf[:], [b_cqs, hm.b_idf], [hm.b_ptf])
            for c in range(nchunk):
                k.act(cqT[:, c0 + c, :], hm.ptf[:, c, :], AF.Identity, [hm.b_ptf, b_qg], [b_cqT],
                      scale=qg[:, c0 + c:c0 + c + 1])
        for h4 in range(4):
            for hl in range(4):
                h = h4 * 4 + hl
                for rc in range(6):
                    k.mm(pA[0:64, hl * 128:(hl + 1) * 128], wqidx[:, rc, h * 64:(h + 1) * 64], cqT[:, rc, :],
                         rc == 0, rc == 5, [b_wqidx, b_cqT], [b_pA])
            k.act(qiT[:, h4 * 4:(h4 + 1) * 4, :].rearrange("p a b -> p (a b)"), pA[0:64, :], AF.Copy, [b_pA], [b_qiT])
        for h4 in range(2):
            for hl in range(4):
                h = h4 * 4 + hl
                for rc in range(6):
                    k.mm(pA[:, hl * 128:(hl + 1) * 128], wuq[:, rc, h * 128:(h + 1) * 128], cqT[:, rc, :],
                         rc == 0, rc == 5, [b_wuq, b_cqT], [b_pA])
            k.act(qT[:, h4 * 4:(h4 + 1) * 4, :].rearrange("p a b -> p (a b)"), pA[:], AF.Copy, [b_pA], [b_qT])
        for rc2 in range(2):
            for h4 in range(2):
                for hl in range(4):
                    h = h4 * 4 + hl
                    k.mm(pA[:, hl * 128:(hl + 1) * 128], wukT[:, h, rc2 * 128:(rc2 + 1) * 128], qT[:, h, :],
                         True, True, [b_wukT, b_qT], [b_pA])
                k.act(qlT[:, rc2, h4 * 4:(h4 + 1) * 4, :].rearrange("p a b -> p (a b)"), pA[:], AF.Copy,
                      [b_pA], [b_qlT], scale=inv_sqrt_d)
        qlf = qlT[:].rearrange("p a b c -> p a (b c)")
        k.tt(sq[:], qlf, qlf, ALU.mult, [b_qlT], [b_sq])
        for hp in range(2):
            for rc2 in range(2):
                k.mm(pA[:], onesb[:], sq[:, rc2, hp * 512:(hp + 1) * 512], rc2 == 0, rc2 == 1, [b_onesb, b_sq], [b_pA])
            k.act(srow[:, hp * 512:(hp + 1) * 512], pA[0:1, :], AF.Sqrt, [b_pA], [b_srow])
        k.ts(nsr[:], srow[:], fac[:, 2:3], fac[:, 3:4], ALU.mult, ALU.add, [b_srow, b_fac], [b_nsr])

    def stG(qi):
        Lk = 512 * (qi + 1)
        nkb = 4 * (qi + 1)
        last = slice(Lk - 512, Lk)
        qlT, b_qlT = qlTs[qi % 2]
        nsr, b_nsr = nsrs[qi % 2]
        ih = 0
        for kt in range(qi + 1):
            ki_, b_ki = kit[kt % 2]
            k.dma(ki_[:], ki_d[:, kt * 512:(kt + 1) * 512], [], [b_ki])
            cs = slice(kt * 512, (kt + 1) * 512)
            for h in range(16):
                p_, b_p = pw[ih % 2]
                r_, b_r = rl[ih % 2]
                ih += 1
                k.mm(p_[:], qiT[:, h, :], ki_[:], True, True, [b_qiT, b_ki], [b_p])
                k.act(r_[:], p_[:], AF.Relu, [b_p], [b_r])
                if h == 0:
                    k.ts(score[:, cs], r_[:], widx[:, 0:1], None, ALU.mult, None, [b_r, b_widx], [b_score])
                else:
                    k.stt(score[:, cs], r_[:], widx[:, h:h + 1], score[:, cs], ALU.mult, ALU.add,
                          [b_r, b_widx, b_score], [b_score])
        k.tt(score[:, last], score[:, last], cb[:], ALU.add, [b_score, b_cb], [b_score])

    def stI(qi):
        Lk = 512 * (qi + 1)
        nkb = 4 * (qi + 1)
        last = slice(Lk - 512, Lk)
        qlT, b_qlT = qlTs[qi % 2]
        nsr, b_nsr = nsrs[qi % 2]
        for r in range(32):
            k.P.op("dve", lambda e, Lk=Lk: e.max(out=m8[:], in_=score[:, 0:Lk]), reads=[b_score], writes=[b_m8])
            k.P.op("dve", lambda e, Lk=Lk: e.match_replace(out=score[:, 0:Lk], in_to_replace=m8[:],
                                                          in_values=score[:, 0:Lk], imm_value=NEG_SEL),
                   reads=[b_score, b_m8], writes=[b_score])

    def stJ(qi):
        Lk = 512 * (qi + 1)
        nkb = 4 * (qi + 1)
        last = slice(Lk - 512, Lk)
        qlT, b_qlT = qlTs[qi % 2]
        nsr, b_nsr = nsrs[qi % 2]
        k.ts(selb[:, 0:Lk], score[:, 0:Lk], -2.0e38, NEG_LOGIT, ALU.is_gt, ALU.mult, [b_score], [b_selb])
        k.tt(selb[:, last], selb[:, last], cbb[:], ALU.add, [b_selb, b_cbb], [b_selb])

    def stK(qi):
        Lk = 512 * (qi + 1)
        nkb = 4 * (qi + 1)
        last = slice(Lk - 512, Lk)
        qlT, b_qlT = qlTs[qi % 2]
        nsr, b_nsr = nsrs[qi % 2]
        it = 0
        for hp in range(2):
            qlh = [qlT[:, rc2, hp * 4:(hp + 1) * 4, :].rearrange("p a b -> p (a b)") for rc2 in range(2)]
            for kt in range(qi + 1):
                c_, b_c = ckt[it % 2]
                v_, b_v = Vt[it % 2]
                it += 1
                k.dma(c_[:], ckT_d[:, :, kt * 512:(kt + 1) * 512], [], [b_c])
                k.dma(v_[:], V_d[kt * 512:(kt + 1) * 512, :].rearrange("(c p) r -> p c r", p=128), [], [b_v])
                for kb4 in range(4):
                    kb = kt * 4 + kb4
                    pl, b_pl = pw[kb % 2]
                    pt_, b_pt = PT[kb % 2]
                    ks = slice(kb4 * 128, (kb4 + 1) * 128)
                    k.mm(pl[:], c_[:, 0, ks], qlh[0], True, False, [b_c, b_qlT], [b_pl])
                    k.mm(pl[:], c_[:, 1, ks], qlh[1], False, False, [b_c, b_qlT], [b_pl])
                    k.mm(pl[:], selb[:, kb * 128:(kb + 1) * 128], I4[:].rearrange("p a b -> p (a b)"), False, False,
                         [b_selb, b_I4], [b_pl])
                    j = kb - (4 * qi - 1)
                    if 0 <= j <= 4:
                        k.mm(pl[:], Jb[:], TBsel[:, j, hp * 512:(hp + 1) * 512], False, False, [b_Jb, b_TBsel], [b_pl])
                    k.mm(pl[:], onesb[0:1, :], nsr[0:1, hp * 512:(hp + 1) * 512], False, True, [b_onesb, b_nsr], [b_pl])
                    k.act(pt_[:], pl[:], AF.Exp, [b_pl], [b_pt])
                    first, lastkb = kb == 0, kb == nkb - 1
                    k.mm(pO[0][0][:], v_[:, kb4, 0:128], pt_[:], first, lastkb, [b_v, b_pt], [pO[0][1]])
                    k.mm(pO[1][0][:], v_[:, kb4, 128:256], pt_[:], first, lastkb, [b_v, b_pt], [pO[1][1]])
                    k.mm(pZ[:], onesb[:], pt_[:], first, lastkb, [b_onesb, b_pt], [b_pZ])
            k.act(zrow[:], pZ[0:1, :], AF.Copy, [b_pZ], [b_zrow])
            for rc in range(2):
                k.act(olT[:, rc, :], pO[rc][0][:], AF.Copy, [pO[rc][1]], [b_olT])
            for hl in range(4):
                k.mm(pw[0][0][:, hl:hl + 1], zrow[0:1, hl * 128:(hl + 1) * 128], ones4[0:1, 0:1], True, True,
                     [b_zrow, b_ones4], [pw[0][1]])
            k.act(zc[:], pw[0][0][:, 0:4], AF.Copy, [pw[0][1]], [b_zc])
            k.tt(rz[:], zc[:], negs4[:], ALU.pow, [b_negs4, b_zc], [b_rz], eng="pool")
            for hl in range(4):
                h = hp * 4 + hl
                for rc in range(2):
                    k.mm(pA[:, hl * 128:(hl + 1) * 128], olT[:, rc, hl * 128:(hl + 1) * 128], wuv[:, rc, h, :],
                         rc == 0, rc == 1, [b_olT, b_wuv], [b_pA])
            for hl in range(4):
                k.act(yat[:, (hp * 4 + hl) * 128:(hp * 4 + hl + 1) * 128], pA[:, hl * 128:(hl + 1) * 128], AF.Identity,
                      [b_pA, b_rz], [b_yat], scale=rz[:, hl:hl + 1])
        k.dma(ya[qi * 128:(qi + 1) * 128, :], yat[:], [b_yat], [], q="actq")

    stAF(0); stG(0); stI(0); stJ(0)
    for qi in range(nq):
        if qi + 1 < nq:
            stAF(qi + 1); stG(qi + 1); stI(qi + 1)
        stK(qi)
        if qi + 1 < nq:
            stJ(qi + 1)
    return k.finish()


def _rel_bucket_np(n):
    n = np.asarray(n, np.int32)
    nf = np.maximum(n, 1).astype(np.float32)
    large = 16 + (np.log(nf / np.float32(16)) / np.float32(np.log(128 / 16)) * np.float32(16)).astype(np.int32)
    large = np.minimum(large, 31)
    return np.where(n < 16, n, large)


_OHV = np.zeros((32, 384), np.float32)
for _n in range(256):
    _OHV[int(_rel_bucket_np(_n)), _n + 128] = 1.0
_J = np.ascontiguousarray(np.eye(128, dtype=np.float32)[::-1])
_IOTA = np.ascontiguousarray(np.broadcast_to(np.arange(512, dtype=np.float32)[None], (128, 512)))


def run_1aq(inp, layer, x, modv_l, ckvnT, V, kidxT, nq=16, runner=None, cores=range(8), w_in_l=None):
    l = layer
    seq = nq * 512
    if w_in_l is None:
        w_in_l = run_wc(np.asarray(inp["w_in"][l]))
    nc = _get_nc(('1aq', nq, seq), lambda: build_1aq(nq, seq))
    wq = np.ascontiguousarray(np.concatenate([w_in_l[:, 0:768], w_in_l[:, 1088:1104]], axis=1))
    qg = np.ascontiguousarray(np.asarray(inp["mla_q_norm"][l], np.float32).reshape(6, 128).T)
    wqidx = np.ascontiguousarray(np.asarray(inp["w_qidx"][l]).reshape(6, 128, 1024).transpose(1, 0, 2))
    wuq = np.ascontiguousarray(np.asarray(inp["w_uq"][l]).reshape(6, 128, 1024).transpose(1, 0, 2))
    wukT = np.ascontiguousarray(np.asarray(inp["w_uk"][l]).transpose(2, 0, 1))
    wuv = np.ascontiguousarray(np.asarray(inp["w_uv"][l]).reshape(8, 2, 128, 128).transpose(2, 1, 0, 3))
    maps = []
    for ci in cores:
        b, g = ci // 4, ci % 4
        mr = np.ascontiguousarray(modv_l[b])
        mcm = np.ascontiguousarray(mr.reshape(6, 32, 128).transpose(0, 2, 1))
        xq = np.ascontiguousarray(x[b].reshape(64, 128, D)[g::4][:nq].reshape(nq * 128, D))
        sel = np.zeros((128, 10), np.float32)
        for j in range(5):
            dl = g + 1 - j
            if dl in (0, 1):
                sel[:, 2 * j + dl] = 1.0
        tpos = (g * 128 + np.arange(128, dtype=np.float32))[:, None]
        maps.append({"xq": xq, "modc": mcm, "identf": _IDENT, "wq": wq, "qg": qg, "wqidx": wqidx, "wuq": wuq,
                     "wukT": wukT, "wuv": wuv, "rb": np.ascontiguousarray(inp["rel_bias"]), "ohv": _OHV,
                     "kvg": np.ascontiguousarray(inp["mla_kv_norm"][l]),
                     "ckvnT": np.ascontiguousarray(ckvnT[b][:, :, :seq]), "V": np.ascontiguousarray(V[b][:seq]),
                     "kidxT": np.ascontiguousarray(kidxT[b][:, :seq]),
                     "tposrel": np.ascontiguousarray(tpos), "iota": _IOTA, "sel": sel, "J": _J})
    if runner is not None:
        return runner(nc, maps)
    res = run_bass_kernel_spmd(nc, maps, core_ids=list(range(8)))
    out = np.empty((2, 64, 128, 1024), ml_dtypes.bfloat16)
    for ci in range(8):
        b, g = ci // 4, ci % 4
        out[b, g::4] = res.results[ci]["ya"].reshape(16, 128, 1024)
    return out.reshape(2, 8192, 1024)


def build_wc(rows, cols):
    k = K()
    w = k.din("w", [rows, cols])
    o = k.dout("wb", [rows, cols], BF16)
    CW = max(c for c in range(1, 2049) if cols % c == 0)
    st = [k.sb([128, 2048]) for _ in range(3)]
    bf = [k.sb([128, 2048], BF16) for _ in range(3)]
    engs = ["pool", "dve", "act"]
    i = 0
    for r in range(rows // 128):
        for c in range(cols // CW):
            s_, b_s = st[i % 3]
            o_, b_o = bf[i % 3]
            k.dma(s_[:, 0:CW], w[r * 128:(r + 1) * 128, c * CW:(c + 1) * CW], [], [b_s])
            e = engs[i % 3]
            if e == "act":
                k.act(o_[:, 0:CW], s_[:, 0:CW], AF.Copy, [b_s], [b_o])
            else:
                k.cp(o_[:, 0:CW], s_[:, 0:CW], [b_s], [b_o], eng=e)
            k.dma(o[r * 128:(r + 1) * 128, c * CW:(c + 1) * CW], o_[:, 0:CW], [b_o], [], q="actq")
            i += 1
    return k.finish()


_NC_CACHE = {}


def _get_nc(key, fn):
    if key not in _NC_CACHE:
        _NC_CACHE[key] = fn()
    return _NC_CACHE[key]


def run_wc(w):
    R, C = w.shape
    rs = R // 8
    nc = _get_nc(("wc", rs, C), lambda: build_wc(rs, C))
    maps = [{"w": np.ascontiguousarray(w[ci * rs:(ci + 1) * rs])} for ci in range(8)]
    res = run_bass_kernel_spmd(nc, maps, core_ids=list(range(8)))
    return np.concatenate([r["wb"] for r in res.results], axis=0)


def kernel(**inp):
    x = np.ascontiguousarray(np.asarray(inp["x"], np.float32))
    modv = run_mod(inp)
    for l in range(2):
        w_in_l = run_wc(np.asarray(inp["w_in"][l]))
        ckvnT, V, kidxT = run_1ak(l, x, modv[l], w_in_l, inp["mla_kv_norm"][l])
        ya = run_1aq(inp, l, x, modv[l], ckvnT, V, kidxT, w_in_l=w_in_l)
        yb = run_1b(inp, l, x, modv[l], w_in_l=w_in_l)
        yc = run_1c(l, x, modv[l], w_in_l, inp["hgrn_lb"], inp["hgrn_onorm"][l])
        y = np.ascontiguousarray(np.concatenate([ya, yb, yc], axis=-1))
        wo = run_wc(np.asarray(inp["w_out"][l]))
        w1 = run_wc(np.asarray(inp["w_ff1"][l]))
        w2 = run_wc(np.asarray(inp["w_ff2"][l]))
        x = run_l2(x, y, modv[l], wo, w1, w2)
    return x
```
